# Optimizing a Trainium2 kernel written in Bass

```python
import math
import jax, jax.numpy as jnp
from jax import lax
import numpy as np

D_MODEL = 1024
BATCH = 32
SEQ = 2048
DEPTH = 2
DEC_BATCH = 4
DEC_SEQ = 4096
PAST_LEN = 128

EPS = 1e-6
NEG_INF = -1e30
Q_BLOCK = 128
CONV_CH = 256
CONV_WIDTH = 31
SGU_CH = 256
SGU_GROUPS = 4
SGU_CHUNK = 128
MLA_HEADS = 4
MLA_Q_RANK = 256
MLA_KV_RANK = 128
MLA_NOPE = 64
MLA_ROPE = 32
MLA_V = 64
MLA_THETA = 10000.0
DIL_GROUPS = ((128, 1), (512, 4), (2048, 16))
N_DIL = 3
DIL_HEADS = 4
DIL_HEAD_DIM = 64
ROPE_THETA = 500000.0
ROPE_DIMS = DIL_HEAD_DIM // 4
N_BRANCH = 4
D_FF = 2816
FFN_CONV_WIDTH = 3
N_A = 2 * CONV_CH
N_B = 2 * SGU_CH
N_C = MLA_Q_RANK + MLA_KV_RANK + MLA_ROPE
N_D = 3 * N_DIL * DIL_HEADS * DIL_HEAD_DIM
N_G = N_BRANCH * D_MODEL
N_IN = N_A + N_B + N_C + N_D + N_G
IN_SPLITS = (N_A, N_A + N_B, N_A + N_B + N_C, N_A + N_B + N_C + N_D)

kernel_name = 'hybrid_gated_encoder'


def rms_norm(x, g):
    xf = x.astype(jnp.float32)
    y = xf * lax.rsqrt(jnp.mean(xf * xf, axis=-1, keepdims=True) + EPS)
    return (y * g.astype(jnp.float32)).astype(x.dtype)


def layer_norm(x, g, b):
    xf = x.astype(jnp.float32)
    mu = jnp.mean(xf, axis=-1, keepdims=True)
    var = jnp.mean(jnp.square(xf - mu), axis=-1, keepdims=True)
    y = (xf - mu) * lax.rsqrt(var + EPS) * g.astype(jnp.float32) + b.astype(jnp.float32)
    return y.astype(x.dtype)


def depthwise_conv(x, w, b):
    k = w.shape[0]
    y = lax.conv_general_dilated(x, w[:, None, :].astype(x.dtype), window_strides=(1,),
                                 padding=[(k // 2, k // 2)],
                                 dimension_numbers=('NWC', 'WIO', 'NWC'),
                                 feature_group_count=x.shape[-1])
    return y + b.astype(x.dtype)


def rope_tables(seq, dims, theta):
    inv = jnp.exp(-math.log(theta) * jnp.arange(0, dims, 2, dtype=jnp.float32) / dims)
    ang = jnp.arange(seq, dtype=jnp.float32)[:, None] * inv[None, :]
    return jnp.cos(ang), jnp.sin(ang)


def apply_rope(x, cos, sin):
    half = x.shape[-1] // 2
    bshape = (1, cos.shape[0]) + (1,) * (x.ndim - 3) + (half,)
    c = cos.reshape(bshape)
    s = sin.reshape(bshape)
    xf = x.astype(jnp.float32)
    x1, x2 = xf[..., :half], xf[..., half:]
    return jnp.concatenate([x1 * c - x2 * s, x1 * s + x2 * c], axis=-1).astype(x.dtype)


def conv_module(z, w_dw, b_dw, ln_g, ln_b, w_o):
    a, g = jnp.split(z, 2, axis=-1)
    h = a * jax.nn.sigmoid(g)
    h = depthwise_conv(h, w_dw, b_dw)
    h = jax.nn.silu(layer_norm(h, ln_g, ln_b))
    return h @ w_o


def spatial_gating(z, ln_g, ln_b, w_s, b_s, w_o):
    B, S, _ = z.shape
    z = jax.nn.gelu(z)
    u, v = jnp.split(z, 2, axis=-1)
    v = layer_norm(v, ln_g, ln_b)
    v = v.reshape(B, S // SGU_CHUNK, SGU_CHUNK, SGU_GROUPS, SGU_CH // SGU_GROUPS)
    sv = jnp.einsum('gts,bcsgd->bctgd', w_s.astype(v.dtype), v)
    sv = sv + b_s.T.astype(v.dtype)[None, None, :, :, None]
    y = u * sv.reshape(B, S, SGU_CH)
    return y @ w_o


def mla(z, g_cq, g_ckv, w_uq, w_ukv, g_qn, g_kn, w_o, cos, sin):
    B, S, _ = z.shape
    dqk = MLA_NOPE + MLA_ROPE
    c_q, c_kv, k_r = jnp.split(z, [MLA_Q_RANK, MLA_Q_RANK + MLA_KV_RANK], axis=-1)
    q = (rms_norm(c_q, g_cq) @ w_uq).reshape(B, S, MLA_HEADS, dqk)
    kv = (rms_norm(c_kv, g_ckv) @ w_ukv).reshape(B, S, MLA_HEADS, MLA_NOPE + MLA_V)
    k_nope, v = jnp.split(kv, [MLA_NOPE], axis=-1)
    k = jnp.concatenate([k_nope, jnp.broadcast_to(k_r[:, :, None, :], (B, S, MLA_HEADS, MLA_ROPE))], axis=-1)
    q = rms_norm(q, g_qn)
    k = rms_norm(k, g_kn)
    q = jnp.concatenate([q[..., :MLA_NOPE], apply_rope(q[..., MLA_NOPE:], cos, sin)], axis=-1)
    k = jnp.concatenate([k[..., :MLA_NOPE], apply_rope(k[..., MLA_NOPE:], cos, sin)], axis=-1)
    q = q * (dqk ** -0.5)
    nb = S // Q_BLOCK
    qb = q.reshape(B, nb, Q_BLOCK, MLA_HEADS, dqk).transpose(1, 0, 2, 3, 4)

    def block(qi):
        s = jnp.einsum('bqhd,bkhd->bhqk', qi, k, preferred_element_type=jnp.float32)
        p = jax.nn.softmax(s, axis=-1)
        return jnp.einsum('bhqk,bkhd->bqhd', p.astype(v.dtype), v)

    o = lax.map(block, qb)
    o = o.transpose(1, 0, 2, 3, 4).reshape(B, S, MLA_HEADS * MLA_V)
    return o @ w_o


def dilated_attention(z, g_qn, g_kn, w_o, cos, sin):
    B, S, _ = z.shape
    q, k, v = jnp.split(z, 3, axis=-1)
    shp = (B, S, N_DIL, DIL_HEADS, DIL_HEAD_DIM)
    q, k, v = q.reshape(shp), k.reshape(shp), v.reshape(shp)
    q = rms_norm(q, g_qn)
    k = rms_norm(k, g_kn)
    q = jnp.concatenate([apply_rope(q[..., :ROPE_DIMS], cos, sin), q[..., ROPE_DIMS:]], axis=-1)
    k = jnp.concatenate([apply_rope(k[..., :ROPE_DIMS], cos, sin), k[..., ROPE_DIMS:]], axis=-1)
    q = q * (DIL_HEAD_DIM ** -0.5)
    k_groups = [k[:, :, g] for g in range(N_DIL)]
    v_groups = [v[:, :, g] for g in range(N_DIL)]
    nb = S // Q_BLOCK
    qb = q.reshape(B, nb, Q_BLOCK, N_DIL, DIL_HEADS, DIL_HEAD_DIM).transpose(1, 0, 2, 3, 4, 5)

    def block(args):
        bi, qi = args
        pos = bi * Q_BLOCK + jnp.arange(Q_BLOCK, dtype=jnp.int32)
        outs, lses = [], []
        for g, (win, dil) in enumerate(DIL_GROUPS):
            n_side = (win // 2) // dil
            offs = dil * jnp.arange(-n_side, n_side + 1, dtype=jnp.int32)
            idx = pos[:, None] + offs[None, :]
            valid = (idx >= 0) & (idx < S)
            idx = jnp.clip(idx, 0, S - 1)
            kg = jnp.take(k_groups[g], idx, axis=1)
            vg = jnp.take(v_groups[g], idx, axis=1)
            s = jnp.einsum('bqhd,bqjhd->bhqj', qi[:, :, g], kg, preferred_element_type=jnp.float32)
            s = jnp.where(valid[None, None], s, NEG_INF)
            lse = jax.nn.logsumexp(s, axis=-1)
            p = jnp.exp(s - lse[..., None])
            outs.append(jnp.einsum('bhqj,bqjhd->bqhd', p.astype(vg.dtype), vg))
            lses.append(lse)
        wts = jax.nn.softmax(jnp.stack(lses, axis=0), axis=0)
        wts = wts.transpose(0, 1, 3, 2)[..., None]
        o = outs[0] * wts[0].astype(outs[0].dtype)
        for g in range(1, N_DIL):
            o = o + outs[g] * wts[g].astype(outs[g].dtype)
        return o

    o = lax.map(block, (jnp.arange(nb, dtype=jnp.int32), qb))
    o = o.transpose(1, 0, 2, 3, 4).reshape(B, S, DIL_HEADS * DIL_HEAD_DIM)
    return o @ w_o


def conv_ffn(h, w_up, w_dw, b_dw, w_down):
    u = depthwise_conv(h @ w_up, w_dw, b_dw)
    a, b = jnp.split(u, 2, axis=-1)
    return (jax.nn.silu(a) * b) @ w_down


def _trunk(x, weights):
    (attn_norm, w_in, conv_w, conv_b, conv_ln_g, conv_ln_b, conv_w_o,
     sgu_ln_g, sgu_ln_b, sgu_w_s, sgu_b_s, sgu_w_o,
     mla_g_cq, mla_g_ckv, mla_w_uq, mla_w_ukv, mla_g_qn, mla_g_kn, mla_w_o,
     dil_g_qn, dil_g_kn, dil_w_o, w_out,
     ffn_norm, ffn_w_up, ffn_conv_w, ffn_conv_b, ffn_w_down) = weights
    B, S, _ = x.shape
    cos_c, sin_c = rope_tables(S, MLA_ROPE, MLA_THETA)
    cos_d, sin_d = rope_tables(S, ROPE_DIMS, ROPE_THETA)
    for l in range(DEPTH):
        h = rms_norm(x, attn_norm[l])
        z = h @ w_in[l]
        z_a, z_b, z_c, z_d, z_g = jnp.split(z, IN_SPLITS, axis=-1)
        gates = jax.nn.sigmoid(z_g.reshape(B, S, N_BRANCH, D_MODEL))
        y_a = conv_module(z_a, conv_w[l], conv_b[l], conv_ln_g[l], conv_ln_b[l], conv_w_o[l])
        y_b = spatial_gating(z_b, sgu_ln_g[l], sgu_ln_b[l], sgu_w_s[l], sgu_b_s[l], sgu_w_o[l])
        y_c = mla(z_c, mla_g_cq[l], mla_g_ckv[l], mla_w_uq[l], mla_w_ukv[l], mla_g_qn[l], mla_g_kn[l],
                  mla_w_o[l], cos_c, sin_c)
        y_d = dilated_attention(z_d, dil_g_qn[l], dil_g_kn[l], dil_w_o[l], cos_d, sin_d)
        merged = (gates[:, :, 0] * y_a + gates[:, :, 1] * y_b
                  + gates[:, :, 2] * y_c + gates[:, :, 3] * y_d)
        x = x + merged @ w_out[l]
        x = x + conv_ffn(rms_norm(x, ffn_norm[l]), ffn_w_up[l], ffn_conv_w[l], ffn_conv_b[l], ffn_w_down[l])
    return x


def setup_inputs(seed: int = 0) -> dict:
    key = jax.random.key(seed)
    ks = iter(jax.random.split(key, 40))
    L = DEPTH

    def nrm(shape, scale):
        return scale * jax.random.normal(next(ks), shape, jnp.float32)

    def gain(shape):
        return 1.0 + 0.05 * jax.random.normal(next(ks), shape, jnp.float32)

    return {
        'x_prompt': nrm((BATCH, SEQ, D_MODEL), 1.0),
        'x_sample': nrm((DEC_BATCH, DEC_SEQ, D_MODEL), 1.0),
        'attn_norm': gain((L, D_MODEL)),
        'w_in': nrm((L, D_MODEL, N_IN), D_MODEL ** -0.5),
        'conv_w': nrm((L, CONV_WIDTH, CONV_CH), CONV_WIDTH ** -0.5),
        'conv_b': nrm((L, CONV_CH), 0.02),
        'conv_ln_g': gain((L, CONV_CH)),
        'conv_ln_b': nrm((L, CONV_CH), 0.02),
        'conv_w_o': nrm((L, CONV_CH, D_MODEL), CONV_CH ** -0.5),
        'sgu_ln_g': gain((L, SGU_CH)),
        'sgu_ln_b': nrm((L, SGU_CH), 0.02),
        'sgu_w_s': nrm((L, SGU_GROUPS, SGU_CHUNK, SGU_CHUNK), SGU_CHUNK ** -0.5),
        'sgu_b_s': gain((L, SGU_GROUPS, SGU_CHUNK)),
        'sgu_w_o': nrm((L, SGU_CH, D_MODEL), SGU_CH ** -0.5),
        'mla_g_cq': gain((L, MLA_Q_RANK)),
        'mla_g_ckv': gain((L, MLA_KV_RANK)),
        'mla_w_uq': nrm((L, MLA_Q_RANK, MLA_HEADS * (MLA_NOPE + MLA_ROPE)), MLA_Q_RANK ** -0.5),
        'mla_w_ukv': nrm((L, MLA_KV_RANK, MLA_HEADS * (MLA_NOPE + MLA_V)), MLA_KV_RANK ** -0.5),
        'mla_g_qn': gain((L, MLA_NOPE + MLA_ROPE)),
        'mla_g_kn': gain((L, MLA_NOPE + MLA_ROPE)),
        'mla_w_o': nrm((L, MLA_HEADS * MLA_V, D_MODEL), (MLA_HEADS * MLA_V) ** -0.5),
        'dil_g_qn': gain((L, DIL_HEAD_DIM)),
        'dil_g_kn': gain((L, DIL_HEAD_DIM)),
        'dil_w_o': nrm((L, DIL_HEADS * DIL_HEAD_DIM, D_MODEL), (DIL_HEADS * DIL_HEAD_DIM) ** -0.5),
        'w_out': nrm((L, D_MODEL, D_MODEL), D_MODEL ** -0.5),
        'ffn_norm': gain((L, D_MODEL)),
        'ffn_w_up': nrm((L, D_MODEL, 2 * D_FF), D_MODEL ** -0.5),
        'ffn_conv_w': nrm((L, FFN_CONV_WIDTH, 2 * D_FF), FFN_CONV_WIDTH ** -0.5),
        'ffn_conv_b': nrm((L, 2 * D_FF), 0.02),
        'ffn_w_down': nrm((L, D_FF, D_MODEL), D_FF ** -0.5),
    }


def reference(x_prompt, x_sample, attn_norm, w_in, conv_w, conv_b, conv_ln_g, conv_ln_b, conv_w_o,
              sgu_ln_g, sgu_ln_b, sgu_w_s, sgu_b_s, sgu_w_o,
              mla_g_cq, mla_g_ckv, mla_w_uq, mla_w_ukv, mla_g_qn, mla_g_kn, mla_w_o,
              dil_g_qn, dil_g_kn, dil_w_o, w_out,
              ffn_norm, ffn_w_up, ffn_conv_w, ffn_conv_b, ffn_w_down):
    weights = (attn_norm, w_in, conv_w, conv_b, conv_ln_g, conv_ln_b, conv_w_o,
               sgu_ln_g, sgu_ln_b, sgu_w_s, sgu_b_s, sgu_w_o,
               mla_g_cq, mla_g_ckv, mla_w_uq, mla_w_ukv, mla_g_qn, mla_g_kn, mla_w_o,
               dil_g_qn, dil_g_kn, dil_w_o, w_out,
               ffn_norm, ffn_w_up, ffn_conv_w, ffn_conv_b, ffn_w_down)
    y_prompt = _trunk(x_prompt, weights)
    y_sample = _trunk(x_sample, weights)
    return (y_prompt, y_sample)
```

```python
import math
from contextlib import ExitStack
import numpy as np
import ml_dtypes
import concourse.bass as bass
import concourse.mybir as mybir
from concourse.bass_utils import run_bass_kernel_spmd

F32 = mybir.dt.float32
BF16 = mybir.dt.bfloat16
AF = mybir.ActivationFunctionType
ALU = mybir.AluOpType

D = 1024
DEPTH = 2
EPS = 1e-6
NIN = 7840
DFF = 2816
PAD = 1024
TW = 512


class T:
    __slots__ = ("name", "w", "r")

    def __init__(self, name=""):
        self.name = name
        self.w = None
        self.r = []


class Op:
    __slots__ = ("eng", "fn", "deps", "inc", "sig_sem", "sig_cnt", "ndma", "waits", "done")

    def __init__(self, eng, fn):
        self.eng = eng
        self.fn = fn
        self.deps = []
        self.inc = False
        self.sig_sem = None
        self.sig_cnt = None
        self.ndma = 0
        self.waits = []
        self.done = False


class G:
    ENGS = ("pe", "act", "dve", "pool", "sp")

    def __init__(self, nc, es):
        self.nc = nc
        self.es = es
        self.sems = {e: es.enter_context(nc.semaphore("s_" + e)) for e in ("pe", "act", "dve", "pool")}
        self.cnt = {e: 0 for e in ("pe", "act", "dve", "pool")}
        self.dsems = {}
        self.dcnt = {}
        self.dlast = {}
        self.waited = {e: {} for e in self.ENGS}
        self.ops = {e: [] for e in self.ENGS}
        self.order = []
        self.nops = 0

    def dsem(self, name):
        if name not in self.dsems:
            self.dsems[name] = self.es.enter_context(self.nc.semaphore("d_" + name))
            self.dcnt[name] = 0
            self.dlast[name] = None
        return name

    def _dep(self, op, p):
        if p is None or p is op or p.done:
            return
        if p.eng == "pe" and op.eng == "pe" and p.ndma == 0 and op.ndma == 0:
            return
        if p not in op.deps:
            op.deps.append(p)
            p.inc = True

    def op(self, eng, fn, reads=(), writes=(), dma=None, ndma=1):
        o = Op(eng, fn)
        for t in reads:
            self._dep(o, t.w)
        for t in writes:
            self._dep(o, t.w)
            for r in t.r:
                self._dep(o, r)
        if dma is not None:
            self.dsem(dma)
            o.ndma = ndma
            o.sig_sem = dma
            self._dep(o, self.dlast[dma])
            self.dlast[dma] = o
        for t in reads:
            t.r.append(o)
        for t in writes:
            t.w = o
            t.r = []
        self.ops[eng].append(o)
        self.order.append(o)
        return o

    def fence(self):
        o = Op("sp", lambda e: e.nop())
        for name, p in self.dlast.items():
            self._dep(o, p)
        self.ops["sp"].append(o)
        self.order.append(o)

    def emit(self):
        nc = self.nc
        for o in self.order:
            if o.ndma:
                self.dcnt[o.sig_sem] += 16 * o.ndma
                o.sig_cnt = self.dcnt[o.sig_sem]
            elif o.inc:
                self.cnt[o.eng] += 1
                o.sig_sem = o.eng
                o.sig_cnt = self.cnt[o.eng]
        for o in self.order:
            wd = self.waited[o.eng]
            for p in o.deps:
                key = (p.sig_sem, p.ndma > 0)
                if wd.get(key, 0) < p.sig_cnt:
                    wd[key] = p.sig_cnt
                    o.waits.append((p.ndma > 0, p.sig_sem, p.sig_cnt))
        ops, sems, dsems = self.ops, self.sems, self.dsems

        def run(eng_obj, lst):
            for o in lst:
                for isd, key, c in o.waits:
                    eng_obj.wait_ge(dsems[key] if isd else sems[key], c)
                if o.ndma:
                    o.fn(eng_obj, dsems[o.sig_sem])
                else:
                    ins = o.fn(eng_obj)
                    if o.inc:
                        ins.then_inc(sems[o.eng], 1)

        with nc.Block() as block:
            @block.tensor
            def _(e):
                run(e, ops["pe"])

            @block.scalar
            def _(e):
                run(e, ops["act"])

            @block.vector
            def _(e):
                run(e, ops["dve"])

            @block.gpsimd
            def _(e):
                run(e, ops["pool"])

            @block.sync
            def _(e):
                run(e, ops["sp"])
        for o in self.order:
            o.done = True
        self.nops += len(self.order)
        self.ops = {e: [] for e in self.ENGS}
        self.order = []


_UID = [0]


def un(name):
    _UID[0] += 1
    return "%s_u%d" % (name, _UID[0])


class Ring:
    def __init__(self, nc, es, name, shape, dtype, n):
        self.tiles = [es.enter_context(nc.sbuf_tensor(un("%s_%d" % (name, i)), shape, dtype)) for i in range(n)]
        self.ts = [T("%s_%d" % (name, i)) for i in range(n)]
        self.name = name
        self.i = -1
        self.n = n

    def next(self):
        self.i = (self.i + 1) % self.n
        return self.tiles[self.i], self.ts[self.i], "%s%d" % (self.name, self.i)


class Rot:
    def __init__(self, items):
        self.items = items
        self.i = -1

    def next(self):
        self.i = (self.i + 1) % len(self.items)
        return self.items[self.i]


def host_consts():
    c = {}
    def tabs(dims, theta):
        inv = np.exp(-math.log(theta) * np.arange(0, dims, 2, dtype=np.float32) / dims).astype(np.float32)
        ang = np.arange(4096, dtype=np.float32)[:, None] * inv[None, :]
        return np.cos(ang).astype(np.float32).T, np.sin(ang).astype(np.float32).T
    cc, sc = tabs(32, 10000.0)
    C = np.ones((96, 4096), np.float32)
    S = np.zeros((96, 4096), np.float32)
    C[64:80] = cc; C[80:96] = cc; S[64:80] = sc; S[80:96] = sc
    c["ropec"] = np.stack([C, S], 0)
    cd, sd = tabs(16, 500000.0)
    C = np.ones((128, 4096), np.float32)
    S = np.zeros((128, 4096), np.float32)
    for b in (0, 64):
        C[b:b + 8] = cd; C[b + 8:b + 16] = cd; S[b:b + 8] = sd; S[b + 8:b + 16] = sd
    c["roped"] = np.stack([C, S], 0)
    mats = np.zeros((5, 128, 128), np.float32)
    mats[0] = 1.0
    mats[1, 0:64, 0:64] = 1.0; mats[1, 64:128, 64:128] = 1.0
    for m in range(64, 80):
        mats[2, m + 16, m] = -1.0
    for m in range(80, 96):
        mats[2, m - 16, m] = 1.0
    for b in (0, 64):
        for m in range(b, b + 8):
            mats[3, m + 8, m] = -1.0
        for m in range(b + 8, b + 16):
            mats[3, m - 8, m] = 1.0
    mats[4] = np.eye(128)
    c["mats"] = mats.astype(ml_dtypes.bfloat16)
    k = np.arange(128)[:, None]
    q = np.arange(128)[None, :]
    L = (k >= q).astype(np.float32)
    U = (k <= q).astype(np.float32)
    masks = np.zeros((4, 128, 512), np.float32)
    masks[0] = np.tile(L, (1, 4))
    masks[1] = np.tile(U, (1, 4))
    masks[2] = np.tile(L[:, :32], (1, 16))
    masks[3] = np.tile(U[:, :32], (1, 16))
    c["masks"] = masks.astype(ml_dtypes.bfloat16)
    return c


WNAMES = ["attn_norm", "w_in", "conv_w", "conv_b", "conv_ln_g", "conv_ln_b", "conv_w_o",
          "sgu_ln_g", "sgu_ln_b", "sgu_w_s", "sgu_b_s", "sgu_w_o",
          "mla_g_cq", "mla_g_ckv", "mla_w_uq", "mla_w_ukv", "mla_g_qn", "mla_g_kn", "mla_w_o",
          "dil_g_qn", "dil_g_kn", "dil_w_o", "w_out",
          "ffn_norm", "ffn_w_up", "ffn_conv_w", "ffn_conv_b", "ffn_w_down"]
WSHAPES = {
    "attn_norm": [2, 1024], "w_in": [2, 1024, 7840], "conv_w": [2, 31, 256], "conv_b": [2, 256],
    "conv_ln_g": [2, 256], "conv_ln_b": [2, 256], "conv_w_o": [2, 256, 1024], "sgu_ln_g": [2, 256],
    "sgu_ln_b": [2, 256], "sgu_w_s": [2, 4, 128, 128], "sgu_b_s": [2, 4, 128], "sgu_w_o": [2, 256, 1024],
    "mla_g_cq": [2, 256], "mla_g_ckv": [2, 128], "mla_w_uq": [2, 256, 384], "mla_w_ukv": [2, 128, 512],
    "mla_g_qn": [2, 96], "mla_g_kn": [2, 96], "mla_w_o": [2, 256, 1024], "dil_g_qn": [2, 64],
    "dil_g_kn": [2, 64], "dil_w_o": [2, 256, 1024], "w_out": [2, 1024, 1024], "ffn_norm": [2, 1024],
    "ffn_w_up": [2, 1024, 5632], "ffn_conv_w": [2, 3, 5632], "ffn_conv_b": [2, 5632], "ffn_w_down": [2, 2816, 1024],
}


def build(seqs, phases=("p1", "p2a", "p2b", "p2c", "p3"), depth=DEPTH, debug=False, joined=False):
    nc = bass.Bass("TRN2", target_bir_lowering=False)
    NTOK = sum(seqs)
    soff = [sum(seqs[:i]) for i in range(len(seqs))]
    poff = [soff[i] + 2 * PAD * i for i in range(len(seqs))]
    NP = NTOK + 2 * PAD * len(seqs)
    groups = ([[0, 1]] + [[i] for i in range(2, len(seqs))]) if joined else [[i] for i in range(len(seqs))]
    gof = {}
    for gi_, grp in enumerate(groups):
        for s_ in grp:
            gof[s_] = gi_
    gbase = {s_: soff[groups[gof[s_]][0]] for s_ in range(len(seqs))}
    gS = {s_: sum(seqs[x] for x in groups[gof[s_]]) for s_ in range(len(seqs))}
    rpos = {s_: soff[s_] - gbase[s_] for s_ in range(len(seqs))}
    tiles = []
    for s, S in enumerate(seqs):
        for t0 in range(0, S, TW):
            tiles.append((s, t0, soff[s] + t0, poff[s] + PAD + t0))

    def din(name, shape, dt=F32):
        return nc.dram_tensor(name, shape, dt, kind="ExternalInput").ap()

    def dscr(name, shape, dt):
        return nc.dram_tensor(name, shape, dt, kind="Internal").ap()

    xT = din("xT", [D, NTOK])
    W = {n: din(n, WSHAPES[n]) for n in WNAMES}
    wsT = din("wsT", [2, 4, 128, 128])
    ropec = din("ropec", [2, 96, 4096])
    roped = din("roped", [2, 128, 4096])
    matsd = din("mats", [5, 128, 128], BF16)
    masksd = din("masks", [4, 128, 512], BF16)
    jfd = din("jf", [128, 2])
    yT = nc.dram_tensor("yT", [D, NTOK], F32, kind="ExternalOutput").ap()

    xa = dscr("xa", [D, NP], F32)
    xb = dscr("xb", [D, NTOK], F32)
    hA = dscr("hA", [256, NP], BF16)
    yB = dscr("yB", [256, NTOK], BF16)
    qc = dscr("qc", [96, 4, NTOK], BF16)
    kc = dscr("kc", [96, 4, NTOK], BF16)
    vc = dscr("vc", [NTOK, 256], BF16)
    qd = dscr("qd", [128, 6, NTOK], BF16)
    kd = dscr("kd", [128, 6, NP], BF16)
    vd = dscr("vd", [NP, 1536], BF16)
    oc = dscr("oc", [64, 4, NTOK], BF16)
    od = dscr("od", [64, 4, NTOK], BF16)
    dbg = {}
    if debug:
        for nm, ap in (("qc", qc), ("kc", kc), ("vc", vc), ("qd", qd), ("kd", kd), ("vd", vd), ("oc", oc),
                       ("od", od), ("hA", hA), ("yB", yB)):
            dbg[nm] = nc.dram_tensor("dbg_" + nm, list(ap.shape), BF16, kind="ExternalOutput").ap()
        dbg["xa"] = nc.dram_tensor("dbg_xa", [D, NP], F32, kind="ExternalOutput").ap()

    es_top = ExitStack()
    with es_top:
        es_top.enter_context(nc.allow_non_contiguous_dma(reason="small strided parameter loads"))
        es_top.enter_context(nc.allow_low_precision(reason="bf16 matmul operands, fp32 accumulation"))
        g = G(nc, es_top)
        PSA = es_top.enter_context(nc.psum_tensor("psall", [128, 4096], F32))
        banks = [PSA[:, i * 512:(i + 1) * 512] for i in range(8)]
        bts = [T("bank%d" % i) for i in range(8)]
        BK = [(banks[i], bts[i]) for i in range(8)]
        mats = es_top.enter_context(nc.sbuf_tensor(un("mats"), [128, 5, 128], BF16))
        tmats = T("mats")
        onesb = es_top.enter_context(nc.sbuf_tensor(un("onesb"), [128, 64], BF16))
        tonesb = T("onesb")
        epsb = es_top.enter_context(nc.sbuf_tensor(un("epsb"), [128, 1], F32))
        tepsb = T("epsb")
        mhalf = es_top.enter_context(nc.sbuf_tensor(un("mhalf"), [128, 1], F32))
        tmhalf = T("mhalf")
        jft = es_top.enter_context(nc.sbuf_tensor(un("jft"), [128, 2], F32))
        tjft = T("jft")
        g.op("sp", lambda e, s: e.dma_start(out=jft[:], in_=jfd).then_inc(s, 16), writes=[tjft], dma="c1")
        es_init = ExitStack()
        zt = es_init.enter_context(nc.sbuf_tensor(un("zt"), [128, 2048], BF16))
        tzt = T("zt")
        g.op("sp", lambda e, s: e.dma_start(out=mats[:], in_=matsd.rearrange("m p n -> p m n")).then_inc(s, 16),
             writes=[tmats], dma="c0")
        g.op("pool", lambda e: e.memset(zt[:], 0.0), writes=[tzt])
        g.op("pool", lambda e: e.memset(onesb[:], 1.0), writes=[tonesb])
        ONES = mats[:, 0, :]
        BONES = mats[:, 1, :]
        RC = mats[:, 2, :]
        RD = mats[:, 3, :]
        zi = [0]

        def zero_dram(ap3):
            sem = "z%d" % (zi[0] % 8)
            zi[0] += 1
            a, n = ap3.shape[1], ap3.shape[2]
            g.op("sp", lambda e, s: e.dma_start(out=ap3, in_=zt[:, 0:a * n].rearrange("p (a n) -> p a n", a=a)).then_inc(s, 16),
                 reads=[tzt], dma=sem)

        for s, S in enumerate(seqs):
            for lo in (poff[s], poff[s] + PAD + S):
                zero_dram(hA.rearrange("(a p) n -> p a n", p=128)[:, :, lo:lo + PAD])
                for c6 in range(0, 6, 2):
                    zero_dram(kd[:, c6:c6 + 2, lo:lo + PAD])
                for r0 in range(0, PAD, 128):
                    zero_dram(vd[lo + r0:lo + r0 + 128, :].rearrange("(a p) n -> p a n", p=128))
            for col in (poff[s] + PAD - 1, poff[s] + PAD + S):
                pass
        xa_bf = None
        zf = es_init.enter_context(nc.sbuf_tensor(un("zf"), [128, 8, 1], F32))
        tzf = T("zf")
        g.op("pool", lambda e: e.memset(zf[:], 0.0), writes=[tzf])
        for s, S in enumerate(seqs):
            for col in (poff[s] + PAD - 1, poff[s] + PAD + S):
                g.op("sp", lambda e, s_, col=col: e.dma_start(out=xa.rearrange("(k p) n -> p k n", p=128)[:, :, col:col + 1],
                                                            in_=zf[:]).then_inc(s_, 16), reads=[tzf], dma="zx")
        g.fence()
        g.emit()
        es_init.close()

        cast_rr = Rot(["act", "dve", "pool"])

        def cast_op(eng, dst, src, scale_ap, reads, writes):
            if eng == "act":
                if scale_ap is None:
                    g.op("act", lambda e: e.activation(out=dst, in_=src, func=AF.Copy), reads=reads, writes=writes)
                else:
                    g.op("act", lambda e: e.activation(out=dst, in_=src, func=AF.Identity, scale=scale_ap), reads=reads, writes=writes)
            else:
                if scale_ap is None:
                    g.op(eng, lambda e: e.tensor_copy(out=dst, in_=src), reads=reads, writes=writes)
                else:
                    g.op(eng, lambda e: e.tensor_scalar(out=dst, in0=src, scalar1=scale_ap, scalar2=0.0, op0=ALU.mult, op1=ALU.add),
                         reads=reads, writes=writes)

        def load_w(stage, dst, tdst, src, gain, tgain, np_=128):
            K, N = dst.shape[1], dst.shape[2]
            CH = stage.CH
            srcv = src.rearrange("(k p) n -> p k n", p=np_)
            if gain is None and N * 2 <= CH:
                kk = CH // N
                for k0 in range(0, K, kk):
                    k1 = min(K, k0 + kk)
                    st, tst, sname = stage.next()
                    stv = st[0:np_, 0:(k1 - k0) * N].rearrange("p (a n) -> p a n", a=k1 - k0)
                    g.op("sp", lambda e, s, stv=stv, k0=k0, k1=k1: e.dma_start(out=stv, in_=srcv[:, k0:k1, :]).then_inc(s, 16), writes=[tst], dma=sname)
                    cast_op(cast_rr.next(), dst[:, k0:k1, :], stv, None, [tst], [tdst])
                return
            for k in range(K):
                for c0 in range(0, N, CH):
                    c1 = min(N, c0 + CH)
                    st, tst, sname = stage.next()
                    g.op("sp", lambda e, s, st=st, k=k, c0=c0, c1=c1: e.dma_start(out=st[0:np_, 0:c1 - c0], in_=srcv[:, k, c0:c1]).then_inc(s, 16),
                         writes=[tst], dma=sname)
                    cast_op(cast_rr.next(), dst[:, k, c0:c1], st[0:np_, 0:c1 - c0],
                            None if gain is None else gain[:, k:k + 1], [tst] + ([tgain] if gain is not None else []), [tdst])

        class BigStage:
            def __init__(self, views, CH, real_ts):
                self.items = [(v, T("stg"), "bstg%d" % i) for i, v in enumerate(views)]
                self.CH = CH
                self.i = -1
                self.real_ts = real_ts

            def next(self):
                self.i = (self.i + 1) % len(self.items)
                return self.items[self.i]

            def release(self):
                g.op("pool", lambda e: e.nop(), writes=[it[1] for it in self.items] + self.real_ts)

        def load_vec(dst, tdst, src_ap, sem):
            g.op("sp", lambda e, s: e.dma_start(out=dst, in_=src_ap).then_inc(s, 16), writes=[tdst], dma=sem)

        def rstd_from(ps_ap, tps, scale, out_ap, tout):
            g.op("act", lambda e: e.activation(out=out_ap, in_=ps_ap, func=AF.Ln, bias=epsb[0:ps_ap.shape[0], :], scale=scale),
                 reads=[tps, tepsb], writes=[tout])
            g.op("act", lambda e: e.activation(out=out_ap, in_=out_ap, func=AF.Exp, scale=-0.5), reads=[tout], writes=[tout])

        g.op("pool", lambda e: e.memset(epsb[:], EPS), writes=[tepsb])
        g.op("pool", lambda e: e.memset(mhalf[:], -0.5), writes=[tmhalf])

        def rms_x(es, xt, txt, nw, sqr, hr, rsr, bank):
            sq, tsq, _ = sqr.next()
            g.op("act", lambda e: e.activation(out=sq[:, :, 0:nw], in_=xt[:, :, 0:nw], func=AF.Square), reads=[txt], writes=[tsq])
            pb, tpb = bank
            for k in range(8):
                g.op("pe", lambda e, k=k: e.matmul(pb[:, 0:nw], lhsT=ONES, rhs=sq[:, k, 0:nw], start=(k == 0), stop=(k == 7)),
                     reads=[tsq, tmats], writes=[tpb])
            rs, trs, _ = rsr.next()
            rstd_from(pb[:, 0:nw], tpb, 1.0 / D, rs[:, 0:nw], trs)
            hT, thT, _ = hr.next()
            g.op("dve", lambda e: e.tensor_tensor(out=hT[:, :, 0:nw], in0=xt[:, :, 0:nw],
                                                  in1=rs[:, 0:nw].unsqueeze(1).broadcast_to([128, 8, nw]), op=ALU.mult),
                 reads=[txt, trs], writes=[thT])
            return hT, thT

        def phase_end():
            g.fence()
            g.emit()

        for l in range(depth):
            xsrc = xT if l == 0 else xb
            xdst3 = xb if l == 0 else yT
            if depth == 1:
                xdst3 = yT
            xsrc_v = xsrc.rearrange("(k p) n -> p k n", p=128)

            if "p1" in phases:
                with ExitStack() as es:
                    sb = lambda n, s, d: es.enter_context(nc.sbuf_tensor(un(n), s, d))
                    Win = sb("Win1", [128, 8, 3744], BF16); tWin = T()
                    wuq = sb("wuq", [128, 2, 384], BF16); twuq = T()
                    wukv = sb("wukv", [128, 1, 512], BF16); twukv = T()
                    wst = sb("wst", [128, 4, 128], BF16); twst = T()
                    gv = sb("gv", [128, 16], F32); tgv = T()
                    gq = sb("gq", [128, 4], F32); tgq = T()
                    lnb = sb("lnb", [128, 2, 256], F32); tlnb = T()
                    bs = sb("bs", [128, 2, 128], F32); tbs = T()
                    load_vec(gv[:, 0:8], tgv, W["attn_norm"][l].rearrange("(k p) -> p k", p=128), "v0")
                    load_vec(gv[:, 8:10], tgv, W["mla_g_cq"][l].rearrange("(k p) -> p k", p=128), "v1")
                    load_vec(gv[:, 10:11], tgv, W["mla_g_ckv"][l].rearrange("(k p) -> p k", p=128), "v2")
                    load_vec(gq[0:96, 0:1], tgq, W["mla_g_qn"][l].rearrange("(p o) -> p o", o=1), "v3")
                    load_vec(gq[0:96, 1:2], tgq, W["mla_g_kn"][l].rearrange("(p o) -> p o", o=1), "v0")
                    for b in (0, 64):
                        load_vec(gq[b:b + 64, 2:3], tgq, W["dil_g_qn"][l].rearrange("(p o) -> p o", o=1), "v1")
                        load_vec(gq[b:b + 64, 3:4], tgq, W["dil_g_kn"][l].rearrange("(p o) -> p o", o=1), "v2")
                    g.op("dve", lambda e: e.tensor_scalar(out=gq[0:96, 0:1], in0=gq[0:96, 0:1], scalar1=96.0 ** -0.5, scalar2=None, op0=ALU.mult),
                         reads=[tgq], writes=[tgq])
                    g.op("dve", lambda e: e.tensor_scalar(out=gq[:, 2:3], in0=gq[:, 2:3], scalar1=0.125, scalar2=None, op0=ALU.mult),
                         reads=[tgq], writes=[tgq])
                    load_vec(lnb[:, 0, :], tlnb, W["sgu_ln_g"][l:l + 1, :].partition_broadcast(128), "v3")
                    load_vec(lnb[:, 1, :], tlnb, W["sgu_ln_b"][l:l + 1, :].partition_broadcast(128), "v0")
                    for gi in range(4):
                        load_vec(bs[(gi % 2) * 64:(gi % 2) * 64 + 64, gi // 2, :], tbs, W["sgu_b_s"][l, gi:gi + 1, :].partition_broadcast(64), "v%d" % (gi % 4))
                    xr = Ring(nc, es, "xt", [128, 8, 512], F32, 1)
                    sqr = Ring(nc, es, "sq", [128, 8, 512], BF16, 1)
                    hr = Ring(nc, es, "hT", [128, 8, 512], BF16, 2)
                    rsr = Ring(nc, es, "rs", [128, 512], F32, 2)
                    hAo = Ring(nc, es, "hAo", [128, 2, 512], BF16, 2)
                    sgr = Ring(nc, es, "sig", [128, 512], BF16, 2)
                    ubr = Ring(nc, es, "ub", [128, 2, 512], BF16, 1)
                    vgr = Ring(nc, es, "vg", [128, 256], F32, 3)
                    vnr = Ring(nc, es, "vn", [128, 256], F32, 3)
                    vbr = Ring(nc, es, "vnb", [128, 256], BF16, 4)
                    str_ = Ring(nc, es, "bst", [128, 8], F32, 4)
                    ybr = Ring(nc, es, "ybo", [128, 2, 512], BF16, 1)
                    tmr = Ring(nc, es, "tmp", [128, 512], F32, 1)
                    cqr = Ring(nc, es, "cqn", [128, 3, 512], BF16, 1)
                    sqs = Ring(nc, es, "sqs", [128, 512], BF16, 2)
                    qor = Ring(nc, es, "qo", [96, 4, 512], BF16, 1)
                    kor = Ring(nc, es, "ko", [96, 4, 512], BF16, 1)
                    vcr = Ring(nc, es, "vco", [128, 4, 256], BF16, 1)
                    qdr = Ring(nc, es, "qdo", [128, 6, 512], BF16, 1)
                    kdr = Ring(nc, es, "kdo", [128, 6, 512], BF16, 1)
                    vdr = Ring(nc, es, "vdo", [128, 4, 1536], BF16, 1)
                    tbc = Ring(nc, es, "tbc", [96, 2, 512], F32, 1)
                    tbd = Ring(nc, es, "tbd", [128, 2, 512], F32, 1)
                    krr = Ring(nc, es, "krs", [128, 512], BF16, 1)
                    sq2 = Ring(nc, es, "sq2", [128, 2, 512], BF16, 2)
                    qb2 = Ring(nc, es, "qb2", [128, 2, 512], BF16, 2)
                    rs2 = Ring(nc, es, "rs2", [128, 2, 512], F32, 2)
                    t12 = Ring(nc, es, "t12", [128, 2, 512], F32, 2)
                    t22 = Ring(nc, es, "t22", [128, 2, 512], BF16, 1)
                    prot = Rot(BK[0:4])
                    psvb = [BK[4], BK[5]]
                    pairrot = Rot([0, 2])
                    for ring in (vdr,):
                        for i_ in range(ring.n):
                            tl, tt = ring.tiles[i_], ring.ts[i_]
                            g.op("pool", lambda e, tl=tl: e.memset(tl[:], 1.0), writes=[tt])

                    def bank2(i):
                        return PSA[:, i * 512:(i + 2) * 512].rearrange("p (a n) -> p a n", a=2), [bts[i], bts[i + 1]]

                    def proj(hT, thT, c0, c1, bank=None, nw=512):
                        pb, tpb = bank if bank is not None else prot.next()
                        for k in range(8):
                            g.op("pe", lambda e, k=k: e.matmul(pb[0:c1 - c0, 0:nw], lhsT=Win[:, k, c0:c1], rhs=hT[:, k, 0:nw],
                                                               start=(k == 0), stop=(k == 7)), reads=[tWin, thT], writes=[tpb])
                        return pb, tpb

                    def nr_stage1(P, gcol, pbase):
                        src, tsrc = bank2(pbase)
                        sq_, tsq_, _ = sq2.next()
                        qb_, tqb_, _ = qb2.next()
                        g.op("act", lambda e: e.activation(out=sq_[0:P], in_=src[0:P], func=AF.Square), reads=tsrc, writes=[tsq_])
                        g.op("act", lambda e: e.activation(out=qb_[0:P], in_=src[0:P], func=AF.Identity, scale=gq[0:P, gcol:gcol + 1]),
                             reads=tsrc + [tgq], writes=[tqb_])
                        return (sq_, tsq_, qb_, tqb_)

                    def nr_stage2(st1, P, onesM, inv_dim, R, tab, ttab, out_ap, tout):
                        (sq_, tsq_, qb_, tqb_) = st1
                        pss, tpss = bank2(4)
                        prr, tprr = bank2(6)
                        for a_ in range(2):
                            g.op("pe", lambda e, a_=a_: e.matmul(pss[0:P, a_, :], lhsT=onesM, rhs=sq_[0:P, a_, :], start=True, stop=True),
                                 reads=[tsq_, tmats], writes=[tpss[a_]])
                        for a_ in range(2):
                            g.op("pe", lambda e, a_=a_: e.matmul(prr[0:P, a_, :], lhsT=R, rhs=qb_[0:P, a_, :], start=True, stop=True),
                                 reads=[tqb_, tmats], writes=[tprr[a_]])
                        rs_, trs_, _ = rs2.next()
                        g.op("act", lambda e: e.activation(out=rs_[0:P], in_=pss[0:P], func=AF.Ln, bias=epsb[0:P, :], scale=inv_dim),
                             reads=tpss + [tepsb], writes=[trs_])
                        g.op("act", lambda e: e.activation(out=rs_[0:P], in_=rs_[0:P], func=AF.Exp, scale=-0.5), reads=[trs_], writes=[trs_])
                        t2_, tt2_, _ = t22.next()
                        g.op("dve", lambda e: e.tensor_tensor(out=t2_[0:P], in0=prr[0:P], in1=tab[0:P, 1:2, :].broadcast_to([P, 2, 512]), op=ALU.mult),
                             reads=tprr + [ttab], writes=[tt2_])
                        t1_, tt1_, _ = t12.next()
                        g.op("dve", lambda e: e.tensor_tensor(out=t1_[0:P], in0=qb_[0:P], in1=tab[0:P, 0:1, :].broadcast_to([P, 2, 512]), op=ALU.mult),
                             reads=[tqb_, ttab], writes=[tt1_])
                        g.op("dve", lambda e: e.tensor_tensor(out=t1_[0:P], in0=t1_[0:P], in1=t2_[0:P], op=ALU.add), reads=[tt1_, tt2_], writes=[tt1_])
                        g.op("pool", lambda e: e.tensor_tensor(out=out_ap, in0=t1_[0:P], in1=rs_[0:P], op=ALU.mult), reads=[tt1_, trs_], writes=[tout])

                    def front(tile):
                        n0 = tile[2]
                        xt, txt, xs = xr.next()
                        g.op("sp", lambda e, s_, xt=xt, n0=n0: e.dma_start(out=xt[:], in_=xsrc_v[:, :, n0:n0 + 512]).then_inc(s_, 16),
                             writes=[txt], dma=xs)
                        return rms_x(es, xt, txt, 512, sqr, hr, rsr, BK[6])

                    xtv = xr.tiles[0][:].rearrange("p a b -> p (a b)")
                    stage = BigStage([xtv[:, 0:2048], xtv[:, 2048:4096], sqr.tiles[0][:].rearrange("p a b -> p (a b)").bitcast(F32)], 2048, [xr.ts[0], sqr.ts[0]])
                    load_w(stage, Win, tWin, W["w_in"][l][:, 0:3744], gv[:, 0:8], tgv)
                    load_w(stage, wuq, twuq, W["mla_w_uq"][l], gv[:, 8:10], tgv)
                    load_w(stage, wukv, twukv, W["mla_w_ukv"][l], gv[:, 10:11], tgv)
                    load_w(stage, wst, twst, wsT[l].rearrange("g s t -> (g s) t"), None, None)
                    stage.release()
                    nxt = front(tiles[0])
                    for ti, (s, t0, n0, pp0) in enumerate(tiles):
                        hT, thT = nxt
                        vbs = []
                        for c in range(4):
                            pv, tpv = prot.next()
                            for k in range(8):
                                g.op("pe", lambda e, k=k, c=c, pv=pv, hT=hT: e.matmul(pv[:, 0:256], lhsT=hT[:, k, c * 128:(c + 1) * 128], rhs=Win[:, k, 768:1024],
                                                                                    start=(k == 0), stop=(k == 7)), reads=[tWin, thT], writes=[tpv])
                            vg, tvg, _ = vgr.next()
                            g.op("act", lambda e, vg=vg, pv=pv: e.activation(out=vg[:], in_=pv[:, 0:256], func=AF.Gelu_apprx_tanh), reads=[tpv], writes=[tvg])
                            st_, tst_, _ = str_.next()
                            g.op("dve", lambda e, st_=st_, vg=vg: e.bn_stats(out=st_[:, 0:6], in_=vg[:]), reads=[tvg], writes=[tst_])
                            g.op("dve", lambda e, st_=st_: e.bn_aggr(out=st_[:, 6:8], in_=st_[:, 0:6]), reads=[tst_], writes=[tst_])
                            g.op("pool", lambda e, st_=st_: e.tensor_scalar(out=st_[:, 7:8], in0=st_[:, 7:8], scalar1=EPS, scalar2=1.0, op0=ALU.add, op1=ALU.mult),
                                 reads=[tst_], writes=[tst_])
                            g.op("pool", lambda e, st_=st_: e.tensor_tensor(out=st_[:, 7:8], in0=st_[:, 7:8], in1=mhalf[:, :], op=ALU.pow), reads=[tst_, tmhalf], writes=[tst_])
                            vn, tvn, _ = vnr.next()
                            g.op("dve", lambda e, vn=vn, vg=vg, st_=st_: e.tensor_scalar(out=vn[:], in0=vg[:], scalar1=st_[:, 6:7], scalar2=st_[:, 7:8],
                                                                                          op0=ALU.subtract, op1=ALU.mult), reads=[tvg, tst_], writes=[tvn])
                            g.op("pool", lambda e, vn=vn: e.tensor_tensor(out=vn[:], in0=vn[:], in1=lnb[:, 0, :], op=ALU.mult), reads=[tvn, tlnb], writes=[tvn])
                            vb, tvb, _ = vbr.next()
                            g.op("pool", lambda e, vn=vn, vb=vb: e.tensor_tensor(out=vb[:], in0=vn[:], in1=lnb[:, 1, :], op=ALU.add), reads=[tvn, tlnb], writes=[tvb])
                            vbs.append((vb, tvb))
                        ub, tub, _ = ubr.next()
                        for j in range(2):
                            pu, tpu = proj(hT, thT, 512 + j * 128, 512 + j * 128 + 128)
                            g.op("act", lambda e, ub=ub, pu=pu, j=j: e.activation(out=ub[:, j, :], in_=pu[:], func=AF.Gelu_apprx_tanh), reads=[tpu], writes=[tub])
                        ho, tho, hos = hAo.next()
                        for j in range(2):
                            pa, tpa = proj(hT, thT, j * 128, j * 128 + 128)
                            pg, tpg = proj(hT, thT, 256 + j * 128, 256 + j * 128 + 128)
                            sg, tsg, _ = sgr.next()
                            g.op("act", lambda e, sg=sg, pg=pg: e.activation(out=sg[:], in_=pg[:], func=AF.Sigmoid), reads=[tpg], writes=[tsg])
                            g.op("dve", lambda e, ho=ho, pa=pa, sg=sg, j=j: e.tensor_tensor(out=ho[:, j, :], in0=pa[:], in1=sg[:], op=ALU.mult),
                                 reads=[tpa, tsg], writes=[tho])
                        psv = psvb
                        for c in range(4):
                            vb, tvb = vbs[c]
                            for gi in range(4):
                                pj, tpj = psv[gi // 2]
                                pbase = (gi % 2) * 64
                                g.op("pe", lambda e, pj=pj, pbase=pbase, vb=vb, gi=gi, c=c: e.matmul(
                                    pj[pbase:pbase + 64, c * 128:(c + 1) * 128], lhsT=vb[:, gi * 64:(gi + 1) * 64], rhs=wst[:, gi, :], start=True, stop=True),
                                    reads=[tvb, twst], writes=[tpj])
                        yb_, tyb, ybs = ybr.next()
                        for j in range(2):
                            pj, tpj = psv[j]
                            tm, ttm, _ = tmr.next()
                            g.op("dve", lambda e, tm=tm, pj=pj, j=j: e.tensor_tensor(out=tm[:].rearrange("p (c t) -> p c t", c=4), in0=pj[:].rearrange("p (c t) -> p c t", c=4),
                                                                                   in1=bs[:, j:j + 1, :].broadcast_to([128, 4, 128]), op=ALU.add),
                                 reads=[tpj, tbs], writes=[ttm])
                            g.op("pool", lambda e, tm=tm, yb_=yb_, ub=ub, j=j: e.tensor_tensor(out=yb_[:, j, :], in0=tm[:], in1=ub[:, j, :], op=ALU.mult),
                                 reads=[ttm, tub], writes=[tyb])
                        if ti + 1 < len(tiles):
                            nxt = front(tiles[ti + 1])
                        tc_, ttc, tcs = tbc.next()
                        g.op("sp", lambda e, s_, tc_=tc_, t0=t0, s=s: e.dma_start(out=tc_[:], in_=ropec.rearrange("c p n -> p c n")[:, :, t0 + rpos[s]:t0 + rpos[s] + 512]).then_inc(s_, 16),
                             writes=[ttc], dma=tcs)
                        td_, ttd, tds = tbd.next()
                        g.op("sp", lambda e, s_, td_=td_, t0=t0, s=s: e.dma_start(out=td_[:], in_=roped.rearrange("c p n -> p c n")[:, :, t0 + rpos[s]:t0 + rpos[s] + 512]).then_inc(s_, 16),
                             writes=[ttd], dma=tds)
                        cq, tcq, _ = cqr.next()
                        pcs = [proj(hT, thT, 1024 + j * 128, 1024 + j * 128 + 128, bank=BK[4 + j]) for j in range(3)]
                        krs, tkrs, _ = krr.next()
                        pkr, tpkr = BK[7]
                        for k in range(8):
                            g.op("pe", lambda e, k=k, hT=hT: e.matmul(pkr[64:96, :], lhsT=Win[:, k, 1408:1440], rhs=hT[:, k, :], start=(k == 0), stop=(k == 7)),
                                 reads=[tWin, thT], writes=[tpkr])
                        g.op("act", lambda e, krs=krs: e.activation(out=krs[64:96, :], in_=pkr[64:96, :], func=AF.Copy), reads=[tpkr], writes=[tkrs])
                        sqa = []
                        for j in range(3):
                            sq_, tsq_, _ = sqs.next() if j < 2 else tmr.next()
                            sqa.append((sq_, tsq_))
                        sqa[2] = sqa[0]
                        for j in range(2):
                            g.op("act", lambda e, sq_=sqa[j][0], p=pcs[j][0]: e.activation(out=sq_[:], in_=p[:], func=AF.Square), reads=[pcs[j][1]], writes=[sqa[j][1]])
                        pn, tpn = BK[7]
                        for j in range(2):
                            g.op("pe", lambda e, j=j, pn=pn: e.matmul(pn[:], lhsT=ONES, rhs=sqa[j][0][:], start=(j == 0), stop=(j == 1)),
                                 reads=[sqa[j][1], tmats], writes=[tpn])
                        rs_, trs_, _ = rsr.next()
                        rstd_from(pn[:], tpn, 1.0 / 256, rs_[:], trs_)
                        for j in range(2):
                            g.op("dve", lambda e, j=j, cq=cq, rs_=rs_, p=pcs[j][0]: e.tensor_tensor(out=cq[:, j, :], in0=p[:], in1=rs_[:], op=ALU.mult),
                                 reads=[pcs[j][1], trs_], writes=[tcq])
                        g.op("act", lambda e, sq_=sqa[2][0], p=pcs[2][0]: e.activation(out=sq_[:], in_=p[:], func=AF.Square), reads=[pcs[2][1]], writes=[sqa[2][1]])
                        g.op("pe", lambda e, pn=pn, sq_=sqa[2][0]: e.matmul(pn[:], lhsT=ONES, rhs=sq_[:], start=True, stop=True), reads=[sqa[2][1], tmats], writes=[tpn])
                        rs2_, trs2_, _ = rsr.next()
                        rstd_from(pn[:], tpn, 1.0 / 128, rs2_[:], trs2_)
                        g.op("dve", lambda e, cq=cq, rs2_=rs2_, p=pcs[2][0]: e.tensor_tensor(out=cq[:, 2, :], in0=p[:], in1=rs2_[:], op=ALU.mult),
                             reads=[pcs[2][1], trs2_], writes=[tcq])
                        qo, tqo, qos = qor.next()
                        ko, tko, kos = kor.next()
                        qdo, tqdo, qds = qdr.next()
                        kdo, tkdo, kds = kdr.next()
                        units = []
                        for c6 in (0, 2, 4):
                            units.append(("dq", c6))
                        for c6 in (0, 2, 4):
                            units.append(("dk", c6))
                        for h in (0, 2):
                            units.append(("cq", h))
                        for h in (0, 2):
                            units.append(("ck", h))

                        def u_stage1(u):
                            kind, i0 = u
                            pbase = pairrot.next()
                            if kind in ("dq", "dk"):
                                base = 1440 if kind == "dq" else 2208
                                for a_ in range(2):
                                    proj(hT, thT, base + (i0 + a_) * 128, base + (i0 + a_) * 128 + 128, bank=BK[pbase + a_])
                                return nr_stage1(128, 2 if kind == "dq" else 3, pbase)
                            if kind == "cq":
                                for a_ in range(2):
                                    pq, tpq = BK[pbase + a_]
                                    h = i0 + a_
                                    for kk in range(2):
                                        g.op("pe", lambda e, pq=pq, kk=kk, h=h, cq=cq: e.matmul(pq[0:96, :], lhsT=wuq[:, kk, h * 96:(h + 1) * 96], rhs=cq[:, kk, :],
                                                                                              start=(kk == 0), stop=(kk == 1)), reads=[twuq, tcq], writes=[tpq])
                                return nr_stage1(96, 0, pbase)
                            for a_ in range(2):
                                pk, tpk = BK[pbase + a_]
                                h = i0 + a_
                                g.op("pe", lambda e, pk=pk, h=h, cq=cq: e.matmul(pk[0:64, :], lhsT=wukv[:, 0, h * 128:h * 128 + 64], rhs=cq[:, 2, :], start=True, stop=True),
                                     reads=[twukv, tcq], writes=[tpk])
                                g.op("pe", lambda e, pk=pk, krs=krs: e.matmul(pk[64:96, :], lhsT=mats[64:96, 4, 64:96], rhs=krs[64:96, :], start=True, stop=True),
                                     reads=[tmats, tkrs], writes=[tpk])
                            return nr_stage1(96, 1, pbase)

                        def u_stage2(u, st1):
                            kind, i0 = u
                            if kind == "dq":
                                nr_stage2(st1, 128, BONES, 1.0 / 64, RD, td_, ttd, qdo[:, i0:i0 + 2, :], tqdo)
                            elif kind == "dk":
                                nr_stage2(st1, 128, BONES, 1.0 / 64, RD, td_, ttd, kdo[:, i0:i0 + 2, :], tkdo)
                            elif kind == "cq":
                                nr_stage2(st1, 96, ONES[0:96, 0:96], 1.0 / 96, RC[0:96, 0:96], tc_, ttc, qo[:, i0:i0 + 2, :], tqo)
                            else:
                                nr_stage2(st1, 96, ONES[0:96, 0:96], 1.0 / 96, RC[0:96, 0:96], tc_, ttc, ko[:, i0:i0 + 2, :], tko)

                        sts = [u_stage1(units[0])]
                        for ui in range(len(units)):
                            if ui + 1 < len(units):
                                sts.append(u_stage1(units[ui + 1]))
                            u_stage2(units[ui], sts[ui])
                        vco, tvco, vcs = vcr.next()
                        wv = wukv[:, 0, :].rearrange("p (h t d) -> p h t d", h=4, t=2)[:, :, 1, :]
                        for c in range(4):
                            pv, tpv = prot.next()
                            g.op("pe", lambda e, pv=pv, c=c, cq=cq: e.matmul(pv[:, 0:256].rearrange("p (h d) -> p h d", h=4), lhsT=cq[:, 2, c * 128:(c + 1) * 128], rhs=wv, start=True, stop=True),
                                 reads=[twukv, tcq], writes=[tpv])
                            g.op("act", lambda e, pv=pv, c=c, vco=vco: e.activation(out=vco[:, c, :], in_=pv[:, 0:256], func=AF.Copy), reads=[tpv], writes=[tvco])
                        vdo, tvdo, vds = vdr.next()
                        for c in range(4):
                            for gi in range(3):
                                pv, tpv = prot.next()
                                for k in range(8):
                                    g.op("pe", lambda e, pv=pv, k=k, c=c, gi=gi, hT=hT: e.matmul(pv[:, 0:256], lhsT=hT[:, k, c * 128:(c + 1) * 128],
                                                                                               rhs=Win[:, k, 2976 + gi * 256:2976 + gi * 256 + 256], start=(k == 0), stop=(k == 7)),
                                         reads=[tWin, thT], writes=[tpv])
                                if gi % 2 == 0:
                                    g.op("act", lambda e, pv=pv, c=c, gi=gi, vdo=vdo: e.activation(out=vdo[:, c, gi * 512:(gi + 1) * 512].rearrange("p (h x) -> p h x", h=4)[:, :, 0:64], in_=pv[:, 0:256].rearrange("p (h d) -> p h d", h=4), func=AF.Copy),
                                         reads=[tpv], writes=[tvdo])
                                else:
                                    g.op("dve", lambda e, pv=pv, c=c, gi=gi, vdo=vdo: e.tensor_copy(out=vdo[:, c, gi * 512:(gi + 1) * 512].rearrange("p (h x) -> p h x", h=4)[:, :, 0:64], in_=pv[:, 0:256].rearrange("p (h d) -> p h d", h=4)),
                                         reads=[tpv], writes=[tvdo])
                        def st(dst, src, tsrc, sem):
                            g.op("sp", lambda e, s_: e.dma_start(out=dst, in_=src).then_inc(s_, 16), reads=[tsrc], dma="st_" + sem)
                        st(hA.rearrange("(j p) n -> p j n", p=128)[:, :, pp0:pp0 + 512], ho[:], tho, hos)
                        st(yB.rearrange("(j p) n -> p j n", p=128)[:, :, n0:n0 + 512], yb_[:], tyb, ybs)
                        st(qc[:, :, n0:n0 + 512], qo[:], tqo, qos)
                        st(kc[:, :, n0:n0 + 512], ko[:], tko, kos)
                        st(vc[n0:n0 + 512, :].rearrange("(c p) f -> p c f", p=128), vco[:], tvco, vcs)
                        st(qd[:, :, n0:n0 + 512], qdo[:], tqdo, qds)
                        st(kd[:, :, pp0:pp0 + 512], kdo[:], tkdo, kds)
                        st(vd[pp0:pp0 + 512, :].rearrange("(c p) f -> p c f", p=128), vdo[:], tvdo, vds)
                        if joined and ((s == 0 and t0 >= seqs[0] - PAD) or (s == 1 and t0 < PAD)):
                            pp2 = (poff[1] + PAD + t0 - seqs[0]) if s == 0 else (poff[0] + PAD + seqs[0] + t0)
                            jcol = jft[:, 0:1]
                            g.op("pool", lambda e, ho=ho: e.tensor_scalar(out=ho[:], in0=ho[:], scalar1=jcol, scalar2=0.0, op0=ALU.mult, op1=ALU.add), reads=[tho, tjft], writes=[tho])
                            g.op("pool", lambda e, kdo=kdo: e.tensor_scalar(out=kdo[:], in0=kdo[:], scalar1=jcol, scalar2=0.0, op0=ALU.mult, op1=ALU.add), reads=[tkdo, tjft], writes=[tkdo])
                            g.op("pool", lambda e, vdo=vdo: e.tensor_scalar(out=vdo[:], in0=vdo[:], scalar1=jcol, scalar2=0.0, op0=ALU.mult, op1=ALU.add), reads=[tvdo, tjft], writes=[tvdo])
                            st(hA.rearrange("(j p) n -> p j n", p=128)[:, :, pp2:pp2 + 512], ho[:], tho, hos)
                            st(kd[:, :, pp2:pp2 + 512], kdo[:], tkdo, kds)
                            st(vd[pp2:pp2 + 512, :].rearrange("(c p) f -> p c f", p=128), vdo[:], tvdo, vds)
                            g.op("pool", lambda e, vdo=vdo: e.memset(vdo[:].rearrange("p c (q x) -> p c q x", x=128)[:, :, :, 64:128], 1.0), writes=[tvdo])
                    phase_end()
                    phase_end()

            if "p2a" in phases:
                with ExitStack() as es:
                    SM = max(gS.values())
                    Kc = es.enter_context(nc.sbuf_tensor(un("Kc"), [96, 4, SM], BF16))
                    Vc = es.enter_context(nc.sbuf_tensor(un("Vc"), [128, SM // 128, 4, 128], BF16))
                    tK = [[T() for _ in range(SM // 512)] for _ in range(4)]
                    tV = [T() for _ in range(SM // 512)]
                    qr = Ring(nc, es, "Qt", [96, 4, 512], BF16, 2)
                    ptr = Ring(nc, es, "pt", [128, 512], BF16, 6)
                    rzr = Ring(nc, es, "rz", [128, 512], F32, 2)
                    ocr = Ring(nc, es, "oct", [64, 4, 512], BF16, 2)
                    srot = Rot(BK[0:6])
                    urot = Rot(BK[6:8])
                    g.op("pool", lambda e: e.memset(Vc[:], 1.0), writes=tV)
                    cur_seq = -1
                    for (s, t0, n0, pp0) in tiles:
                        S = gS[s]
                        half = rpos[s] // 2048
                        if gof[s] != cur_seq:
                            cur_seq = gof[s]
                            b0 = gbase[s]
                            for c in range(S // 512):
                                for h in range(4):
                                    g.op("sp", lambda e, s_, h=h, c=c, b0=b0: e.dma_start(out=Kc[:, h, c * 512:(c + 1) * 512], in_=kc[:, h, b0 + c * 512:b0 + (c + 1) * 512]).then_inc(s_, 16),
                                         writes=[tK[h][c]], dma="K%d_%d" % (h, c % 2))
                                def ldv(e, s_, c=c, b0=b0):
                                    for a_ in range(4):
                                        r0 = b0 + c * 512 + a_ * 128
                                        e.dma_start(out=Vc[:, c * 4 + a_, :, 0:64], in_=vc[r0:r0 + 128, :].rearrange("p (h d) -> p h d", h=4)).then_inc(s_, 16)
                                g.op("sp", ldv, writes=[tV[c]], dma="V%d" % (c % 2), ndma=4)
                        Qt, tQ, qs = qr.next()
                        g.op("sp", lambda e, s_, Qt=Qt, n0=n0: e.dma_start(out=Qt[:], in_=qc[:, :, n0:n0 + 512]).then_inc(s_, 16), writes=[tQ], dma=qs)
                        oct_, toc, ocs = ocr.next()
                        nkt = S // 128
                        items = [(h, kt) for h in range(4) for kt in range(nkt)]
                        LA = 5
                        pts = {}
                        hb = {}
                        for i in range(len(items) + LA):
                            if i < len(items):
                                h, kt = items[i]
                                ps, tps = srot.next()
                                g.op("pe", lambda e, ps=ps, h=h, kt=kt, Qt=Qt: e.matmul(ps[:], lhsT=Kc[:, h, kt * 128:(kt + 1) * 128], rhs=Qt[:, h, :], start=True, stop=True),
                                     reads=[tK[h][kt // 4], tQ], writes=[tps])
                                pt, tpt, _ = ptr.next()
                                if S > 2048 and (kt // 16) != half:
                                    g.op("act", lambda e, ps=ps, pt=pt: e.activation(out=pt[:], in_=ps[:], func=AF.Exp, bias=jft[:, 1:2], scale=1.0), reads=[tps, tjft], writes=[tpt])
                                else:
                                    g.op("act", lambda e, ps=ps, pt=pt: e.activation(out=pt[:], in_=ps[:], func=AF.Exp), reads=[tps], writes=[tpt])
                                pts[i] = (pt, tpt)
                            j = i - LA
                            if j >= 0:
                                h, kt = items[j]
                                if kt == 0:
                                    hb[h] = urot.next()
                                pu, tpu = hb[h]
                                pt, tpt = pts.pop(j)
                                g.op("pe", lambda e, pu=pu, h=h, kt=kt, pt=pt, nkt=nkt: e.matmul(pu[:, :], lhsT=Vc[:, kt, h, :], rhs=pt[:], start=(kt == 0), stop=(kt == nkt - 1)),
                                     reads=[tV[kt // 4], tpt], writes=[tpu])
                                if kt == nkt - 1:
                                    rz, trz, _ = rzr.next()
                                    g.op("dve", lambda e, rz=rz, pu=pu: e.reciprocal(out=rz[64:128, :], in_=pu[64:128, :]), reads=[tpu], writes=[trz])
                                    g.op("dve", lambda e, rz=rz, pu=pu, h=h, oct_=oct_: e.tensor_tensor(out=oct_[:, h, :], in0=pu[0:64, :], in1=rz[64:128, :], op=ALU.mult),
                                         reads=[tpu, trz], writes=[toc])
                        g.op("sp", lambda e, s_, oct_=oct_, n0=n0: e.dma_start(out=oc[:, :, n0:n0 + 512], in_=oct_[:]).then_inc(s_, 16), reads=[toc], dma="st_" + ocs)
                    phase_end()

            if "p2b" in phases:
                with ExitStack() as es:
                    DIL = (1, 4, 16)
                    masks = es.enter_context(nc.sbuf_tensor(un("masks"), [128, 4, 512], BF16))
                    tmasks = T("masks")
                    g.op("sp", lambda e, s_: e.dma_start(out=masks[:], in_=masksd.rearrange("m p n -> p m n")).then_inc(s_, 16),
                         writes=[tmasks], dma="c1")
                    qr = Ring(nc, es, "Qd", [128, 4, 512], BF16, 2)
                    kw = [Ring(nc, es, "Kw%d" % gi, [128, 2, 512 + 128 * DIL[gi]], BF16, 2) for gi in range(2)]
                    NVT = (5, 8)
                    vw = [Ring(nc, es, "Vw%d" % gi, [128, NVT[gi], 512], BF16, 2) for gi in range(2)]
                    Qd2 = es.enter_context(nc.sbuf_tensor(un("Qd2"), [128, 2, 2048], BF16)); tQ2 = T()
                    Kw2 = es.enter_context(nc.sbuf_tensor(un("Kw2"), [128, 2, 4096], BF16)); tK2 = T()
                    Vw2 = es.enter_context(nc.sbuf_tensor(un("Vw2"), [128, 32, 512], BF16)); tV2 = T()
                    acc2 = es.enter_context(nc.sbuf_tensor(un("acc2"), [128, 4, 2048], F32))
                    tacc2 = [T() for _ in range(4)]
                    plr = Ring(nc, es, "pl", [128, 512], BF16, 3)
                    pur = Ring(nc, es, "pu", [128, 512], BF16, 3)
                    aur = Ring(nc, es, "accu", [128, 512], F32, 2)
                    odr = Ring(nc, es, "odt", [64, 4, 512], BF16, 2)
                    srot = Rot([(BK[0], BK[1]), (BK[2], BK[3])])
                    uzrot = Rot([BK[4], BK[5], BK[6]])
                    vdv = vd
                    ML = masks[:, 0, :]
                    MU = masks[:, 1, :]

                    def load_g2(s):
                        b0 = poff[s]
                        g.op("sp", lambda e, s_: e.dma_start(out=Qd2[:], in_=qd[:, 4:6, soff[s]:soff[s] + 2048]).then_inc(s_, 16), writes=[tQ2], dma="q2")
                        g.op("sp", lambda e, s_: e.dma_start(out=Kw2[:], in_=kd[:, 4:6, b0:b0 + 4096]).then_inc(s_, 16), writes=[tK2], dma="k2")

                        def ldv2(e, s_):
                            for m in range(2):
                                src = vdv[b0 + 2048 * m:b0 + 2048 * m + 2048, 1024:1536].rearrange("(k r) f -> k r f", r=16)
                                e.dma_start(out=Vw2[:, :, :].rearrange("p (r m) f -> p r m f", m=2)[:, :, m, :], in_=src).then_inc(s_, 16)
                        g.op("sp", ldv2, writes=[tV2], dma="v2", ndma=2)

                    def exp_mask(pL, tpL, pU, tpU):
                        el, tel, _ = plr.next()
                        eu, teu, _ = pur.next()
                        g.op("act", lambda e: e.activation(out=el[:], in_=pL[:], func=AF.Exp), reads=[tpL], writes=[tel])
                        g.op("act", lambda e: e.activation(out=eu[:], in_=pU[:], func=AF.Exp), reads=[tpU], writes=[teu])
                        g.op("pool", lambda e: e.tensor_tensor(out=el[:], in0=el[:], in1=ML, op=ALU.mult), reads=[tel, tmasks], writes=[tel])
                        g.op("dve", lambda e: e.tensor_tensor(out=eu[:], in0=eu[:], in1=MU, op=ALU.mult), reads=[teu, tmasks], writes=[teu])
                        return el, tel, eu, teu

                    def g2S(hh, rg):
                        pb_ = (hh % 2) * 64
                        (pL, tpL), (pU, tpU) = srot.next()
                        for j in range(4):
                            r = 4 * rg + j
                            rq = Qd2[pb_:pb_ + 64, hh // 2, r:r + 127 * 16 + 1:16]
                            k0 = Kw2[pb_:pb_ + 64, hh // 2, r:r + 127 * 16 + 1:16]
                            k1 = Kw2[pb_:pb_ + 64, hh // 2, r + 2048:r + 2048 + 127 * 16 + 1:16]
                            g.op("pe", lambda e, pL=pL, k0=k0, rq=rq, j=j: e.matmul(pL[:, j * 128:(j + 1) * 128], lhsT=k0, rhs=rq, start=True, stop=True), reads=[tK2, tQ2], writes=[tpL])
                            g.op("pe", lambda e, pU=pU, k1=k1, rq=rq, j=j: e.matmul(pU[:, j * 128:(j + 1) * 128], lhsT=k1, rhs=rq, start=True, stop=True), reads=[tK2, tQ2], writes=[tpU])
                        return (hh, rg) + exp_mask(pL, tpL, pU, tpU)

                    def g2PV(ctx):
                        (hh, rg, el, tel, eu, teu) = ctx
                        po, tpo = uzrot.next()
                        for j in range(4):
                            r = 4 * rg + j
                            g.op("pe", lambda e, po=po, r=r, j=j, el=el, hh=hh: e.matmul(po[:, j * 128:(j + 1) * 128], lhsT=Vw2[:, 2 * r, hh * 128:(hh + 1) * 128], rhs=el[:, j * 128:(j + 1) * 128], start=True, stop=False),
                                 reads=[tV2, tel], writes=[tpo])
                            g.op("pe", lambda e, po=po, r=r, j=j, eu=eu, hh=hh: e.matmul(po[:, j * 128:(j + 1) * 128], lhsT=Vw2[:, 2 * r + 1, hh * 128:(hh + 1) * 128], rhs=eu[:, j * 128:(j + 1) * 128], start=False, stop=True),
                                 reads=[tV2, teu], writes=[tpo])
                        dst = acc2[:, hh, :].rearrange("p (i r) -> p r i", r=16)[:, 4 * rg:4 * rg + 4, :]
                        src = po[:].rearrange("p (r i) -> p r i", r=4)
                        if rg % 2 == 0:
                            g.op("act", lambda e: e.activation(out=dst, in_=src, func=AF.Copy), reads=[tpo], writes=[tacc2[hh]])
                        else:
                            g.op("dve", lambda e: e.tensor_copy(out=dst, in_=src), reads=[tpo], writes=[tacc2[hh]])

                    nseg = len(seqs)
                    load_g2(0)
                    for sg in range(nseg):
                        its2 = [(hh, rg) for hh in range(4) for rg in range(4)]
                        ctx2 = [g2S(*its2[0])]
                        for ii in range(len(its2)):
                            if ii + 1 < len(its2):
                                ctx2.append(g2S(*its2[ii + 1]))
                            g2PV(ctx2[ii])
                        if sg + 1 < nseg:
                            load_g2(sg + 1)
                        for (s, t0, n0, pp0) in [t_ for t_ in tiles if t_[0] == sg]:
                            Qd, tQ, qs = qr.next()
                            g.op("sp", lambda e, s_, Qd=Qd, n0=n0: e.dma_start(out=Qd[:], in_=qd[:, 0:4, n0:n0 + 512]).then_inc(s_, 16), writes=[tQ], dma=qs)
                            KW, VW = [], []
                            for gi in range(2):
                                d = DIL[gi]
                                k_, tk_, ks = kw[gi].next()
                                g.op("sp", lambda e, s_, k_=k_, gi=gi, d=d, pp0=pp0: e.dma_start(out=k_[:], in_=kd[:, 2 * gi:2 * gi + 2, pp0 - 64 * d:pp0 + 512 + 64 * d]).then_inc(s_, 16),
                                     writes=[tk_], dma=ks)
                                KW.append((k_, tk_))
                                v_, tv_, vs = vw[gi].next()
                                base = pp0 - 64 * d
                                if gi == 0:
                                    g.op("sp", lambda e, s_, v_=v_, base=base: e.dma_start(out=v_[:], in_=vdv[base:base + 640, 0:512].rearrange("(m p) f -> p m f", p=128)).then_inc(s_, 16),
                                         writes=[tv_], dma=vs)
                                else:
                                    def ld1(e, s_, v_=v_, base=base):
                                        for m in range(2):
                                            src = vdv[base + 512 * m:base + 512 * m + 512, 512:1024].rearrange("(k r) f -> k r f", r=4)
                                            e.dma_start(out=v_[:, :, :].rearrange("p (r m) f -> p r m f", m=2)[:, :, m, :], in_=src).then_inc(s_, 16)
                                    g.op("sp", ld1, writes=[tv_], dma=vs, ndma=2)
                                VW.append((v_, tv_))
                            odt, tod, ods = odr.next()
                            accs = {}

                            def stageS(hh, gi):
                                pb_ = (hh % 2) * 64
                                d = DIL[gi]
                                k_, tk_ = KW[gi]
                                cq_ = 2 * gi + hh // 2
                                (pL, tpL), (pU, tpU) = srot.next()
                                for b in range(4):
                                    if gi == 0:
                                        rq = Qd[pb_:pb_ + 64, cq_, b * 128:(b + 1) * 128]
                                        kl = [k_[pb_:pb_ + 64, hh // 2, b * 128:b * 128 + 128], k_[pb_:pb_ + 64, hh // 2, (b + 1) * 128:(b + 1) * 128 + 128]]
                                    else:
                                        rq = Qd[pb_:pb_ + 64, cq_, b:512:d]
                                        kl = [k_[pb_:pb_ + 64, hh // 2, b:b + 127 * d + 1:d], k_[pb_:pb_ + 64, hh // 2, b + 128 * d:b + 128 * d + 127 * d + 1:d]]
                                    g.op("pe", lambda e, pL=pL, kl=kl, rq=rq, b=b: e.matmul(pL[:, b * 128:(b + 1) * 128], lhsT=kl[0], rhs=rq, start=True, stop=True),
                                         reads=[tk_, tQ], writes=[tpL])
                                    g.op("pe", lambda e, pU=pU, kl=kl, rq=rq, b=b: e.matmul(pU[:, b * 128:(b + 1) * 128], lhsT=kl[1], rhs=rq, start=True, stop=True),
                                         reads=[tk_, tQ], writes=[tpU])
                                return (hh, gi) + exp_mask(pL, tpL, pU, tpU)

                            def stagePV(ctx):
                                (hh, gi, el, tel, eu, teu) = ctx
                                d = DIL[gi]
                                v_, tv_ = VW[gi]
                                if gi == 0:
                                    accs[hh] = aur.next()
                                (acc, tacc, _) = accs[hh]
                                po, tpo = uzrot.next()
                                for b in range(4):
                                    vt = (b, b + 1) if gi == 0 else (2 * b, 2 * b + 1)
                                    g.op("pe", lambda e, po=po, v_=v_, vt=vt, el=el, b=b, hh=hh: e.matmul(
                                        po[:, b * 128:(b + 1) * 128], lhsT=v_[:, vt[0], hh * 128:(hh + 1) * 128], rhs=el[:, b * 128:(b + 1) * 128], start=True, stop=False),
                                        reads=[tv_, tel], writes=[tpo])
                                    g.op("pe", lambda e, po=po, v_=v_, vt=vt, eu=eu, b=b, hh=hh: e.matmul(
                                        po[:, b * 128:(b + 1) * 128], lhsT=v_[:, vt[1], hh * 128:(hh + 1) * 128], rhs=eu[:, b * 128:(b + 1) * 128], start=False, stop=True),
                                        reads=[tv_, teu], writes=[tpo])
                                if gi == 0:
                                    g.op("dve", lambda e, acc=acc, po=po, hh=hh, t0=t0: e.tensor_tensor(out=acc[:], in0=po[:], in1=acc2[:, hh, t0:t0 + 512], op=ALU.add),
                                         reads=[tpo, tacc2[hh]], writes=[tacc])
                                else:
                                    pf, tpf = BK[7]
                                    avz = acc[64:128, :].rearrange("p (i r) -> p r i", r=d)
                                    g.op("dve", lambda e, avz=avz, po=po, d=d: e.tensor_tensor(out=avz, in0=po[64:128, :].rearrange("p (r i) -> p r i", r=d), in1=avz, op=ALU.add),
                                         reads=[tpo, tacc], writes=[tacc])
                                    g.op("act", lambda e, acc=acc: e.activation(out=acc[64:128, :], in_=acc[64:128, :], func=AF.Ln), reads=[tacc], writes=[tacc])
                                    g.op("act", lambda e, acc=acc: e.activation(out=acc[64:128, :], in_=acc[64:128, :], func=AF.Exp, scale=-1.0), reads=[tacc], writes=[tacc])
                                    g.op("dve", lambda e, acc=acc, po=po, pf=pf, d=d: e.tensor_tensor(out=pf[0:64, :].rearrange("p (i r) -> p r i", r=d), in0=po[0:64, :].rearrange("p (r i) -> p r i", r=d),
                                                                                           in1=acc[0:64, :].rearrange("p (i r) -> p r i", r=d), op=ALU.add),
                                         reads=[tpo, tacc], writes=[tpf])
                                    g.op("dve", lambda e, acc=acc, pf=pf, odt=odt, hh=hh: e.tensor_tensor(out=odt[:, hh, :], in0=pf[0:64, :], in1=acc[64:128, :], op=ALU.mult),
                                         reads=[tpf, tacc], writes=[tod])

                            its = [(hh, gi) for hh in range(4) for gi in range(2)]
                            ctxs = [stageS(*its[0])]
                            for ii in range(len(its)):
                                if ii + 1 < len(its):
                                    ctxs.append(stageS(*its[ii + 1]))
                                stagePV(ctxs[ii])
                            g.op("sp", lambda e, s_, odt=odt, n0=n0: e.dma_start(out=od[:, :, n0:n0 + 512], in_=odt[:]).then_inc(s_, 16), reads=[tod], dma="st_" + ods)
                    phase_end()

            if "p2c" in phases:
                with ExitStack() as es:
                    sb = lambda n, s, d: es.enter_context(nc.sbuf_tensor(un(n), s, d))
                    WinG = sb("WinG", [128, 8, 4096], BF16); tWinG = T()
                    woa = sb("woa", [128, 2, 1024], BF16); twoa = T()
                    wob = sb("wob", [128, 2, 1024], BF16); twob = T()
                    woc = sb("woc", [64, 4, 1024], BF16); twoc = T()
                    wod = sb("wod", [64, 4, 1024], BF16); twod = T()
                    wout = sb("wout", [128, 8, 1024], BF16); twout = T()
                    Dg = sb("Dg", [128, 2, 31, 128], BF16); tDg = T()
                    gv = sb("gv", [128, 8], F32); tgv = T()
                    cvp = sb("cvp", [128, 2, 34], F32); tcvp = T()
                    load_vec(gv[:, 0:8], tgv, W["attn_norm"][l].rearrange("(k p) -> p k", p=128), "v0")
                    for j in range(2):
                        load_vec(cvp[:, j, 0:31], tcvp, W["conv_w"][l][:, j * 128:(j + 1) * 128].rearrange("k p -> p k"), "v1")
                        load_vec(cvp[:, j, 31:32], tcvp, W["conv_b"][l][j * 128:(j + 1) * 128].rearrange("(p o) -> p o", o=1), "v2")
                        load_vec(cvp[:, j, 32:33], tcvp, W["conv_ln_g"][l][j * 128:(j + 1) * 128].rearrange("(p o) -> p o", o=1), "v3")
                        load_vec(cvp[:, j, 33:34], tcvp, W["conv_ln_b"][l][j * 128:(j + 1) * 128].rearrange("(p o) -> p o", o=1), "v0")
                    for j in range(2):
                        for k in range(31):
                            eng = "dve" if k % 2 == 0 else "pool"
                            g.op(eng, lambda e, j=j, k=k: e.tensor_scalar(out=Dg[:, j, k, :], in0=mats[:, 4, :], scalar1=cvp[:, j, k:k + 1], scalar2=0.0, op0=ALU.mult, op1=ALU.add),
                                 reads=[tmats, tcvp], writes=[tDg])
                    xr = Ring(nc, es, "xt", [128, 8, 512], F32, 1)
                    hr = Ring(nc, es, "hT", [128, 8, 512], BF16, 1)
                    rsr = Ring(nc, es, "rs", [128, 512], F32, 2)
                    gtr = Ring(nc, es, "gt", [128, 4, 512], BF16, 2)
                    haw = Ring(nc, es, "haw", [128, 2, 542], BF16, 2)
                    cvr = Ring(nc, es, "cv", [128, 2, 512], F32, 1)
                    cbr = Ring(nc, es, "cvb", [128, 4, 512], BF16, 1)
                    mnr = Ring(nc, es, "mn", [128, 3, 512], F32, 1)
                    bar = Ring(nc, es, "bA", [128, 2, 512], BF16, 1)
                    ybr = Ring(nc, es, "yBt", [128, 2, 512], BF16, 1)
                    ocr = Ring(nc, es, "oct", [64, 4, 512], BF16, 1)
                    odr = Ring(nc, es, "odt", [64, 4, 512], BF16, 1)
                    mgr = Ring(nc, es, "mg", [128, 8, 512], BF16, 1)
                    sqr = hr
                    tpr = Ring(nc, es, "tp", [128, 4, 512], BF16, 1)
                    xor2 = Ring(nc, es, "xo2", [128, 512], F32, 2)
                    edr = Ring(nc, es, "edge", [128, 8, 1], F32, 1)
                    prot = Rot(BK[0:8])
                    xtv = xr.tiles[0][:].rearrange("p a b -> p (a b)")
                    stage = BigStage([xtv[:, 0:2048], xtv[:, 2048:4096], mgr.tiles[0][:].rearrange("p a b -> p (a b)").bitcast(F32)], 2048, [xr.ts[0], mgr.ts[0]])
                    load_w(stage, WinG, tWinG, W["w_in"][l][:, 3744:7840], gv[:, 0:8], tgv)
                    load_w(stage, woa, twoa, W["conv_w_o"][l], None, None)
                    load_w(stage, wob, twob, W["sgu_w_o"][l], None, None)
                    load_w(stage, woc, twoc, W["mla_w_o"][l], None, None, np_=64)
                    load_w(stage, wod, twod, W["dil_w_o"][l], None, None, np_=64)
                    load_w(stage, wout, twout, W["w_out"][l], None, None)
                    stage.release()

                    def front2(tile):
                        (s, t0, n0, pp0) = tile
                        xt, txt, xs = xr.next()
                        g.op("sp", lambda e, s_, xt=xt, n0=n0: e.dma_start(out=xt[:], in_=xsrc_v[:, :, n0:n0 + 512]).then_inc(s_, 16), writes=[txt], dma=xs)
                        hw, thw, hws = haw.next()
                        g.op("sp", lambda e, s_, hw=hw, pp0=pp0: e.dma_start(out=hw[:], in_=hA.rearrange("(j p) n -> p j n", p=128)[:, :, pp0 - 15:pp0 + 527]).then_inc(s_, 16), writes=[thw], dma=hws)
                        ybt, tybt, ybs = ybr.next()
                        g.op("sp", lambda e, s_, ybt=ybt, n0=n0: e.dma_start(out=ybt[:], in_=yB.rearrange("(j p) n -> p j n", p=128)[:, :, n0:n0 + 512]).then_inc(s_, 16), writes=[tybt], dma=ybs)
                        oct_, toct, ocs = ocr.next()
                        g.op("sp", lambda e, s_, oct_=oct_, n0=n0: e.dma_start(out=oct_[:], in_=oc[:, :, n0:n0 + 512]).then_inc(s_, 16), writes=[toct], dma=ocs)
                        odt, todt, ods = odr.next()
                        g.op("sp", lambda e, s_, odt=odt, n0=n0: e.dma_start(out=odt[:], in_=od[:, :, n0:n0 + 512]).then_inc(s_, 16), writes=[todt], dma=ods)
                        cv, tcv, _ = cvr.next()
                        cb, tcb, _ = cbr.next()
                        for j in range(2):
                            pc, tpc = prot.next()
                            for k in range(31):
                                g.op("pe", lambda e, pc=pc, j=j, k=k, hw=hw: e.matmul(pc[:], lhsT=Dg[:, j, k, :], rhs=hw[:, j, k:k + 512], start=(k == 0), stop=(k == 30)),
                                     reads=[tDg, thw], writes=[tpc])
                            g.op("act", lambda e, pc=pc, cv=cv, j=j: e.activation(out=cv[:, j, :], in_=pc[:], func=AF.Identity, bias=cvp[:, j, 31:32], scale=1.0),
                                 reads=[tpc, tcvp], writes=[tcv])
                            g.op("pool", lambda e, cv=cv, cb=cb, j=j: e.tensor_copy(out=cb[:, j, :], in_=cv[:, j, :]), reads=[tcv], writes=[tcb])
                            g.op("act", lambda e, cv=cv, cb=cb, j=j: e.activation(out=cb[:, 2 + j, :], in_=cv[:, j, :], func=AF.Square), reads=[tcv], writes=[tcb])
                        hT, thT = rms_x(es, xt, txt, 512, sqr, hr, rsr, prot.next())
                        p1, tp1 = prot.next()
                        p2, tp2 = prot.next()
                        for j in range(2):
                            g.op("pe", lambda e, p1=p1, cb=cb, j=j: e.matmul(p1[:], lhsT=ONES, rhs=cb[:, j, :], start=(j == 0), stop=(j == 1)), reads=[tcb, tmats], writes=[tp1])
                        for j in range(2):
                            g.op("pe", lambda e, p2=p2, cb=cb, j=j: e.matmul(p2[:], lhsT=ONES, rhs=cb[:, 2 + j, :], start=(j == 0), stop=(j == 1)), reads=[tcb, tmats], writes=[tp2])
                        mn, tmn, _ = mnr.next()
                        g.op("dve", lambda e, mn=mn, p1=p1: e.tensor_scalar(out=mn[:, 0, :], in0=p1[:], scalar1=1.0 / 256, scalar2=None, op0=ALU.mult), reads=[tp1], writes=[tmn])
                        g.op("pool", lambda e, mn=mn: e.tensor_tensor(out=mn[:, 1, :], in0=mn[:, 0, :], in1=mn[:, 0, :], op=ALU.mult), reads=[tmn], writes=[tmn])
                        g.op("dve", lambda e, mn=mn, p2=p2: e.scalar_tensor_tensor(out=mn[:, 2, :], in0=p2[:], scalar=1.0 / 256, in1=mn[:, 1, :], op0=ALU.mult, op1=ALU.subtract),
                             reads=[tp2, tmn], writes=[tmn])
                        g.op("act", lambda e, mn=mn: e.activation(out=mn[:, 2, :], in_=mn[:, 2, :], func=AF.Ln, bias=epsb[:, :], scale=1.0), reads=[tmn, tepsb], writes=[tmn])
                        g.op("act", lambda e, mn=mn: e.activation(out=mn[:, 2, :], in_=mn[:, 2, :], func=AF.Exp, scale=-0.5), reads=[tmn], writes=[tmn])
                        bA, tbA, _ = bar.next()
                        for j in range(2):
                            g.op("dve", lambda e, cv=cv, mn=mn, j=j: e.tensor_tensor(out=cv[:, j, :], in0=cv[:, j, :], in1=mn[:, 0, :], op=ALU.subtract), reads=[tcv, tmn], writes=[tcv])
                            g.op("pool", lambda e, cv=cv, mn=mn, j=j: e.tensor_tensor(out=cv[:, j, :], in0=cv[:, j, :], in1=mn[:, 2, :], op=ALU.mult), reads=[tcv, tmn], writes=[tcv])
                            g.op("act", lambda e, cv=cv, bA=bA, j=j: e.activation(out=bA[:, j, :], in_=cv[:, j, :], func=AF.Silu, bias=cvp[:, j, 33:34], scale=cvp[:, j, 32:33]),
                                 reads=[tcv, tcvp], writes=[tbA])
                        return (hT, thT, bA, tbA, ybt, tybt, oct_, toct, odt, todt)

                    nxt2 = front2(tiles[0])
                    for ti, (s, t0, n0, pp0) in enumerate(tiles):
                        (hT, thT, bA, tbA, ybt, tybt, oct_, toct, odt, todt) = nxt2
                        mg, tmg, _ = mgr.next()
                        for m in range(8):
                            gt, tgt, _ = gtr.next()
                            for i in range(4):
                                pg, tpg = prot.next()
                                c0 = (i * 8 + m) * 128
                                for k in range(8):
                                    g.op("pe", lambda e, pg=pg, k=k, c0=c0, hT=hT: e.matmul(pg[:], lhsT=WinG[:, k, c0:c0 + 128], rhs=hT[:, k, :], start=(k == 0), stop=(k == 7)),
                                         reads=[tWinG, thT], writes=[tpg])
                                g.op("act", lambda e, pg=pg, gt=gt, i=i: e.activation(out=gt[:, i, :], in_=pg[:], func=AF.Sigmoid), reads=[tpg], writes=[tgt])
                            tp, ttp, _ = tpr.next()
                            ys = []
                            pya = prot.next()
                            for j in range(2):
                                g.op("pe", lambda e, p=pya[0], j=j, m=m: e.matmul(p[:], lhsT=woa[:, j, m * 128:(m + 1) * 128], rhs=bA[:, j, :], start=(j == 0), stop=(j == 1)),
                                     reads=[twoa, tbA], writes=[pya[1]])
                            pyb = prot.next()
                            for j in range(2):
                                g.op("pe", lambda e, p=pyb[0], j=j, m=m: e.matmul(p[:], lhsT=wob[:, j, m * 128:(m + 1) * 128], rhs=ybt[:, j, :], start=(j == 0), stop=(j == 1)),
                                     reads=[twob, tybt], writes=[pyb[1]])
                            pyc = prot.next()
                            for h in range(4):
                                g.op("pe", lambda e, p=pyc[0], h=h, m=m: e.matmul(p[:], lhsT=woc[:, h, m * 128:(m + 1) * 128], rhs=oct_[:, h, :], start=(h == 0), stop=(h == 3)),
                                     reads=[twoc, toct], writes=[pyc[1]])
                            pyd = prot.next()
                            for h in range(4):
                                g.op("pe", lambda e, p=pyd[0], h=h, m=m: e.matmul(p[:], lhsT=wod[:, h, m * 128:(m + 1) * 128], rhs=odt[:, h, :], start=(h == 0), stop=(h == 3)),
                                     reads=[twod, todt], writes=[pyd[1]])
                            for i, (p, tp_) in enumerate((pya, pyb, pyc, pyd)):
                                g.op("dve", lambda e, p=p, i=i, tp=tp, gt=gt: e.tensor_tensor(out=tp[:, i, :], in0=p[:], in1=gt[:, i, :], op=ALU.mult), reads=[tp_, tgt], writes=[ttp])
                            g.op("pool", lambda e, tp=tp: e.tensor_tensor(out=tp[:, 0:2, :], in0=tp[:, 0:2, :], in1=tp[:, 2:4, :], op=ALU.add), reads=[ttp], writes=[ttp])
                            g.op("pool", lambda e, tp=tp, mg=mg, m=m: e.tensor_tensor(out=mg[:, m, :], in0=tp[:, 0, :], in1=tp[:, 1, :], op=ALU.add), reads=[ttp], writes=[tmg])
                        if ti + 1 < len(tiles):
                            nxt2 = front2(tiles[ti + 1])
                        edge_needed = joined and ((s == 0 and t0 == seqs[0] - 512) or (s == 1 and t0 == 0))
                        if edge_needed:
                            ed, ted, eds = edr.next()
                            ecol = 511 if s == 0 else 0
                        for m2 in range(8):
                            xo, txo, xos = xor2.next()
                            g.op("sp", lambda e, s_, xo=xo, n0=n0, m2=m2: e.dma_start(out=xo[:], in_=xsrc_v[:, m2, n0:n0 + 512]).then_inc(s_, 16), writes=[txo], dma=xos)
                            po, tpo = prot.next()
                            for k in range(8):
                                g.op("pe", lambda e, po=po, k=k, m2=m2, mg=mg: e.matmul(po[:], lhsT=wout[:, k, m2 * 128:(m2 + 1) * 128], rhs=mg[:, k, :], start=(k == 0), stop=(k == 7)),
                                     reads=[twout, tmg], writes=[tpo])
                            g.op("dve", lambda e, po=po, xo=xo: e.tensor_tensor(out=xo[:], in0=po[:], in1=xo[:], op=ALU.add), reads=[tpo, txo], writes=[txo])
                            g.op("sp", lambda e, s_, xo=xo, pp0=pp0, m2=m2: e.dma_start(out=xa.rearrange("(k p) n -> p k n", p=128)[:, m2, pp0:pp0 + 512], in_=xo[:]).then_inc(s_, 16),
                                 reads=[txo], dma="st_" + xos)
                            if edge_needed:
                                g.op("pool", lambda e, xo=xo, ed=ed, m2=m2, ecol=ecol: e.tensor_scalar(out=ed[:, m2, :], in0=xo[:, ecol:ecol + 1], scalar1=jft[:, 0:1], scalar2=0.0, op0=ALU.mult, op1=ALU.add),
                                     reads=[txo, tjft], writes=[ted])
                        if edge_needed:
                            dcol = (poff[1] + PAD - 1) if s == 0 else (poff[0] + PAD + seqs[0])
                            g.op("sp", lambda e, s_, ed=ed, dcol=dcol: e.dma_start(out=xa.rearrange("(k p) n -> p k n", p=128)[:, :, dcol:dcol + 1], in_=ed[:]).then_inc(s_, 16),
                                 reads=[ted], dma="st_" + eds)
                    phase_end()

            if "p3" in phases:
                with ExitStack() as es:
                    sb = lambda n, s, d: es.enter_context(nc.sbuf_tensor(un(n), s, d))
                    Wup = sb("Wup", [128, 8, 5632], BF16); tWup = T()
                    Wdn = sb("Wdn", [128, 22, 1024], BF16); tWdn = T()
                    gv = sb("gv", [128, 8], F32); tgv = T()
                    cw = sb("cw", [128, 44, 4], F32); tcw = T()
                    load_vec(gv[:, 0:8], tgv, W["ffn_norm"][l].rearrange("(k p) -> p k", p=128), "v0")
                    for k in range(3):
                        load_vec(cw[:, :, k:k + 1], tcw, W["ffn_conv_w"][l][k].rearrange("(c p o) -> p c o", p=128, o=1), "v%d" % (k + 1))
                    load_vec(cw[:, :, 3:4], tcw, W["ffn_conv_b"][l].rearrange("(c p o) -> p c o", p=128, o=1), "v0")
                    xr = Ring(nc, es, "xw", [128, 8, 512], F32, 1)
                    sqr = Ring(nc, es, "sq", [128, 8, 512], BF16, 1)
                    hr = Ring(nc, es, "hT", [128, 8, 512], BF16, 1)
                    rsr = Ring(nc, es, "rs", [128, 512], F32, 2)
                    uar = Ring(nc, es, "ua", [128, 512], F32, 2)
                    ubr = Ring(nc, es, "ubf", [128, 512], F32, 2)
                    gTr = Ring(nc, es, "gT", [128, 22, 512], BF16, 1)
                    xor_ = Ring(nc, es, "xo", [128, 512], F32, 2)
                    prot = Rot(BK[0:8])
                    xav = xa.rearrange("(k p) n -> p k n", p=128)
                    xdv = xdst3.rearrange("(k p) n -> p k n", p=128)
                    gtv = gTr.tiles[0][:].rearrange("p a b -> p (a b)").bitcast(F32)
                    stage = BigStage([gtv[:, 0:2816], gtv[:, 2816:5632], xr.tiles[0][:].rearrange("p a b -> p (a b)")[:, 0:2816]], 2816, [gTr.ts[0], xr.ts[0]])
                    load_w(stage, Wup, tWup, W["ffn_w_up"][l], gv[:, 0:8], tgv)
                    load_w(stage, Wdn, tWdn, W["ffn_w_down"][l], None, None)
                    stage.release()
                    wins = []
                    for s, S in enumerate(seqs):
                        for i in range(S // 512):
                            wins.append((s, 512 * i, 510))

                    def front3(w):
                        s, ta, NO = w
                        NW = NO + 2
                        col0 = poff[s] + PAD + ta - 1
                        xw, txw, xs = xr.next()
                        g.op("sp", lambda e, s_, xw=xw, col0=col0, NW=NW: e.dma_start(out=xw[:, :, 0:NW], in_=xav[:, :, col0:col0 + NW]).then_inc(s_, 16), writes=[txw], dma=xs)
                        return rms_x(es, xw, txw, NW, sqr, hr, rsr, prot.next())

                    nxt3 = front3(wins[0])
                    for wi, (s, ta, NO) in enumerate(wins):
                            NW = NO + 2
                            col0 = poff[s] + PAD + ta - 1
                            hT, thT = nxt3
                            gT, tgT, _ = gTr.next()
                            for ca in range(22):
                                res = []
                                for half, cc in enumerate((ca, 22 + ca)):
                                    pu, tpu = prot.next()
                                    for k in range(8):
                                        g.op("pe", lambda e, pu=pu, k=k, cc=cc, NW=NW, hT=hT: e.matmul(pu[:, 0:NW], lhsT=Wup[:, k, cc * 128:(cc + 1) * 128], rhs=hT[:, k, 0:NW], start=(k == 0), stop=(k == 7)),
                                             reads=[tWup, thT], writes=[tpu])
                                    u, tu, _ = (uar if half == 0 else ubr).next()
                                    g.op("act", lambda e, u=u, pu=pu, cc=cc, NO=NO: e.activation(out=u[:, 0:NO], in_=pu[:, 1:NO + 1], func=AF.Identity, bias=cw[:, cc, 3:4], scale=cw[:, cc, 1:2]),
                                         reads=[tpu, tcw], writes=[tu])
                                    g.op("dve", lambda e, u=u, pu=pu, cc=cc, NO=NO: e.scalar_tensor_tensor(out=u[:, 0:NO], in0=pu[:, 0:NO], scalar=cw[:, cc, 0:1], in1=u[:, 0:NO], op0=ALU.mult, op1=ALU.add),
                                         reads=[tpu, tcw, tu], writes=[tu])
                                    g.op("dve", lambda e, u=u, pu=pu, cc=cc, NO=NO: e.scalar_tensor_tensor(out=u[:, 0:NO], in0=pu[:, 2:NO + 2], scalar=cw[:, cc, 2:3], in1=u[:, 0:NO], op0=ALU.mult, op1=ALU.add),
                                         reads=[tpu, tcw, tu], writes=[tu])
                                    res.append((u, tu))
                                (ua, tua), (ub, tub) = res
                                g.op("act", lambda e, ua=ua, NO=NO: e.activation(out=ua[:, 0:NO], in_=ua[:, 0:NO], func=AF.Silu), reads=[tua], writes=[tua])
                                g.op("pool", lambda e, ua=ua, ub=ub, gT=gT, ca=ca, NO=NO: e.tensor_tensor(out=gT[:, ca, 0:NO], in0=ua[:, 0:NO], in1=ub[:, 0:NO], op=ALU.mult),
                                     reads=[tua, tub], writes=[tgT])
                            if wi + 1 < len(wins):
                                nxt3 = front3(wins[wi + 1])
                            n0 = soff[s] + ta
                            for m in range(8):
                                xo, txo, xos = xor_.next()
                                g.op("sp", lambda e, s_, xo=xo, col0=col0, NO=NO, m=m: e.dma_start(out=xo[:, 0:NO], in_=xav[:, m, col0 + 1:col0 + 1 + NO]).then_inc(s_, 16), writes=[txo], dma=xos)
                                pd, tpd = prot.next()
                                for c in range(22):
                                    g.op("pe", lambda e, pd=pd, c=c, m=m, NO=NO, gT=gT: e.matmul(pd[:, 0:NO], lhsT=Wdn[:, c, m * 128:(m + 1) * 128], rhs=gT[:, c, 0:NO], start=(c == 0), stop=(c == 21)),
                                         reads=[tWdn, tgT], writes=[tpd])
                                g.op("dve", lambda e, pd=pd, xo=xo, NO=NO: e.tensor_tensor(out=xo[:, 0:NO], in0=pd[:, 0:NO], in1=xo[:, 0:NO], op=ALU.add),
                                     reads=[tpd, txo], writes=[txo])
                                g.op("sp", lambda e, s_, xo=xo, n0=n0, NO=NO, m=m: e.dma_start(out=xdv[:, m, n0:n0 + NO], in_=xo[:, 0:NO]).then_inc(s_, 16), reads=[txo], dma="st_" + xos)
                    tgroups = [(s, i) for s in range(len(seqs)) for i in range(seqs[s] // 512)]
                    NG = len(tgroups)
                    NWt, NOt = 4 * NG, 2 * NG
                    xw, txw, xs = xr.next()

                    def ldt(e, s_, xw=xw):
                        for gi_, (s, i) in enumerate(tgroups):
                            c0 = poff[s] + PAD + 512 * i + 509
                            e.dma_start(out=xw[:, :, 4 * gi_:4 * gi_ + 4], in_=xav[:, :, c0:c0 + 4]).then_inc(s_, 16)
                    g.op("sp", ldt, writes=[txw], dma=xs, ndma=NG)
                    hT, thT = rms_x(es, xw, txw, NWt, sqr, hr, rsr, prot.next())
                    gT, tgT, _ = gTr.next()
                    for ca in range(22):
                        res = []
                        for half, cc in enumerate((ca, 22 + ca)):
                            pu, tpu = prot.next()
                            for k in range(8):
                                g.op("pe", lambda e, pu=pu, k=k, cc=cc, hT=hT: e.matmul(pu[:, 0:NWt], lhsT=Wup[:, k, cc * 128:(cc + 1) * 128], rhs=hT[:, k, 0:NWt], start=(k == 0), stop=(k == 7)),
                                     reads=[tWup, thT], writes=[tpu])
                            u, tu, _ = (uar if half == 0 else ubr).next()
                            puv = pu[:, 0:NWt].rearrange("p (g c) -> p g c", c=4)
                            uv = u[:, 0:NOt].rearrange("p (g c) -> p g c", c=2)
                            g.op("act", lambda e, uv=uv, puv=puv, cc=cc: e.activation(out=uv, in_=puv[:, :, 1:3], func=AF.Identity, bias=cw[:, cc, 3:4], scale=cw[:, cc, 1:2]),
                                 reads=[tpu, tcw], writes=[tu])
                            g.op("dve", lambda e, uv=uv, puv=puv, cc=cc: e.scalar_tensor_tensor(out=uv, in0=puv[:, :, 0:2], scalar=cw[:, cc, 0:1], in1=uv, op0=ALU.mult, op1=ALU.add),
                                 reads=[tpu, tcw, tu], writes=[tu])
                            g.op("dve", lambda e, uv=uv, puv=puv, cc=cc: e.scalar_tensor_tensor(out=uv, in0=puv[:, :, 2:4], scalar=cw[:, cc, 2:3], in1=uv, op0=ALU.mult, op1=ALU.add),
                                 reads=[tpu, tcw, tu], writes=[tu])
                            res.append((u, tu))
                        (ua, tua), (ub, tub) = res
                        g.op("act", lambda e, ua=ua: e.activation(out=ua[:, 0:NOt], in_=ua[:, 0:NOt], func=AF.Silu), reads=[tua], writes=[tua])
                        g.op("pool", lambda e, ua=ua, ub=ub, gT=gT, ca=ca: e.tensor_tensor(out=gT[:, ca, 0:NOt], in0=ua[:, 0:NOt], in1=ub[:, 0:NOt], op=ALU.mult),
                             reads=[tua, tub], writes=[tgT])
                    for m in range(8):
                        xo, txo, xos = xor_.next()
                        pd, tpd = prot.next()
                        for c in range(22):
                            g.op("pe", lambda e, pd=pd, c=c, m=m, gT=gT: e.matmul(pd[:, 0:NOt], lhsT=Wdn[:, c, m * 128:(m + 1) * 128], rhs=gT[:, c, 0:NOt], start=(c == 0), stop=(c == 21)),
                                 reads=[tWdn, tgT], writes=[tpd])
                        g.op("dve", lambda e, pd=pd, xo=xo, xw=xw, m=m: e.tensor_tensor(out=xo[:, 0:NOt].rearrange("p (g c) -> p g c", c=2), in0=pd[:, 0:NOt].rearrange("p (g c) -> p g c", c=2),
                                                                                      in1=xw[:, m, 0:NWt].rearrange("p (g c) -> p g c", c=4)[:, :, 1:3], op=ALU.add),
                             reads=[tpd, txw], writes=[txo])
                        g.op("sp", lambda e, s_, xo=xo, m=m: e.dma_start(out=xdv[:, m, 0:NTOK].rearrange("p (q c) -> p q c", c=512)[:, :, 510:512],
                                                                        in_=xo[:, 0:NOt].rearrange("p (g c) -> p g c", c=2)).then_inc(s_, 16), reads=[txo], dma="st_" + xos)
                    phase_end()

        if debug:
            with ExitStack() as es:
                cp = es.enter_context(nc.sbuf_tensor(un("cpb"), [128, 2048], F32))
                tcp = T()
                def dcopy(dst, src):
                    g.op("sp", lambda e, s_: e.dma_start(out=dst, in_=src).then_inc(s_, 16), dma="dbgc")
                for nm, src in (("qc", qc), ("kc", kc), ("vc", vc), ("qd", qd), ("kd", kd), ("vd", vd), ("oc", oc), ("od", od), ("hA", hA), ("yB", yB), ("xa", xa)):
                    dcopy(dbg[nm], src)
                phase_end()
        print("total ops recorded:", g.nops)
    return nc


def gelu_to(g, src, tsrc, dst, tdst, tmr, nw=512):
    P = src.shape[0]
    t1, tt1, _ = tmr.next()
    g.op("act", lambda e: e.activation(out=t1[0:P, 0:nw], in_=src, func=AF.Square), reads=[tsrc], writes=[tt1])
    g.op("dve", lambda e: e.tensor_scalar(out=t1[0:P, 0:nw], in0=t1[0:P, 0:nw], scalar1=0.044715, scalar2=1.0, op0=ALU.mult, op1=ALU.add), reads=[tt1], writes=[tt1])
    g.op("dve", lambda e: e.tensor_tensor(out=t1[0:P, 0:nw], in0=t1[0:P, 0:nw], in1=src, op=ALU.mult), reads=[tt1, tsrc], writes=[tt1])
    g.op("act", lambda e: e.activation(out=t1[0:P, 0:nw], in_=t1[0:P, 0:nw], func=AF.Sigmoid, scale=1.5957691216057308), reads=[tt1], writes=[tt1])
    g.op("dve", lambda e: e.tensor_tensor(out=dst, in0=t1[0:P, 0:nw], in1=src, op=ALU.mult), reads=[tt1, tsrc], writes=[tdst])


SEQS = [2048, 2048, 2048, 2048, 2048]
_CACHE = {}


def _core_parts(c):
    if c < 4:
        return [("s", c, 0), ("s", c, 1)] + [("p", 3 * c + i, 0) for i in range(3)]
    return [("p", 12 + 5 * (c - 4) + i, 0) for i in range(5)]


def kernel(**inputs):
    x_prompt = np.asarray(inputs["x_prompt"], np.float32)
    x_sample = np.asarray(inputs["x_sample"], np.float32)
    n = 8
    consts = host_consts()
    wsT = np.ascontiguousarray(np.transpose(np.asarray(inputs["sgu_w_s"], np.float32), (0, 1, 3, 2)))
    if "nc" not in _CACHE:
        _CACHE["nc"] = build(SEQS, joined=True)
    nc = _CACHE["nc"]
    in_maps = []
    for c in range(n):
        parts = []
        for kind, idx, hf in _core_parts(c):
            parts.append(x_sample[idx, hf * 2048:(hf + 1) * 2048] if kind == "s" else x_prompt[idx])
        xT = np.ascontiguousarray(np.concatenate(parts, 0).T)
        jf = np.zeros((128, 2), np.float32)
        if c < 4:
            jf[:, 0] = 1.0
        else:
            jf[:, 1] = -30000.0
        m = {"xT": xT, "wsT": wsT, "jf": jf}
        for k in WNAMES:
            m[k] = np.ascontiguousarray(np.asarray(inputs[k], np.float32))
        m.update(consts)
        in_maps.append(m)
    res = run_bass_kernel_spmd(nc, in_maps, core_ids=list(range(n)))
    y_prompt = np.empty_like(x_prompt)
    y_sample = np.empty_like(x_sample)
    for c in range(n):
        y = np.asarray(res.results[c]["yT"]).T
        for i, (kind, idx, hf) in enumerate(_core_parts(c)):
            blk = y[2048 * i:2048 * (i + 1)]
            if kind == "s":
                y_sample[idx, hf * 2048:(hf + 1) * 2048] = blk
            else:
                y_prompt[idx] = blk
    return (y_prompt, y_sample)
```

```python
import math
from contextlib import ExitStack
import numpy as np
import ml_dtypes
import concourse.bass as bass
import concourse.mybir as mybir
from concourse.bass_utils import run_bass_kernel_spmd

F32 = mybir.dt.float32
BF16 = mybir.dt.bfloat16
AF = mybir.ActivationFunctionType
ALU = mybir.AluOpType

D = 1024
DEPTH = 2
EPS = 1e-6
NIN = 7840
DFF = 2816
PAD = 1024
TW = 512


class T:
    __slots__ = ("name", "w", "r")

    def __init__(self, name=""):
        self.name = name
        self.w = None
        self.r = []


class Op:
    __slots__ = ("eng", "fn", "deps", "inc", "sig_sem", "sig_cnt", "ndma", "waits", "done")

    def __init__(self, eng, fn):
        self.eng = eng
        self.fn = fn
        self.deps = []
        self.inc = False
        self.sig_sem = None
        self.sig_cnt = None
        self.ndma = 0
        self.waits = []
        self.done = False


class G:
    ENGS = ("pe", "act", "dve", "pool", "sp")

    def __init__(self, nc, es):
        self.nc = nc
        self.es = es
        self.sems = {e: es.enter_context(nc.semaphore("s_" + e)) for e in ("pe", "act", "dve", "pool")}
        self.cnt = {e: 0 for e in ("pe", "act", "dve", "pool")}
        self.dsems = {}
        self.dcnt = {}
        self.dlast = {}
        self.waited = {e: {} for e in self.ENGS}
        self.ops = {e: [] for e in self.ENGS}
        self.order = []
        self.nops = 0

    def dsem(self, name):
        if name not in self.dsems:
            self.dsems[name] = self.es.enter_context(self.nc.semaphore("d_" + name))
            self.dcnt[name] = 0
            self.dlast[name] = None
        return name

    def _dep(self, op, p):
        if p is None or p is op or p.done:
            return
        if p.eng == "pe" and op.eng == "pe" and p.ndma == 0 and op.ndma == 0:
            return
        if p not in op.deps:
            op.deps.append(p)
            p.inc = True

    def op(self, eng, fn, reads=(), writes=(), dma=None, ndma=1):
        o = Op(eng, fn)
        for t in reads:
            self._dep(o, t.w)
        for t in writes:
            self._dep(o, t.w)
            for r in t.r:
                self._dep(o, r)
        if dma is not None:
            self.dsem(dma)
            o.ndma = ndma
            o.sig_sem = dma
            self._dep(o, self.dlast[dma])
            self.dlast[dma] = o
        for t in reads:
            t.r.append(o)
        for t in writes:
            t.w = o
            t.r = []
        self.ops[eng].append(o)
        self.order.append(o)
        return o

    def fence(self):
        o = Op("sp", lambda e: e.nop())
        for name, p in self.dlast.items():
            self._dep(o, p)
        self.ops["sp"].append(o)
        self.order.append(o)

    def emit(self):
        nc = self.nc
        for o in self.order:
            if o.ndma:
                self.dcnt[o.sig_sem] += 16 * o.ndma
                o.sig_cnt = self.dcnt[o.sig_sem]
            elif o.inc:
                self.cnt[o.eng] += 1
                o.sig_sem = o.eng
                o.sig_cnt = self.cnt[o.eng]
        for o in self.order:
            wd = self.waited[o.eng]
            for p in o.deps:
                key = (p.sig_sem, p.ndma > 0)
                if wd.get(key, 0) < p.sig_cnt:
                    wd[key] = p.sig_cnt
                    o.waits.append((p.ndma > 0, p.sig_sem, p.sig_cnt))
        ops, sems, dsems = self.ops, self.sems, self.dsems

        def run(eng_obj, lst):
            for o in lst:
                for isd, key, c in o.waits:
                    eng_obj.wait_ge(dsems[key] if isd else sems[key], c)
                if o.ndma:
                    o.fn(eng_obj, dsems[o.sig_sem])
                else:
                    ins = o.fn(eng_obj)
                    if o.inc:
                        ins.then_inc(sems[o.eng], 1)

        with nc.Block() as block:
            @block.tensor
            def _(e):
                run(e, ops["pe"])

            @block.scalar
            def _(e):
                run(e, ops["act"])

            @block.vector
            def _(e):
                run(e, ops["dve"])

            @block.gpsimd
            def _(e):
                run(e, ops["pool"])

            @block.sync
            def _(e):
                run(e, ops["sp"])
        for o in self.order:
            o.done = True
        self.nops += len(self.order)
        self.ops = {e: [] for e in self.ENGS}
        self.order = []


_UID = [0]


def un(name):
    _UID[0] += 1
    return "%s_u%d" % (name, _UID[0])


class Ring:
    def __init__(self, nc, es, name, shape, dtype, n):
        self.tiles = [es.enter_context(nc.sbuf_tensor(un("%s_%d" % (name, i)), shape, dtype)) for i in range(n)]
        self.ts = [T("%s_%d" % (name, i)) for i in range(n)]
        self.name = name
        self.i = -1
        self.n = n

    def next(self):
        self.i = (self.i + 1) % self.n
        return self.tiles[self.i], self.ts[self.i], "%s%d" % (self.name, self.i)


class Rot:
    def __init__(self, items):
        self.items = items
        self.i = -1

    def next(self):
        self.i = (self.i + 1) % len(self.items)
        return self.items[self.i]


def host_consts():
    c = {}
    def tabs(dims, theta):
        inv = np.exp(-math.log(theta) * np.arange(0, dims, 2, dtype=np.float32) / dims).astype(np.float32)
        ang = np.arange(4096, dtype=np.float32)[:, None] * inv[None, :]
        return np.cos(ang).astype(np.float32).T, np.sin(ang).astype(np.float32).T
    cc, sc = tabs(32, 10000.0)
    C = np.ones((96, 4096), np.float32)
    S = np.zeros((96, 4096), np.float32)
    C[64:80] = cc; C[80:96] = cc; S[64:80] = sc; S[80:96] = sc
    c["ropec"] = np.stack([C, S], 0)
    cd, sd = tabs(16, 500000.0)
    C = np.ones((128, 4096), np.float32)
    S = np.zeros((128, 4096), np.float32)
    for b in (0, 64):
        C[b:b + 8] = cd; C[b + 8:b + 16] = cd; S[b:b + 8] = sd; S[b + 8:b + 16] = sd
    c["roped"] = np.stack([C, S], 0)
    mats = np.zeros((5, 128, 128), np.float32)
    mats[0] = 1.0
    mats[1, 0:64, 0:64] = 1.0; mats[1, 64:128, 64:128] = 1.0
    for m in range(64, 80):
        mats[2, m + 16, m] = -1.0
    for m in range(80, 96):
        mats[2, m - 16, m] = 1.0
    for b in (0, 64):
        for m in range(b, b + 8):
            mats[3, m + 8, m] = -1.0
        for m in range(b + 8, b + 16):
            mats[3, m - 8, m] = 1.0
    mats[4] = np.eye(128)
    c["mats"] = mats.astype(ml_dtypes.bfloat16)
    k = np.arange(128)[:, None]
    q = np.arange(128)[None, :]
    L = (k >= q).astype(np.float32)
    U = (k <= q).astype(np.float32)
    masks = np.zeros((4, 128, 512), np.float32)
    masks[0] = np.tile(L, (1, 4))
    masks[1] = np.tile(U, (1, 4))
    masks[2] = np.tile(L[:, :32], (1, 16))
    masks[3] = np.tile(U[:, :32], (1, 16))
    c["masks"] = masks.astype(ml_dtypes.bfloat16)
    return c


WNAMES = ["attn_norm", "w_in", "conv_w", "conv_b", "conv_ln_g", "conv_ln_b", "conv_w_o",
          "sgu_ln_g", "sgu_ln_b", "sgu_w_s", "sgu_b_s", "sgu_w_o",
          "mla_g_cq", "mla_g_ckv", "mla_w_uq", "mla_w_ukv", "mla_g_qn", "mla_g_kn", "mla_w_o",
          "dil_g_qn", "dil_g_kn", "dil_w_o", "w_out",
          "ffn_norm", "ffn_w_up", "ffn_conv_w", "ffn_conv_b", "ffn_w_down"]
WSHAPES = {
    "attn_norm": [2, 1024], "w_in": [2, 1024, 7840], "conv_w": [2, 31, 256], "conv_b": [2, 256],
    "conv_ln_g": [2, 256], "conv_ln_b": [2, 256], "conv_w_o": [2, 256, 1024], "sgu_ln_g": [2, 256],
    "sgu_ln_b": [2, 256], "sgu_w_s": [2, 4, 128, 128], "sgu_b_s": [2, 4, 128], "sgu_w_o": [2, 256, 1024],
    "mla_g_cq": [2, 256], "mla_g_ckv": [2, 128], "mla_w_uq": [2, 256, 384], "mla_w_ukv": [2, 128, 512],
    "mla_g_qn": [2, 96], "mla_g_kn": [2, 96], "mla_w_o": [2, 256, 1024], "dil_g_qn": [2, 64],
    "dil_g_kn": [2, 64], "dil_w_o": [2, 256, 1024], "w_out": [2, 1024, 1024], "ffn_norm": [2, 1024],
    "ffn_w_up": [2, 1024, 5632], "ffn_conv_w": [2, 3, 5632], "ffn_conv_b": [2, 5632], "ffn_w_down": [2, 2816, 1024],
}


def build(seqs, phases=("p1", "p2a", "p2b", "p2c", "p3"), depth=DEPTH, debug=False, joined=False):
    nc = bass.Bass("TRN2", target_bir_lowering=False)
    NTOK = sum(seqs)
    soff = [sum(seqs[:i]) for i in range(len(seqs))]
    poff = [soff[i] + 2 * PAD * i for i in range(len(seqs))]
    NP = NTOK + 2 * PAD * len(seqs)
    groups = ([[0, 1]] + [[i] for i in range(2, len(seqs))]) if joined else [[i] for i in range(len(seqs))]
    gof = {}
    for gi_, grp in enumerate(groups):
        for s_ in grp:
            gof[s_] = gi_
    gbase = {s_: soff[groups[gof[s_]][0]] for s_ in range(len(seqs))}
    gS = {s_: sum(seqs[x] for x in groups[gof[s_]]) for s_ in range(len(seqs))}
    rpos = {s_: soff[s_] - gbase[s_] for s_ in range(len(seqs))}
    tiles = []
    for s, S in enumerate(seqs):
        for t0 in range(0, S, TW):
            tiles.append((s, t0, soff[s] + t0, poff[s] + PAD + t0))

    def din(name, shape, dt=F32):
        return nc.dram_tensor(name, shape, dt, kind="ExternalInput").ap()

    def dscr(name, shape, dt):
        return nc.dram_tensor(name, shape, dt, kind="Internal").ap()

    xT = din("xT", [D, NTOK])
    W = {n: din(n, WSHAPES[n]) for n in WNAMES}
    wsT = din("wsT", [2, 4, 128, 128])
    ropec = din("ropec", [2, 96, 4096])
    roped = din("roped", [2, 128, 4096])
    matsd = din("mats", [5, 128, 128], BF16)
    masksd = din("masks", [4, 128, 512], BF16)
    jfd = din("jf", [128, 2])
    yT = nc.dram_tensor("yT", [D, NTOK], F32, kind="ExternalOutput").ap()

    xa = dscr("xa", [D, NP], F32)
    xb = dscr("xb", [D, NTOK], F32)
    hA = dscr("hA", [256, NP], BF16)
    yB = dscr("yB", [256, NTOK], BF16)
    qc = dscr("qc", [96, 4, NTOK], BF16)
    kc = dscr("kc", [96, 4, NTOK], BF16)
    vc = dscr("vc", [NTOK, 256], BF16)
    qd = dscr("qd", [128, 6, NTOK], BF16)
    kd = dscr("kd", [128, 6, NP], BF16)
    vd = dscr("vd", [NP, 1536], BF16)
    oc = dscr("oc", [64, 4, NTOK], BF16)
    od = dscr("od", [64, 4, NTOK], BF16)
    dbg = {}
    if debug:
        for nm, ap in (("qc", qc), ("kc", kc), ("vc", vc), ("qd", qd), ("kd", kd), ("vd", vd), ("oc", oc),
                       ("od", od), ("hA", hA), ("yB", yB)):
            dbg[nm] = nc.dram_tensor("dbg_" + nm, list(ap.shape), BF16, kind="ExternalOutput").ap()
        dbg["xa"] = nc.dram_tensor("dbg_xa", [D, NP], F32, kind="ExternalOutput").ap()

    es_top = ExitStack()
    with es_top:
        es_top.enter_context(nc.allow_non_contiguous_dma(reason="small strided parameter loads"))
        es_top.enter_context(nc.allow_low_precision(reason="bf16 matmul operands, fp32 accumulation"))
        g = G(nc, es_top)
        PSA = es_top.enter_context(nc.psum_tensor("psall", [128, 4096], F32))
        banks = [PSA[:, i * 512:(i + 1) * 512] for i in range(8)]
        bts = [T("bank%d" % i) for i in range(8)]
        BK = [(banks[i], bts[i]) for i in range(8)]
        mats = es_top.enter_context(nc.sbuf_tensor(un("mats"), [128, 5, 128], BF16))
        tmats = T("mats")
        onesb = es_top.enter_context(nc.sbuf_tensor(un("onesb"), [128, 64], BF16))
        tonesb = T("onesb")
        epsb = es_top.enter_context(nc.sbuf_tensor(un("epsb"), [128, 1], F32))
        tepsb = T("epsb")
        mhalf = es_top.enter_context(nc.sbuf_tensor(un("mhalf"), [128, 1], F32))
        tmhalf = T("mhalf")
        jft = es_top.enter_context(nc.sbuf_tensor(un("jft"), [128, 2], F32))
        tjft = T("jft")
        g.op("sp", lambda e, s: e.dma_start(out=jft[:], in_=jfd).then_inc(s, 16), writes=[tjft], dma="c1")
        es_init = ExitStack()
        zt = es_init.enter_context(nc.sbuf_tensor(un("zt"), [128, 2048], BF16))
        tzt = T("zt")
        g.op("sp", lambda e, s: e.dma_start(out=mats[:], in_=matsd.rearrange("m p n -> p m n")).then_inc(s, 16),
             writes=[tmats], dma="c0")
        g.op("pool", lambda e: e.memset(zt[:], 0.0), writes=[tzt])
        g.op("pool", lambda e: e.memset(onesb[:], 1.0), writes=[tonesb])
        ONES = mats[:, 0, :]
        BONES = mats[:, 1, :]
        RC = mats[:, 2, :]
        RD = mats[:, 3, :]
        zi = [0]

        def zero_dram(ap3):
            sem = "z%d" % (zi[0] % 4)
            zi[0] += 1
            a, n = ap3.shape[1], ap3.shape[2]
            g.op("sp", lambda e, s: e.dma_start(out=ap3, in_=zt[:, 0:a * n].rearrange("p (a n) -> p a n", a=a)).then_inc(s, 16),
                 reads=[tzt], dma=sem)

        for s, S in enumerate(seqs):
            for lo in (poff[s], poff[s] + PAD + S):
                zero_dram(hA.rearrange("(a p) n -> p a n", p=128)[:, :, lo:lo + PAD])
                for c6 in range(0, 6, 2):
                    zero_dram(kd[:, c6:c6 + 2, lo:lo + PAD])
                for r0 in range(0, PAD, 128):
                    zero_dram(vd[lo + r0:lo + r0 + 128, :].rearrange("(a p) n -> p a n", p=128))
            for col in (poff[s] + PAD - 1, poff[s] + PAD + S):
                pass
        xa_bf = None
        zf = es_init.enter_context(nc.sbuf_tensor(un("zf"), [128, 8, 1], F32))
        tzf = T("zf")
        g.op("pool", lambda e: e.memset(zf[:], 0.0), writes=[tzf])
        for s, S in enumerate(seqs):
            for col in (poff[s] + PAD - 1, poff[s] + PAD + S):
                g.op("sp", lambda e, s_, col=col: e.dma_start(out=xa.rearrange("(k p) n -> p k n", p=128)[:, :, col:col + 1],
                                                            in_=zf[:]).then_inc(s_, 16), reads=[tzf], dma="zx")
        g.fence()
        g.emit()
        es_init.close()

        cast_rr = Rot(["act", "dve", "pool"])

        def cast_op(eng, dst, src, scale_ap, reads, writes):
            if eng == "act":
                if scale_ap is None:
                    g.op("act", lambda e: e.activation(out=dst, in_=src, func=AF.Copy), reads=reads, writes=writes)
                else:
                    g.op("act", lambda e: e.activation(out=dst, in_=src, func=AF.Identity, scale=scale_ap), reads=reads, writes=writes)
            else:
                if scale_ap is None:
                    g.op(eng, lambda e: e.tensor_copy(out=dst, in_=src), reads=reads, writes=writes)
                else:
                    g.op(eng, lambda e: e.tensor_scalar(out=dst, in0=src, scalar1=scale_ap, scalar2=0.0, op0=ALU.mult, op1=ALU.add),
                         reads=reads, writes=writes)

        def load_w(stage, dst, tdst, src, gain, tgain, np_=128):
            K, N = dst.shape[1], dst.shape[2]
            CH = stage.CH
            srcv = src.rearrange("(k p) n -> p k n", p=np_)
            if gain is None and N * 2 <= CH:
                kk = CH // N
                for k0 in range(0, K, kk):
                    k1 = min(K, k0 + kk)
                    st, tst, sname = stage.next()
                    stv = st[0:np_, 0:(k1 - k0) * N].rearrange("p (a n) -> p a n", a=k1 - k0)
                    g.op("sp", lambda e, s, stv=stv, k0=k0, k1=k1: e.dma_start(out=stv, in_=srcv[:, k0:k1, :]).then_inc(s, 16), writes=[tst], dma=sname)
                    cast_op(cast_rr.next(), dst[:, k0:k1, :], stv, None, [tst], [tdst])
                return
            for k in range(K):
                for c0 in range(0, N, CH):
                    c1 = min(N, c0 + CH)
                    st, tst, sname = stage.next()
                    g.op("sp", lambda e, s, st=st, k=k, c0=c0, c1=c1: e.dma_start(out=st[0:np_, 0:c1 - c0], in_=srcv[:, k, c0:c1]).then_inc(s, 16),
                         writes=[tst], dma=sname)
                    cast_op(cast_rr.next(), dst[:, k, c0:c1], st[0:np_, 0:c1 - c0],
                            None if gain is None else gain[:, k:k + 1], [tst] + ([tgain] if gain is not None else []), [tdst])

        class BigStage:
            def __init__(self, views, CH, real_ts):
                self.items = [(v, T("stg"), "bstg%d" % i) for i, v in enumerate(views)]
                self.CH = CH
                self.i = -1
                self.real_ts = real_ts

            def next(self):
                self.i = (self.i + 1) % len(self.items)
                return self.items[self.i]

            def release(self):
                g.op("pool", lambda e: e.nop(), writes=[it[1] for it in self.items] + self.real_ts)

        def load_vec(dst, tdst, src_ap, sem):
            g.op("sp", lambda e, s: e.dma_start(out=dst, in_=src_ap).then_inc(s, 16), writes=[tdst], dma=sem)

        def rstd_from(ps_ap, tps, scale, out_ap, tout):
            g.op("act", lambda e: e.activation(out=out_ap, in_=ps_ap, func=AF.Ln, bias=epsb[0:ps_ap.shape[0], :], scale=scale),
                 reads=[tps, tepsb], writes=[tout])
            g.op("act", lambda e: e.activation(out=out_ap, in_=out_ap, func=AF.Exp, scale=-0.5), reads=[tout], writes=[tout])

        g.op("pool", lambda e: e.memset(epsb[:], EPS), writes=[tepsb])
        g.op("pool", lambda e: e.memset(mhalf[:], -0.5), writes=[tmhalf])

        def rms_x(es, xt, txt, nw, sqr, hr, rsr, bank):
            sq, tsq, _ = sqr.next()
            g.op("act", lambda e: e.activation(out=sq[:, :, 0:nw], in_=xt[:, :, 0:nw], func=AF.Square), reads=[txt], writes=[tsq])
            pb, tpb = bank
            for k in range(8):
                g.op("pe", lambda e, k=k: e.matmul(pb[:, 0:nw], lhsT=ONES, rhs=sq[:, k, 0:nw], start=(k == 0), stop=(k == 7)),
                     reads=[tsq, tmats], writes=[tpb])
            rs, trs, _ = rsr.next()
            rstd_from(pb[:, 0:nw], tpb, 1.0 / D, rs[:, 0:nw], trs)
            hT, thT, _ = hr.next()
            g.op("dve", lambda e: e.tensor_tensor(out=hT[:, :, 0:nw], in0=xt[:, :, 0:nw],
                                                  in1=rs[:, 0:nw].unsqueeze(1).broadcast_to([128, 8, nw]), op=ALU.mult),
                 reads=[txt, trs], writes=[thT])
            return hT, thT

        def phase_end():
            g.fence()
            g.emit()

        for l in range(depth):
            xsrc = xT if l == 0 else xb
            xdst3 = xb if l == 0 else yT
            if depth == 1:
                xdst3 = yT
            xsrc_v = xsrc.rearrange("(k p) n -> p k n", p=128)

            if "p1" in phases:
                with ExitStack() as es:
                    sb = lambda n, s, d: es.enter_context(nc.sbuf_tensor(un(n), s, d))
                    Win = sb("Win1", [128, 8, 3744], BF16); tWin = T()
                    wuq = sb("wuq", [128, 2, 384], BF16); twuq = T()
                    wukv = sb("wukv", [128, 1, 512], BF16); twukv = T()
                    wst = sb("wst", [128, 4, 128], BF16); twst = T()
                    gv = sb("gv", [128, 16], F32); tgv = T()
                    gq = sb("gq", [128, 4], F32); tgq = T()
                    lnb = sb("lnb", [128, 2, 256], F32); tlnb = T()
                    bs = sb("bs", [128, 2, 128], F32); tbs = T()
                    load_vec(gv[:, 0:8], tgv, W["attn_norm"][l].rearrange("(k p) -> p k", p=128), "v0")
                    load_vec(gv[:, 8:10], tgv, W["mla_g_cq"][l].rearrange("(k p) -> p k", p=128), "v1")
                    load_vec(gv[:, 10:11], tgv, W["mla_g_ckv"][l].rearrange("(k p) -> p k", p=128), "v2")
                    load_vec(gq[0:96, 0:1], tgq, W["mla_g_qn"][l].rearrange("(p o) -> p o", o=1), "v3")
                    load_vec(gq[0:96, 1:2], tgq, W["mla_g_kn"][l].rearrange("(p o) -> p o", o=1), "v0")
                    for b in (0, 64):
                        load_vec(gq[b:b + 64, 2:3], tgq, W["dil_g_qn"][l].rearrange("(p o) -> p o", o=1), "v1")
                        load_vec(gq[b:b + 64, 3:4], tgq, W["dil_g_kn"][l].rearrange("(p o) -> p o", o=1), "v2")
                    g.op("dve", lambda e: e.tensor_scalar(out=gq[0:96, 0:1], in0=gq[0:96, 0:1], scalar1=96.0 ** -0.5, scalar2=None, op0=ALU.mult),
                         reads=[tgq], writes=[tgq])
                    g.op("dve", lambda e: e.tensor_scalar(out=gq[:, 2:3], in0=gq[:, 2:3], scalar1=0.125, scalar2=None, op0=ALU.mult),
                         reads=[tgq], writes=[tgq])
                    load_vec(lnb[:, 0, :], tlnb, W["sgu_ln_g"][l:l + 1, :].partition_broadcast(128), "v3")
                    load_vec(lnb[:, 1, :], tlnb, W["sgu_ln_b"][l:l + 1, :].partition_broadcast(128), "v0")
                    for gi in range(4):
                        load_vec(bs[(gi % 2) * 64:(gi % 2) * 64 + 64, gi // 2, :], tbs, W["sgu_b_s"][l, gi:gi + 1, :].partition_broadcast(64), "v%d" % (gi % 4))
                    xr = Ring(nc, es, "xt", [128, 8, 512], F32, 1)
                    sqr = Ring(nc, es, "sq", [128, 8, 512], BF16, 1)
                    hr = Ring(nc, es, "hT", [128, 8, 512], BF16, 2)
                    rsr = Ring(nc, es, "rs", [128, 512], F32, 2)
                    hAo = Ring(nc, es, "hAo", [128, 2, 512], BF16, 2)
                    sgr = Ring(nc, es, "sig", [128, 512], BF16, 2)
                    ubr = Ring(nc, es, "ub", [128, 2, 512], BF16, 1)
                    vgr = Ring(nc, es, "vg", [128, 256], F32, 3)
                    vnr = Ring(nc, es, "vn", [128, 256], F32, 3)
                    vbr = Ring(nc, es, "vnb", [128, 256], BF16, 4)
                    str_ = Ring(nc, es, "bst", [128, 8], F32, 4)
                    ybr = Ring(nc, es, "ybo", [128, 2, 512], BF16, 1)
                    tmr = Ring(nc, es, "tmp", [128, 512], F32, 1)
                    cqr = Ring(nc, es, "cqn", [128, 3, 512], BF16, 1)
                    sqs = Ring(nc, es, "sqs", [128, 512], BF16, 2)
                    qor = Ring(nc, es, "qo", [96, 4, 512], BF16, 1)
                    kor = Ring(nc, es, "ko", [96, 4, 512], BF16, 1)
                    vcr = Ring(nc, es, "vco", [128, 4, 256], BF16, 1)
                    qdr = Ring(nc, es, "qdo", [128, 6, 512], BF16, 1)
                    kdr = Ring(nc, es, "kdo", [128, 6, 512], BF16, 1)
                    vdr = Ring(nc, es, "vdo", [128, 4, 1536], BF16, 1)
                    tbc = Ring(nc, es, "tbc", [96, 2, 512], F32, 1)
                    tbd = Ring(nc, es, "tbd", [128, 2, 512], F32, 1)
                    krr = Ring(nc, es, "krs", [128, 512], BF16, 1)
                    sq2 = Ring(nc, es, "sq2", [128, 2, 512], BF16, 2)
                    qb2 = Ring(nc, es, "qb2", [128, 2, 512], BF16, 2)
                    rs2 = Ring(nc, es, "rs2", [128, 2, 512], F32, 2)
                    t12 = Ring(nc, es, "t12", [128, 2, 512], F32, 2)
                    t22 = Ring(nc, es, "t22", [128, 2, 512], BF16, 1)
                    prot = Rot(BK[0:4])
                    psvb = [BK[4], BK[5]]
                    pairrot = Rot([0, 2])
                    for ring in (vdr,):
                        for i_ in range(ring.n):
                            tl, tt = ring.tiles[i_], ring.ts[i_]
                            g.op("pool", lambda e, tl=tl: e.memset(tl[:], 1.0), writes=[tt])

                    def bank2(i):
                        return PSA[:, i * 512:(i + 2) * 512].rearrange("p (a n) -> p a n", a=2), [bts[i], bts[i + 1]]

                    def proj(hT, thT, c0, c1, bank=None, nw=512):
                        pb, tpb = bank if bank is not None else prot.next()
                        for k in range(8):
                            g.op("pe", lambda e, k=k: e.matmul(pb[0:c1 - c0, 0:nw], lhsT=Win[:, k, c0:c1], rhs=hT[:, k, 0:nw],
                                                               start=(k == 0), stop=(k == 7)), reads=[tWin, thT], writes=[tpb])
                        return pb, tpb

                    def nr_stage1(P, gcol, pbase):
                        src, tsrc = bank2(pbase)
                        sq_, tsq_, _ = sq2.next()
                        qb_, tqb_, _ = qb2.next()
                        g.op("act", lambda e: e.activation(out=sq_[0:P], in_=src[0:P], func=AF.Square), reads=tsrc, writes=[tsq_])
                        g.op("act", lambda e: e.activation(out=qb_[0:P], in_=src[0:P], func=AF.Identity, scale=gq[0:P, gcol:gcol + 1]),
                             reads=tsrc + [tgq], writes=[tqb_])
                        return (sq_, tsq_, qb_, tqb_)

                    def nr_stage2(st1, P, onesM, inv_dim, R, tab, ttab, out_ap, tout):
                        (sq_, tsq_, qb_, tqb_) = st1
                        pss, tpss = bank2(4)
                        prr, tprr = bank2(6)
                        for a_ in range(2):
                            g.op("pe", lambda e, a_=a_: e.matmul(pss[0:P, a_, :], lhsT=onesM, rhs=sq_[0:P, a_, :], start=True, stop=True),
                                 reads=[tsq_, tmats], writes=[tpss[a_]])
                        for a_ in range(2):
                            g.op("pe", lambda e, a_=a_: e.matmul(prr[0:P, a_, :], lhsT=R, rhs=qb_[0:P, a_, :], start=True, stop=True),
                                 reads=[tqb_, tmats], writes=[tprr[a_]])
                        rs_, trs_, _ = rs2.next()
                        g.op("act", lambda e: e.activation(out=rs_[0:P], in_=pss[0:P], func=AF.Ln, bias=epsb[0:P, :], scale=inv_dim),
                             reads=tpss + [tepsb], writes=[trs_])
                        g.op("act", lambda e: e.activation(out=rs_[0:P], in_=rs_[0:P], func=AF.Exp, scale=-0.5), reads=[trs_], writes=[trs_])
                        t2_, tt2_, _ = t22.next()
                        g.op("dve", lambda e: e.tensor_tensor(out=t2_[0:P], in0=prr[0:P], in1=tab[0:P, 1:2, :].broadcast_to([P, 2, 512]), op=ALU.mult),
                             reads=tprr + [ttab], writes=[tt2_])
                        t1_, tt1_, _ = t12.next()
                        g.op("dve", lambda e: e.tensor_tensor(out=t1_[0:P], in0=qb_[0:P], in1=tab[0:P, 0:1, :].broadcast_to([P, 2, 512]), op=ALU.mult),
                             reads=[tqb_, ttab], writes=[tt1_])
                        g.op("dve", lambda e: e.tensor_tensor(out=t1_[0:P], in0=t1_[0:P], in1=t2_[0:P], op=ALU.add), reads=[tt1_, tt2_], writes=[tt1_])
                        g.op("pool", lambda e: e.tensor_tensor(out=out_ap, in0=t1_[0:P], in1=rs_[0:P], op=ALU.mult), reads=[tt1_, trs_], writes=[tout])

                    def front(tile):
                        n0 = tile[2]
                        xt, txt, xs = xr.next()
                        g.op("sp", lambda e, s_, xt=xt, n0=n0: e.dma_start(out=xt[:], in_=xsrc_v[:, :, n0:n0 + 512]).then_inc(s_, 16),
                             writes=[txt], dma=xs)
                        return rms_x(es, xt, txt, 512, sqr, hr, rsr, BK[6])

                    xtv = xr.tiles[0][:].rearrange("p a b -> p (a b)")
                    stage = BigStage([xtv[:, 0:2048], xtv[:, 2048:4096], sqr.tiles[0][:].rearrange("p a b -> p (a b)").bitcast(F32)], 2048, [xr.ts[0], sqr.ts[0]])
                    load_w(stage, Win, tWin, W["w_in"][l][:, 0:3744], gv[:, 0:8], tgv)
                    load_w(stage, wuq, twuq, W["mla_w_uq"][l], gv[:, 8:10], tgv)
                    load_w(stage, wukv, twukv, W["mla_w_ukv"][l], gv[:, 10:11], tgv)
                    load_w(stage, wst, twst, wsT[l].rearrange("g s t -> (g s) t"), None, None)
                    stage.release()
                    nxt = front(tiles[0])
                    for ti, (s, t0, n0, pp0) in enumerate(tiles):
                        hT, thT = nxt
                        vbs = []
                        for c in range(4):
                            pv, tpv = prot.next()
                            for k in range(8):
                                g.op("pe", lambda e, k=k, c=c, pv=pv, hT=hT: e.matmul(pv[:, 0:256], lhsT=hT[:, k, c * 128:(c + 1) * 128], rhs=Win[:, k, 768:1024],
                                                                                    start=(k == 0), stop=(k == 7)), reads=[tWin, thT], writes=[tpv])
                            vg, tvg, _ = vgr.next()
                            g.op("act", lambda e, vg=vg, pv=pv: e.activation(out=vg[:], in_=pv[:, 0:256], func=AF.Gelu_apprx_tanh), reads=[tpv], writes=[tvg])
                            st_, tst_, _ = str_.next()
                            g.op("dve", lambda e, st_=st_, vg=vg: e.bn_stats(out=st_[:, 0:6], in_=vg[:]), reads=[tvg], writes=[tst_])
                            g.op("dve", lambda e, st_=st_: e.bn_aggr(out=st_[:, 6:8], in_=st_[:, 0:6]), reads=[tst_], writes=[tst_])
                            g.op("pool", lambda e, st_=st_: e.tensor_scalar(out=st_[:, 7:8], in0=st_[:, 7:8], scalar1=EPS, scalar2=1.0, op0=ALU.add, op1=ALU.mult),
                                 reads=[tst_], writes=[tst_])
                            g.op("pool", lambda e, st_=st_: e.tensor_tensor(out=st_[:, 7:8], in0=st_[:, 7:8], in1=mhalf[:, :], op=ALU.pow), reads=[tst_, tmhalf], writes=[tst_])
                            vn, tvn, _ = vnr.next()
                            g.op("dve", lambda e, vn=vn, vg=vg, st_=st_: e.tensor_scalar(out=vn[:], in0=vg[:], scalar1=st_[:, 6:7], scalar2=st_[:, 7:8],
                                                                                          op0=ALU.subtract, op1=ALU.mult), reads=[tvg, tst_], writes=[tvn])
                            g.op("pool", lambda e, vn=vn: e.tensor_tensor(out=vn[:], in0=vn[:], in1=lnb[:, 0, :], op=ALU.mult), reads=[tvn, tlnb], writes=[tvn])
                            vb, tvb, _ = vbr.next()
                            g.op("pool", lambda e, vn=vn, vb=vb: e.tensor_tensor(out=vb[:], in0=vn[:], in1=lnb[:, 1, :], op=ALU.add), reads=[tvn, tlnb], writes=[tvb])
                            vbs.append((vb, tvb))
                        ub, tub, _ = ubr.next()
                        for j in range(2):
                            pu, tpu = proj(hT, thT, 512 + j * 128, 512 + j * 128 + 128)
                            g.op("act", lambda e, ub=ub, pu=pu, j=j: e.activation(out=ub[:, j, :], in_=pu[:], func=AF.Gelu_apprx_tanh), reads=[tpu], writes=[tub])
                        ho, tho, hos = hAo.next()
                        for j in range(2):
                            pa, tpa = proj(hT, thT, j * 128, j * 128 + 128)
                            pg, tpg = proj(hT, thT, 256 + j * 128, 256 + j * 128 + 128)
                            sg, tsg, _ = sgr.next()
                            g.op("act", lambda e, sg=sg, pg=pg: e.activation(out=sg[:], in_=pg[:], func=AF.Sigmoid), reads=[tpg], writes=[tsg])
                            g.op("dve", lambda e, ho=ho, pa=pa, sg=sg, j=j: e.tensor_tensor(out=ho[:, j, :], in0=pa[:], in1=sg[:], op=ALU.mult),
                                 reads=[tpa, tsg], writes=[tho])
                        psv = psvb
                        for c in range(4):
                            vb, tvb = vbs[c]
                            for gi in range(4):
                                pj, tpj = psv[gi // 2]
                                pbase = (gi % 2) * 64
                                g.op("pe", lambda e, pj=pj, pbase=pbase, vb=vb, gi=gi, c=c: e.matmul(
                                    pj[pbase:pbase + 64, c * 128:(c + 1) * 128], lhsT=vb[:, gi * 64:(gi + 1) * 64], rhs=wst[:, gi, :], start=True, stop=True),
                                    reads=[tvb, twst], writes=[tpj])
                        yb_, tyb, ybs = ybr.next()
                        for j in range(2):
                            pj, tpj = psv[j]
                            tm, ttm, _ = tmr.next()
                            g.op("dve", lambda e, tm=tm, pj=pj, j=j: e.tensor_tensor(out=tm[:].rearrange("p (c t) -> p c t", c=4), in0=pj[:].rearrange("p (c t) -> p c t", c=4),
                                                                                   in1=bs[:, j:j + 1, :].broadcast_to([128, 4, 128]), op=ALU.add),
                                 reads=[tpj, tbs], writes=[ttm])
                            g.op("pool", lambda e, tm=tm, yb_=yb_, ub=ub, j=j: e.tensor_tensor(out=yb_[:, j, :], in0=tm[:], in1=ub[:, j, :], op=ALU.mult),
                                 reads=[ttm, tub], writes=[tyb])
                        if ti + 1 < len(tiles):
                            nxt = front(tiles[ti + 1])
                        tc_, ttc, tcs = tbc.next()
                        g.op("sp", lambda e, s_, tc_=tc_, t0=t0, s=s: e.dma_start(out=tc_[:], in_=ropec.rearrange("c p n -> p c n")[:, :, t0 + rpos[s]:t0 + rpos[s] + 512]).then_inc(s_, 16),
                             writes=[ttc], dma=tcs)
                        td_, ttd, tds = tbd.next()
                        g.op("sp", lambda e, s_, td_=td_, t0=t0, s=s: e.dma_start(out=td_[:], in_=roped.rearrange("c p n -> p c n")[:, :, t0 + rpos[s]:t0 + rpos[s] + 512]).then_inc(s_, 16),
                             writes=[ttd], dma=tds)
                        cq, tcq, _ = cqr.next()
                        pcs = [proj(hT, thT, 1024 + j * 128, 1024 + j * 128 + 128, bank=BK[4 + j]) for j in range(3)]
                        krs, tkrs, _ = krr.next()
                        pkr, tpkr = BK[7]
                        for k in range(8):
                            g.op("pe", lambda e, k=k, hT=hT: e.matmul(pkr[64:96, :], lhsT=Win[:, k, 1408:1440], rhs=hT[:, k, :], start=(k == 0), stop=(k == 7)),
                                 reads=[tWin, thT], writes=[tpkr])
                        g.op("act", lambda e, krs=krs: e.activation(out=krs[64:96, :], in_=pkr[64:96, :], func=AF.Copy), reads=[tpkr], writes=[tkrs])
                        sqa = []
                        for j in range(3):
                            sq_, tsq_, _ = sqs.next() if j < 2 else tmr.next()
                            sqa.append((sq_, tsq_))
                        sqa[2] = sqa[0]
                        for j in range(2):
                            g.op("act", lambda e, sq_=sqa[j][0], p=pcs[j][0]: e.activation(out=sq_[:], in_=p[:], func=AF.Square), reads=[pcs[j][1]], writes=[sqa[j][1]])
                        pn, tpn = BK[7]
                        for j in range(2):
                            g.op("pe", lambda e, j=j, pn=pn: e.matmul(pn[:], lhsT=ONES, rhs=sqa[j][0][:], start=(j == 0), stop=(j == 1)),
                                 reads=[sqa[j][1], tmats], writes=[tpn])
                        rs_, trs_, _ = rsr.next()
                        rstd_from(pn[:], tpn, 1.0 / 256, rs_[:], trs_)
                        for j in range(2):
                            g.op("dve", lambda e, j=j, cq=cq, rs_=rs_, p=pcs[j][0]: e.tensor_tensor(out=cq[:, j, :], in0=p[:], in1=rs_[:], op=ALU.mult),
                                 reads=[pcs[j][1], trs_], writes=[tcq])
                        g.op("act", lambda e, sq_=sqa[2][0], p=pcs[2][0]: e.activation(out=sq_[:], in_=p[:], func=AF.Square), reads=[pcs[2][1]], writes=[sqa[2][1]])
                        g.op("pe", lambda e, pn=pn, sq_=sqa[2][0]: e.matmul(pn[:], lhsT=ONES, rhs=sq_[:], start=True, stop=True), reads=[sqa[2][1], tmats], writes=[tpn])
                        rs2_, trs2_, _ = rsr.next()
                        rstd_from(pn[:], tpn, 1.0 / 128, rs2_[:], trs2_)
                        g.op("dve", lambda e, cq=cq, rs2_=rs2_, p=pcs[2][0]: e.tensor_tensor(out=cq[:, 2, :], in0=p[:], in1=rs2_[:], op=ALU.mult),
                             reads=[pcs[2][1], trs2_], writes=[tcq])
                        qo, tqo, qos = qor.next()
                        ko, tko, kos = kor.next()
                        qdo, tqdo, qds = qdr.next()
                        kdo, tkdo, kds = kdr.next()
                        units = []
                        for c6 in (0, 2, 4):
                            units.append(("dq", c6))
                        for c6 in (0, 2, 4):
                            units.append(("dk", c6))
                        for h in (0, 2):
                            units.append(("cq", h))
                        for h in (0, 2):
                            units.append(("ck", h))

                        def u_stage1(u):
                            kind, i0 = u
                            pbase = pairrot.next()
                            if kind in ("dq", "dk"):
                                base = 1440 if kind == "dq" else 2208
                                for a_ in range(2):
                                    proj(hT, thT, base + (i0 + a_) * 128, base + (i0 + a_) * 128 + 128, bank=BK[pbase + a_])
                                return nr_stage1(128, 2 if kind == "dq" else 3, pbase)
                            if kind == "cq":
                                for a_ in range(2):
                                    pq, tpq = BK[pbase + a_]
                                    h = i0 + a_
                                    for kk in range(2):
                                        g.op("pe", lambda e, pq=pq, kk=kk, h=h, cq=cq: e.matmul(pq[0:96, :], lhsT=wuq[:, kk, h * 96:(h + 1) * 96], rhs=cq[:, kk, :],
                                                                                              start=(kk == 0), stop=(kk == 1)), reads=[twuq, tcq], writes=[tpq])
                                return nr_stage1(96, 0, pbase)
                            for a_ in range(2):
                                pk, tpk = BK[pbase + a_]
                                h = i0 + a_
                                g.op("pe", lambda e, pk=pk, h=h, cq=cq: e.matmul(pk[0:64, :], lhsT=wukv[:, 0, h * 128:h * 128 + 64], rhs=cq[:, 2, :], start=True, stop=True),
                                     reads=[twukv, tcq], writes=[tpk])
                                g.op("pe", lambda e, pk=pk, krs=krs: e.matmul(pk[64:96, :], lhsT=mats[64:96, 4, 64:96], rhs=krs[64:96, :], start=True, stop=True),
                                     reads=[tmats, tkrs], writes=[tpk])
                            return nr_stage1(96, 1, pbase)

                        def u_stage2(u, st1):
                            kind, i0 = u
                            if kind == "dq":
                                nr_stage2(st1, 128, BONES, 1.0 / 64, RD, td_, ttd, qdo[:, i0:i0 + 2, :], tqdo)
                            elif kind == "dk":
                                nr_stage2(st1, 128, BONES, 1.0 / 64, RD, td_, ttd, kdo[:, i0:i0 + 2, :], tkdo)
                            elif kind == "cq":
                                nr_stage2(st1, 96, ONES[0:96, 0:96], 1.0 / 96, RC[0:96, 0:96], tc_, ttc, qo[:, i0:i0 + 2, :], tqo)
                            else:
                                nr_stage2(st1, 96, ONES[0:96, 0:96], 1.0 / 96, RC[0:96, 0:96], tc_, ttc, ko[:, i0:i0 + 2, :], tko)

                        sts = [u_stage1(units[0])]
                        for ui in range(len(units)):
                            if ui + 1 < len(units):
                                sts.append(u_stage1(units[ui + 1]))
                            u_stage2(units[ui], sts[ui])
                        vco, tvco, vcs = vcr.next()
                        wv = wukv[:, 0, :].rearrange("p (h t d) -> p h t d", h=4, t=2)[:, :, 1, :]
                        for c in range(4):
                            pv, tpv = prot.next()
                            g.op("pe", lambda e, pv=pv, c=c, cq=cq: e.matmul(pv[:, 0:256].rearrange("p (h d) -> p h d", h=4), lhsT=cq[:, 2, c * 128:(c + 1) * 128], rhs=wv, start=True, stop=True),
                                 reads=[twukv, tcq], writes=[tpv])
                            g.op("act", lambda e, pv=pv, c=c, vco=vco: e.activation(out=vco[:, c, :], in_=pv[:, 0:256], func=AF.Copy), reads=[tpv], writes=[tvco])
                        vdo, tvdo, vds = vdr.next()
                        for c in range(4):
                            for gi in range(3):
                                pv, tpv = prot.next()
                                for k in range(8):
                                    g.op("pe", lambda e, pv=pv, k=k, c=c, gi=gi, hT=hT: e.matmul(pv[:, 0:256], lhsT=hT[:, k, c * 128:(c + 1) * 128],
                                                                                               rhs=Win[:, k, 2976 + gi * 256:2976 + gi * 256 + 256], start=(k == 0), stop=(k == 7)),
                                         reads=[tWin, thT], writes=[tpv])
                                if gi % 2 == 0:
                                    g.op("act", lambda e, pv=pv, c=c, gi=gi, vdo=vdo: e.activation(out=vdo[:, c, gi * 512:(gi + 1) * 512].rearrange("p (h x) -> p h x", h=4)[:, :, 0:64], in_=pv[:, 0:256].rearrange("p (h d) -> p h d", h=4), func=AF.Copy),
                                         reads=[tpv], writes=[tvdo])
                                else:
                                    g.op("dve", lambda e, pv=pv, c=c, gi=gi, vdo=vdo: e.tensor_copy(out=vdo[:, c, gi * 512:(gi + 1) * 512].rearrange("p (h x) -> p h x", h=4)[:, :, 0:64], in_=pv[:, 0:256].rearrange("p (h d) -> p h d", h=4)),
                                         reads=[tpv], writes=[tvdo])
                        def st(dst, src, tsrc, sem):
                            g.op("sp", lambda e, s_: e.dma_start(out=dst, in_=src).then_inc(s_, 16), reads=[tsrc], dma="st_" + sem)
                        st(hA.rearrange("(j p) n -> p j n", p=128)[:, :, pp0:pp0 + 512], ho[:], tho, hos)
                        st(yB.rearrange("(j p) n -> p j n", p=128)[:, :, n0:n0 + 512], yb_[:], tyb, ybs)
                        st(qc[:, :, n0:n0 + 512], qo[:], tqo, qos)
                        st(kc[:, :, n0:n0 + 512], ko[:], tko, kos)
                        st(vc[n0:n0 + 512, :].rearrange("(c p) f -> p c f", p=128), vco[:], tvco, vcs)
                        st(qd[:, :, n0:n0 + 512], qdo[:], tqdo, qds)
                        st(kd[:, :, pp0:pp0 + 512], kdo[:], tkdo, kds)
                        st(vd[pp0:pp0 + 512, :].rearrange("(c p) f -> p c f", p=128), vdo[:], tvdo, vds)
                        if joined and ((s == 0 and t0 >= seqs[0] - PAD) or (s == 1 and t0 < PAD)):
                            pp2 = (poff[1] + PAD + t0 - seqs[0]) if s == 0 else (poff[0] + PAD + seqs[0] + t0)
                            jcol = jft[:, 0:1]
                            g.op("pool", lambda e, ho=ho: e.tensor_scalar(out=ho[:], in0=ho[:], scalar1=jcol, scalar2=0.0, op0=ALU.mult, op1=ALU.add), reads=[tho, tjft], writes=[tho])
                            g.op("pool", lambda e, kdo=kdo: e.tensor_scalar(out=kdo[:], in0=kdo[:], scalar1=jcol, scalar2=0.0, op0=ALU.mult, op1=ALU.add), reads=[tkdo, tjft], writes=[tkdo])
                            g.op("pool", lambda e, vdo=vdo: e.tensor_scalar(out=vdo[:], in0=vdo[:], scalar1=jcol, scalar2=0.0, op0=ALU.mult, op1=ALU.add), reads=[tvdo, tjft], writes=[tvdo])
                            st(hA.rearrange("(j p) n -> p j n", p=128)[:, :, pp2:pp2 + 512], ho[:], tho, hos)
                            st(kd[:, :, pp2:pp2 + 512], kdo[:], tkdo, kds)
                            st(vd[pp2:pp2 + 512, :].rearrange("(c p) f -> p c f", p=128), vdo[:], tvdo, vds)
                            g.op("pool", lambda e, vdo=vdo: e.memset(vdo[:].rearrange("p c (q x) -> p c q x", x=128)[:, :, :, 64:128], 1.0), writes=[tvdo])
                    phase_end()
                    phase_end()

            if "p2a" in phases:
                with ExitStack() as es:
                    SM = max(gS.values())
                    Kc = es.enter_context(nc.sbuf_tensor(un("Kc"), [96, 4, SM], BF16))
                    Vc = es.enter_context(nc.sbuf_tensor(un("Vc"), [128, SM // 128, 4, 128], BF16))
                    tK = [[T() for _ in range(SM // 512)] for _ in range(4)]
                    tV = [T() for _ in range(SM // 512)]
                    qr = Ring(nc, es, "Qt", [96, 4, 512], BF16, 2)
                    ptr = Ring(nc, es, "pt", [128, 512], BF16, 6)
                    rzr = Ring(nc, es, "rz", [128, 512], F32, 2)
                    ocr = Ring(nc, es, "oct", [64, 4, 512], BF16, 2)
                    srot = Rot(BK[0:6])
                    urot = Rot(BK[6:8])
                    g.op("pool", lambda e: e.memset(Vc[:], 1.0), writes=tV)
                    cur_seq = -1
                    for (s, t0, n0, pp0) in tiles:
                        S = gS[s]
                        half = rpos[s] // 2048
                        if gof[s] != cur_seq:
                            cur_seq = gof[s]
                            b0 = gbase[s]
                            for c in range(S // 512):
                                for h in range(4):
                                    g.op("sp", lambda e, s_, h=h, c=c, b0=b0: e.dma_start(out=Kc[:, h, c * 512:(c + 1) * 512], in_=kc[:, h, b0 + c * 512:b0 + (c + 1) * 512]).then_inc(s_, 16),
                                         writes=[tK[h][c]], dma="K%d_%d" % (h, c % 2))
                                def ldv(e, s_, c=c, b0=b0):
                                    for a_ in range(4):
                                        r0 = b0 + c * 512 + a_ * 128
                                        e.dma_start(out=Vc[:, c * 4 + a_, :, 0:64], in_=vc[r0:r0 + 128, :].rearrange("p (h d) -> p h d", h=4)).then_inc(s_, 16)
                                g.op("sp", ldv, writes=[tV[c]], dma="V%d" % (c % 2), ndma=4)
                        Qt, tQ, qs = qr.next()
                        g.op("sp", lambda e, s_, Qt=Qt, n0=n0: e.dma_start(out=Qt[:], in_=qc[:, :, n0:n0 + 512]).then_inc(s_, 16), writes=[tQ], dma=qs)
                        oct_, toc, ocs = ocr.next()
                        nkt = S // 128
                        items = [(h, kt) for h in range(4) for kt in range(nkt)]
                        LA = 4
                        pts = {}
                        hb = {}
                        for i in range(len(items) + LA):
                            if i < len(items):
                                h, kt = items[i]
                                ps, tps = srot.next()
                                g.op("pe", lambda e, ps=ps, h=h, kt=kt, Qt=Qt: e.matmul(ps[:], lhsT=Kc[:, h, kt * 128:(kt + 1) * 128], rhs=Qt[:, h, :], start=True, stop=True),
                                     reads=[tK[h][kt // 4], tQ], writes=[tps])
                                pt, tpt, _ = ptr.next()
                                if S > 2048 and (kt // 16) != half:
                                    g.op("act", lambda e, ps=ps, pt=pt: e.activation(out=pt[:], in_=ps[:], func=AF.Exp, bias=jft[:, 1:2], scale=1.0), reads=[tps, tjft], writes=[tpt])
                                else:
                                    g.op("act", lambda e, ps=ps, pt=pt: e.activation(out=pt[:], in_=ps[:], func=AF.Exp), reads=[tps], writes=[tpt])
                                pts[i] = (pt, tpt)
                            j = i - LA
                            if j >= 0:
                                h, kt = items[j]
                                if kt == 0:
                                    hb[h] = urot.next()
                                pu, tpu = hb[h]
                                pt, tpt = pts.pop(j)
                                g.op("pe", lambda e, pu=pu, h=h, kt=kt, pt=pt, nkt=nkt: e.matmul(pu[:, :], lhsT=Vc[:, kt, h, :], rhs=pt[:], start=(kt == 0), stop=(kt == nkt - 1)),
                                     reads=[tV[kt // 4], tpt], writes=[tpu])
                                if kt == nkt - 1:
                                    rz, trz, _ = rzr.next()
                                    g.op("dve", lambda e, rz=rz, pu=pu: e.reciprocal(out=rz[64:128, :], in_=pu[64:128, :]), reads=[tpu], writes=[trz])
                                    g.op("dve", lambda e, rz=rz, pu=pu, h=h, oct_=oct_: e.tensor_tensor(out=oct_[:, h, :], in0=pu[0:64, :], in1=rz[64:128, :], op=ALU.mult),
                                         reads=[tpu, trz], writes=[toc])
                        g.op("sp", lambda e, s_, oct_=oct_, n0=n0: e.dma_start(out=oc[:, :, n0:n0 + 512], in_=oct_[:]).then_inc(s_, 16), reads=[toc], dma="st_" + ocs)
                    phase_end()

            if "p2b" in phases:
                with ExitStack() as es:
                    DIL = (1, 4, 16)
                    masks = es.enter_context(nc.sbuf_tensor(un("masks"), [128, 4, 512], BF16))
                    tmasks = T("masks")
                    g.op("sp", lambda e, s_: e.dma_start(out=masks[:], in_=masksd.rearrange("m p n -> p m n")).then_inc(s_, 16),
                         writes=[tmasks], dma="c1")
                    qr = Ring(nc, es, "Qd", [128, 4, 512], BF16, 2)
                    kw = [Ring(nc, es, "Kw%d" % gi, [128, 2, 512 + 128 * DIL[gi]], BF16, 2) for gi in range(2)]
                    NVT = (5, 8)
                    vw = [Ring(nc, es, "Vw%d" % gi, [128, NVT[gi], 512], BF16, 2) for gi in range(2)]
                    Qd2 = es.enter_context(nc.sbuf_tensor(un("Qd2"), [128, 2, 2048], BF16)); tQ2 = T()
                    Kw2 = es.enter_context(nc.sbuf_tensor(un("Kw2"), [128, 2, 4096], BF16)); tK2 = T()
                    Vw2 = es.enter_context(nc.sbuf_tensor(un("Vw2"), [128, 32, 512], BF16)); tV2 = T()
                    acc2 = es.enter_context(nc.sbuf_tensor(un("acc2"), [128, 4, 2048], F32))
                    tacc2 = [T() for _ in range(4)]
                    plr = Ring(nc, es, "pl", [128, 512], BF16, 3)
                    pur = Ring(nc, es, "pu", [128, 512], BF16, 3)
                    aur = Ring(nc, es, "accu", [128, 512], F32, 2)
                    odr = Ring(nc, es, "odt", [64, 4, 512], BF16, 2)
                    srot = Rot([(BK[0], BK[1]), (BK[2], BK[3])])
                    uzrot = Rot([BK[4], BK[5], BK[6]])
                    vdv = vd
                    ML = masks[:, 0, :]
                    MU = masks[:, 1, :]

                    def load_g2(s):
                        b0 = poff[s]
                        g.op("sp", lambda e, s_: e.dma_start(out=Qd2[:], in_=qd[:, 4:6, soff[s]:soff[s] + 2048]).then_inc(s_, 16), writes=[tQ2], dma="q2")
                        g.op("sp", lambda e, s_: e.dma_start(out=Kw2[:], in_=kd[:, 4:6, b0:b0 + 4096]).then_inc(s_, 16), writes=[tK2], dma="k2")

                        def ldv2(e, s_):
                            for m in range(2):
                                src = vdv[b0 + 2048 * m:b0 + 2048 * m + 2048, 1024:1536].rearrange("(k r) f -> k r f", r=16)
                                e.dma_start(out=Vw2[:, :, :].rearrange("p (r m) f -> p r m f", m=2)[:, :, m, :], in_=src).then_inc(s_, 16)
                        g.op("sp", ldv2, writes=[tV2], dma="v2", ndma=2)

                    emc = [0]

                    def exp_mask(pL, tpL, pU, tpU):
                        el, tel, _ = plr.next()
                        eu, teu, _ = pur.next()
                        g.op("act", lambda e: e.activation(out=el[:], in_=pL[:], func=AF.Exp), reads=[tpL], writes=[tel])
                        g.op("act", lambda e: e.activation(out=eu[:], in_=pU[:], func=AF.Exp), reads=[tpU], writes=[teu])
                        emc[0] += 1
                        e1_, e2_ = ("pool", "dve") if emc[0] % 2 == 0 else ("dve", "pool")
                        g.op(e1_, lambda e: e.tensor_tensor(out=el[:], in0=el[:], in1=ML, op=ALU.mult), reads=[tel, tmasks], writes=[tel])
                        g.op(e2_, lambda e: e.tensor_tensor(out=eu[:], in0=eu[:], in1=MU, op=ALU.mult), reads=[teu, tmasks], writes=[teu])
                        return el, tel, eu, teu

                    def g2S(hh, rg):
                        pb_ = (hh % 2) * 64
                        (pL, tpL), (pU, tpU) = srot.next()
                        for j in range(4):
                            r = 4 * rg + j
                            rq = Qd2[pb_:pb_ + 64, hh // 2, r:r + 127 * 16 + 1:16]
                            k0 = Kw2[pb_:pb_ + 64, hh // 2, r:r + 127 * 16 + 1:16]
                            k1 = Kw2[pb_:pb_ + 64, hh // 2, r + 2048:r + 2048 + 127 * 16 + 1:16]
                            g.op("pe", lambda e, pL=pL, k0=k0, rq=rq, j=j: e.matmul(pL[:, j * 128:(j + 1) * 128], lhsT=k0, rhs=rq, start=True, stop=True), reads=[tK2, tQ2], writes=[tpL])
                            g.op("pe", lambda e, pU=pU, k1=k1, rq=rq, j=j: e.matmul(pU[:, j * 128:(j + 1) * 128], lhsT=k1, rhs=rq, start=True, stop=True), reads=[tK2, tQ2], writes=[tpU])
                        return (hh, rg) + exp_mask(pL, tpL, pU, tpU)

                    def g2PV(ctx):
                        (hh, rg, el, tel, eu, teu) = ctx
                        po, tpo = uzrot.next()
                        for j in range(4):
                            r = 4 * rg + j
                            g.op("pe", lambda e, po=po, r=r, j=j, el=el, hh=hh: e.matmul(po[:, j * 128:(j + 1) * 128], lhsT=Vw2[:, 2 * r, hh * 128:(hh + 1) * 128], rhs=el[:, j * 128:(j + 1) * 128], start=True, stop=False),
                                 reads=[tV2, tel], writes=[tpo])
                            g.op("pe", lambda e, po=po, r=r, j=j, eu=eu, hh=hh: e.matmul(po[:, j * 128:(j + 1) * 128], lhsT=Vw2[:, 2 * r + 1, hh * 128:(hh + 1) * 128], rhs=eu[:, j * 128:(j + 1) * 128], start=False, stop=True),
                                 reads=[tV2, teu], writes=[tpo])
                        dst = acc2[:, hh, :].rearrange("p (i r) -> p r i", r=16)[:, 4 * rg:4 * rg + 4, :]
                        src = po[:].rearrange("p (r i) -> p r i", r=4)
                        if rg % 2 == 0:
                            g.op("act", lambda e: e.activation(out=dst, in_=src, func=AF.Copy), reads=[tpo], writes=[tacc2[hh]])
                        else:
                            g.op("dve", lambda e: e.tensor_copy(out=dst, in_=src), reads=[tpo], writes=[tacc2[hh]])

                    nseg = len(seqs)
                    load_g2(0)
                    for sg in range(nseg):
                        its2 = [(hh, rg) for hh in range(4) for rg in range(4)]
                        ctx2 = [g2S(*its2[0])]
                        for ii in range(len(its2)):
                            if ii + 1 < len(its2):
                                ctx2.append(g2S(*its2[ii + 1]))
                            g2PV(ctx2[ii])
                        if sg + 1 < nseg:
                            load_g2(sg + 1)
                        for (s, t0, n0, pp0) in [t_ for t_ in tiles if t_[0] == sg]:
                            Qd, tQ, qs = qr.next()
                            g.op("sp", lambda e, s_, Qd=Qd, n0=n0: e.dma_start(out=Qd[:], in_=qd[:, 0:4, n0:n0 + 512]).then_inc(s_, 16), writes=[tQ], dma=qs)
                            KW, VW = [], []
                            for gi in range(2):
                                d = DIL[gi]
                                k_, tk_, ks = kw[gi].next()
                                g.op("sp", lambda e, s_, k_=k_, gi=gi, d=d, pp0=pp0: e.dma_start(out=k_[:], in_=kd[:, 2 * gi:2 * gi + 2, pp0 - 64 * d:pp0 + 512 + 64 * d]).then_inc(s_, 16),
                                     writes=[tk_], dma=ks)
                                KW.append((k_, tk_))
                                v_, tv_, vs = vw[gi].next()
                                base = pp0 - 64 * d
                                if gi == 0:
                                    g.op("sp", lambda e, s_, v_=v_, base=base: e.dma_start(out=v_[:], in_=vdv[base:base + 640, 0:512].rearrange("(m p) f -> p m f", p=128)).then_inc(s_, 16),
                                         writes=[tv_], dma=vs)
                                else:
                                    def ld1(e, s_, v_=v_, base=base):
                                        for m in range(2):
                                            src = vdv[base + 512 * m:base + 512 * m + 512, 512:1024].rearrange("(k r) f -> k r f", r=4)
                                            e.dma_start(out=v_[:, :, :].rearrange("p (r m) f -> p r m f", m=2)[:, :, m, :], in_=src).then_inc(s_, 16)
                                    g.op("sp", ld1, writes=[tv_], dma=vs, ndma=2)
                                VW.append((v_, tv_))
                            odt, tod, ods = odr.next()
                            accs = {}

                            def stageS(hh, gi):
                                pb_ = (hh % 2) * 64
                                d = DIL[gi]
                                k_, tk_ = KW[gi]
                                cq_ = 2 * gi + hh // 2
                                (pL, tpL), (pU, tpU) = srot.next()
                                for b in range(4):
                                    if gi == 0:
                                        rq = Qd[pb_:pb_ + 64, cq_, b * 128:(b + 1) * 128]
                                        kl = [k_[pb_:pb_ + 64, hh // 2, b * 128:b * 128 + 128], k_[pb_:pb_ + 64, hh // 2, (b + 1) * 128:(b + 1) * 128 + 128]]
                                    else:
                                        rq = Qd[pb_:pb_ + 64, cq_, b:512:d]
                                        kl = [k_[pb_:pb_ + 64, hh // 2, b:b + 127 * d + 1:d], k_[pb_:pb_ + 64, hh // 2, b + 128 * d:b + 128 * d + 127 * d + 1:d]]
                                    g.op("pe", lambda e, pL=pL, kl=kl, rq=rq, b=b: e.matmul(pL[:, b * 128:(b + 1) * 128], lhsT=kl[0], rhs=rq, start=True, stop=True),
                                         reads=[tk_, tQ], writes=[tpL])
                                    g.op("pe", lambda e, pU=pU, kl=kl, rq=rq, b=b: e.matmul(pU[:, b * 128:(b + 1) * 128], lhsT=kl[1], rhs=rq, start=True, stop=True),
                                         reads=[tk_, tQ], writes=[tpU])
                                return (hh, gi) + exp_mask(pL, tpL, pU, tpU)

                            def stagePV(ctx):
                                (hh, gi, el, tel, eu, teu) = ctx
                                d = DIL[gi]
                                v_, tv_ = VW[gi]
                                if gi == 0:
                                    accs[hh] = aur.next()
                                (acc, tacc, _) = accs[hh]
                                po, tpo = uzrot.next()
                                for b in range(4):
                                    vt = (b, b + 1) if gi == 0 else (2 * b, 2 * b + 1)
                                    g.op("pe", lambda e, po=po, v_=v_, vt=vt, el=el, b=b, hh=hh: e.matmul(
                                        po[:, b * 128:(b + 1) * 128], lhsT=v_[:, vt[0], hh * 128:(hh + 1) * 128], rhs=el[:, b * 128:(b + 1) * 128], start=True, stop=False),
                                        reads=[tv_, tel], writes=[tpo])
                                    g.op("pe", lambda e, po=po, v_=v_, vt=vt, eu=eu, b=b, hh=hh: e.matmul(
                                        po[:, b * 128:(b + 1) * 128], lhsT=v_[:, vt[1], hh * 128:(hh + 1) * 128], rhs=eu[:, b * 128:(b + 1) * 128], start=False, stop=True),
                                        reads=[tv_, teu], writes=[tpo])
                                if gi == 0:
                                    g.op("dve", lambda e, acc=acc, po=po, hh=hh, t0=t0: e.tensor_tensor(out=acc[:], in0=po[:], in1=acc2[:, hh, t0:t0 + 512], op=ALU.add),
                                         reads=[tpo, tacc2[hh]], writes=[tacc])
                                else:
                                    pf, tpf = BK[7]
                                    avz = acc[64:128, :].rearrange("p (i r) -> p r i", r=d)
                                    g.op("dve", lambda e, avz=avz, po=po, d=d: e.tensor_tensor(out=avz, in0=po[64:128, :].rearrange("p (r i) -> p r i", r=d), in1=avz, op=ALU.add),
                                         reads=[tpo, tacc], writes=[tacc])
                                    g.op("act", lambda e, acc=acc: e.activation(out=acc[64:128, :], in_=acc[64:128, :], func=AF.Ln), reads=[tacc], writes=[tacc])
                                    g.op("act", lambda e, acc=acc: e.activation(out=acc[64:128, :], in_=acc[64:128, :], func=AF.Exp, scale=-1.0), reads=[tacc], writes=[tacc])
                                    g.op("dve", lambda e, acc=acc, po=po, pf=pf, d=d: e.tensor_tensor(out=pf[0:64, :].rearrange("p (i r) -> p r i", r=d), in0=po[0:64, :].rearrange("p (r i) -> p r i", r=d),
                                                                                           in1=acc[0:64, :].rearrange("p (i r) -> p r i", r=d), op=ALU.add),
                                         reads=[tpo, tacc], writes=[tpf])
                                    g.op("dve", lambda e, acc=acc, pf=pf, odt=odt, hh=hh: e.tensor_tensor(out=odt[:, hh, :], in0=pf[0:64, :], in1=acc[64:128, :], op=ALU.mult),
                                         reads=[tpf, tacc], writes=[tod])

                            its = [(hh, gi) for hh in range(4) for gi in range(2)]
                            ctxs = [stageS(*its[0])]
                            for ii in range(len(its)):
                                if ii + 1 < len(its):
                                    ctxs.append(stageS(*its[ii + 1]))
                                stagePV(ctxs[ii])
                            g.op("sp", lambda e, s_, odt=odt, n0=n0: e.dma_start(out=od[:, :, n0:n0 + 512], in_=odt[:]).then_inc(s_, 16), reads=[tod], dma="st_" + ods)
                    phase_end()

            if "p2c" in phases:
                with ExitStack() as es:
                    sb = lambda n, s, d: es.enter_context(nc.sbuf_tensor(un(n), s, d))
                    WinG = sb("WinG", [128, 8, 4096], BF16); tWinG = T()
                    woa = sb("woa", [128, 2, 1024], BF16); twoa = T()
                    wob = sb("wob", [128, 2, 1024], BF16); twob = T()
                    woc = sb("woc", [64, 4, 1024], BF16); twoc = T()
                    wod = sb("wod", [64, 4, 1024], BF16); twod = T()
                    wout = sb("wout", [128, 8, 1024], BF16); twout = T()
                    Dg = sb("Dg", [128, 2, 31, 128], BF16); tDg = T()
                    gv = sb("gv", [128, 8], F32); tgv = T()
                    cvp = sb("cvp", [128, 2, 34], F32); tcvp = T()
                    load_vec(gv[:, 0:8], tgv, W["attn_norm"][l].rearrange("(k p) -> p k", p=128), "v0")
                    for j in range(2):
                        load_vec(cvp[:, j, 0:31], tcvp, W["conv_w"][l][:, j * 128:(j + 1) * 128].rearrange("k p -> p k"), "v1")
                        load_vec(cvp[:, j, 31:32], tcvp, W["conv_b"][l][j * 128:(j + 1) * 128].rearrange("(p o) -> p o", o=1), "v2")
                        load_vec(cvp[:, j, 32:33], tcvp, W["conv_ln_g"][l][j * 128:(j + 1) * 128].rearrange("(p o) -> p o", o=1), "v3")
                        load_vec(cvp[:, j, 33:34], tcvp, W["conv_ln_b"][l][j * 128:(j + 1) * 128].rearrange("(p o) -> p o", o=1), "v0")
                    for j in range(2):
                        for k in range(31):
                            eng = "dve" if k % 2 == 0 else "pool"
                            g.op(eng, lambda e, j=j, k=k: e.tensor_scalar(out=Dg[:, j, k, :], in0=mats[:, 4, :], scalar1=cvp[:, j, k:k + 1], scalar2=0.0, op0=ALU.mult, op1=ALU.add),
                                 reads=[tmats, tcvp], writes=[tDg])
                    xr = Ring(nc, es, "xt", [128, 8, 512], F32, 1)
                    hr = Ring(nc, es, "hT", [128, 8, 512], BF16, 1)
                    rsr = Ring(nc, es, "rs", [128, 512], F32, 2)
                    gtr = Ring(nc, es, "gt", [128, 4, 512], BF16, 2)
                    haw = Ring(nc, es, "haw", [128, 2, 542], BF16, 2)
                    cvr = Ring(nc, es, "cv", [128, 2, 512], F32, 1)
                    cbr = Ring(nc, es, "cvb", [128, 4, 512], BF16, 1)
                    mnr = Ring(nc, es, "mn", [128, 3, 512], F32, 1)
                    bar = Ring(nc, es, "bA", [128, 2, 512], BF16, 1)
                    ybr = Ring(nc, es, "yBt", [128, 2, 512], BF16, 1)
                    ocr = Ring(nc, es, "oct", [64, 4, 512], BF16, 1)
                    odr = Ring(nc, es, "odt", [64, 4, 512], BF16, 1)
                    mgr = Ring(nc, es, "mg", [128, 8, 512], BF16, 1)
                    sqr = hr
                    tpr = Ring(nc, es, "tp", [128, 4, 512], BF16, 1)
                    xor2 = Ring(nc, es, "xo2", [128, 512], F32, 2)
                    edr = Ring(nc, es, "edge", [128, 8, 1], F32, 1)
                    prot = Rot(BK[0:8])
                    xtv = xr.tiles[0][:].rearrange("p a b -> p (a b)")
                    stage = BigStage([xtv[:, 0:2048], xtv[:, 2048:4096], mgr.tiles[0][:].rearrange("p a b -> p (a b)").bitcast(F32)], 2048, [xr.ts[0], mgr.ts[0]])
                    load_w(stage, WinG, tWinG, W["w_in"][l][:, 3744:7840], gv[:, 0:8], tgv)
                    load_w(stage, woa, twoa, W["conv_w_o"][l], None, None)
                    load_w(stage, wob, twob, W["sgu_w_o"][l], None, None)
                    load_w(stage, woc, twoc, W["mla_w_o"][l], None, None, np_=64)
                    load_w(stage, wod, twod, W["dil_w_o"][l], None, None, np_=64)
                    load_w(stage, wout, twout, W["w_out"][l], None, None)
                    stage.release()

                    def front2(tile):
                        (s, t0, n0, pp0) = tile
                        xt, txt, xs = xr.next()
                        g.op("sp", lambda e, s_, xt=xt, n0=n0: e.dma_start(out=xt[:], in_=xsrc_v[:, :, n0:n0 + 512]).then_inc(s_, 16), writes=[txt], dma=xs)
                        hw, thw, hws = haw.next()
                        g.op("sp", lambda e, s_, hw=hw, pp0=pp0: e.dma_start(out=hw[:], in_=hA.rearrange("(j p) n -> p j n", p=128)[:, :, pp0 - 15:pp0 + 527]).then_inc(s_, 16), writes=[thw], dma=hws)
                        ybt, tybt, ybs = ybr.next()
                        g.op("sp", lambda e, s_, ybt=ybt, n0=n0: e.dma_start(out=ybt[:], in_=yB.rearrange("(j p) n -> p j n", p=128)[:, :, n0:n0 + 512]).then_inc(s_, 16), writes=[tybt], dma=ybs)
                        oct_, toct, ocs = ocr.next()
                        g.op("sp", lambda e, s_, oct_=oct_, n0=n0: e.dma_start(out=oct_[:], in_=oc[:, :, n0:n0 + 512]).then_inc(s_, 16), writes=[toct], dma=ocs)
                        odt, todt, ods = odr.next()
                        g.op("sp", lambda e, s_, odt=odt, n0=n0: e.dma_start(out=odt[:], in_=od[:, :, n0:n0 + 512]).then_inc(s_, 16), writes=[todt], dma=ods)
                        cv, tcv, _ = cvr.next()
                        cb, tcb, _ = cbr.next()
                        for j in range(2):
                            pc, tpc = prot.next()
                            for k in range(31):
                                g.op("pe", lambda e, pc=pc, j=j, k=k, hw=hw: e.matmul(pc[:], lhsT=Dg[:, j, k, :], rhs=hw[:, j, k:k + 512], start=(k == 0), stop=(k == 30)),
                                     reads=[tDg, thw], writes=[tpc])
                            g.op("act", lambda e, pc=pc, cv=cv, j=j: e.activation(out=cv[:, j, :], in_=pc[:], func=AF.Identity, bias=cvp[:, j, 31:32], scale=1.0),
                                 reads=[tpc, tcvp], writes=[tcv])
                            g.op("pool", lambda e, cv=cv, cb=cb, j=j: e.tensor_copy(out=cb[:, j, :], in_=cv[:, j, :]), reads=[tcv], writes=[tcb])
                            g.op("act", lambda e, cv=cv, cb=cb, j=j: e.activation(out=cb[:, 2 + j, :], in_=cv[:, j, :], func=AF.Square), reads=[tcv], writes=[tcb])
                        hT, thT = rms_x(es, xt, txt, 512, sqr, hr, rsr, prot.next())
                        p1, tp1 = prot.next()
                        p2, tp2 = prot.next()
                        for j in range(2):
                            g.op("pe", lambda e, p1=p1, cb=cb, j=j: e.matmul(p1[:], lhsT=ONES, rhs=cb[:, j, :], start=(j == 0), stop=(j == 1)), reads=[tcb, tmats], writes=[tp1])
                        for j in range(2):
                            g.op("pe", lambda e, p2=p2, cb=cb, j=j: e.matmul(p2[:], lhsT=ONES, rhs=cb[:, 2 + j, :], start=(j == 0), stop=(j == 1)), reads=[tcb, tmats], writes=[tp2])
                        mn, tmn, _ = mnr.next()
                        g.op("dve", lambda e, mn=mn, p1=p1: e.tensor_scalar(out=mn[:, 0, :], in0=p1[:], scalar1=1.0 / 256, scalar2=None, op0=ALU.mult), reads=[tp1], writes=[tmn])
                        g.op("pool", lambda e, mn=mn: e.tensor_tensor(out=mn[:, 1, :], in0=mn[:, 0, :], in1=mn[:, 0, :], op=ALU.mult), reads=[tmn], writes=[tmn])
                        g.op("dve", lambda e, mn=mn, p2=p2: e.scalar_tensor_tensor(out=mn[:, 2, :], in0=p2[:], scalar=1.0 / 256, in1=mn[:, 1, :], op0=ALU.mult, op1=ALU.subtract),
                             reads=[tp2, tmn], writes=[tmn])
                        g.op("act", lambda e, mn=mn: e.activation(out=mn[:, 2, :], in_=mn[:, 2, :], func=AF.Ln, bias=epsb[:, :], scale=1.0), reads=[tmn, tepsb], writes=[tmn])
                        g.op("act", lambda e, mn=mn: e.activation(out=mn[:, 2, :], in_=mn[:, 2, :], func=AF.Exp, scale=-0.5), reads=[tmn], writes=[tmn])
                        bA, tbA, _ = bar.next()
                        for j in range(2):
                            g.op("dve", lambda e, cv=cv, mn=mn, j=j: e.tensor_tensor(out=cv[:, j, :], in0=cv[:, j, :], in1=mn[:, 0, :], op=ALU.subtract), reads=[tcv, tmn], writes=[tcv])
                            g.op("pool", lambda e, cv=cv, mn=mn, j=j: e.tensor_tensor(out=cv[:, j, :], in0=cv[:, j, :], in1=mn[:, 2, :], op=ALU.mult), reads=[tcv, tmn], writes=[tcv])
                            g.op("act", lambda e, cv=cv, bA=bA, j=j: e.activation(out=bA[:, j, :], in_=cv[:, j, :], func=AF.Silu, bias=cvp[:, j, 33:34], scale=cvp[:, j, 32:33]),
                                 reads=[tcv, tcvp], writes=[tbA])
                        return (hT, thT, bA, tbA, ybt, tybt, oct_, toct, odt, todt)

                    nxt2 = front2(tiles[0])
                    for ti, (s, t0, n0, pp0) in enumerate(tiles):
                        (hT, thT, bA, tbA, ybt, tybt, oct_, toct, odt, todt) = nxt2
                        mg, tmg, _ = mgr.next()
                        for m in range(8):
                            gt, tgt, _ = gtr.next()
                            for i in range(4):
                                pg, tpg = prot.next()
                                c0 = (i * 8 + m) * 128
                                for k in range(8):
                                    g.op("pe", lambda e, pg=pg, k=k, c0=c0, hT=hT: e.matmul(pg[:], lhsT=WinG[:, k, c0:c0 + 128], rhs=hT[:, k, :], start=(k == 0), stop=(k == 7)),
                                         reads=[tWinG, thT], writes=[tpg])
                                g.op("act", lambda e, pg=pg, gt=gt, i=i: e.activation(out=gt[:, i, :], in_=pg[:], func=AF.Sigmoid), reads=[tpg], writes=[tgt])
                            tp, ttp, _ = tpr.next()
                            ys = []
                            pya = prot.next()
                            for j in range(2):
                                g.op("pe", lambda e, p=pya[0], j=j, m=m: e.matmul(p[:], lhsT=woa[:, j, m * 128:(m + 1) * 128], rhs=bA[:, j, :], start=(j == 0), stop=(j == 1)),
                                     reads=[twoa, tbA], writes=[pya[1]])
                            pyb = prot.next()
                            for j in range(2):
                                g.op("pe", lambda e, p=pyb[0], j=j, m=m: e.matmul(p[:], lhsT=wob[:, j, m * 128:(m + 1) * 128], rhs=ybt[:, j, :], start=(j == 0), stop=(j == 1)),
                                     reads=[twob, tybt], writes=[pyb[1]])
                            pyc = prot.next()
                            for h in range(4):
                                g.op("pe", lambda e, p=pyc[0], h=h, m=m: e.matmul(p[:], lhsT=woc[:, h, m * 128:(m + 1) * 128], rhs=oct_[:, h, :], start=(h == 0), stop=(h == 3)),
                                     reads=[twoc, toct], writes=[pyc[1]])
                            pyd = prot.next()
                            for h in range(4):
                                g.op("pe", lambda e, p=pyd[0], h=h, m=m: e.matmul(p[:], lhsT=wod[:, h, m * 128:(m + 1) * 128], rhs=odt[:, h, :], start=(h == 0), stop=(h == 3)),
                                     reads=[twod, todt], writes=[pyd[1]])
                            for i, (p, tp_) in enumerate((pya, pyb, pyc, pyd)):
                                g.op("dve", lambda e, p=p, i=i, tp=tp, gt=gt: e.tensor_tensor(out=tp[:, i, :], in0=p[:], in1=gt[:, i, :], op=ALU.mult), reads=[tp_, tgt], writes=[ttp])
                            g.op("pool", lambda e, tp=tp: e.tensor_tensor(out=tp[:, 0:2, :], in0=tp[:, 0:2, :], in1=tp[:, 2:4, :], op=ALU.add), reads=[ttp], writes=[ttp])
                            g.op("pool", lambda e, tp=tp, mg=mg, m=m: e.tensor_tensor(out=mg[:, m, :], in0=tp[:, 0, :], in1=tp[:, 1, :], op=ALU.add), reads=[ttp], writes=[tmg])
                        if ti + 1 < len(tiles):
                            nxt2 = front2(tiles[ti + 1])
                        edge_needed = joined and ((s == 0 and t0 == seqs[0] - 512) or (s == 1 and t0 == 0))
                        if edge_needed:
                            ed, ted, eds = edr.next()
                            ecol = 511 if s == 0 else 0
                        for m2 in range(8):
                            xo, txo, xos = xor2.next()
                            g.op("sp", lambda e, s_, xo=xo, n0=n0, m2=m2: e.dma_start(out=xo[:], in_=xsrc_v[:, m2, n0:n0 + 512]).then_inc(s_, 16), writes=[txo], dma=xos)
                            po, tpo = prot.next()
                            for k in range(8):
                                g.op("pe", lambda e, po=po, k=k, m2=m2, mg=mg: e.matmul(po[:], lhsT=wout[:, k, m2 * 128:(m2 + 1) * 128], rhs=mg[:, k, :], start=(k == 0), stop=(k == 7)),
                                     reads=[twout, tmg], writes=[tpo])
                            g.op("dve", lambda e, po=po, xo=xo: e.tensor_tensor(out=xo[:], in0=po[:], in1=xo[:], op=ALU.add), reads=[tpo, txo], writes=[txo])
                            g.op("sp", lambda e, s_, xo=xo, pp0=pp0, m2=m2: e.dma_start(out=xa.rearrange("(k p) n -> p k n", p=128)[:, m2, pp0:pp0 + 512], in_=xo[:]).then_inc(s_, 16),
                                 reads=[txo], dma="st_" + xos)
                            if edge_needed:
                                g.op("pool", lambda e, xo=xo, ed=ed, m2=m2, ecol=ecol: e.tensor_scalar(out=ed[:, m2, :], in0=xo[:, ecol:ecol + 1], scalar1=jft[:, 0:1], scalar2=0.0, op0=ALU.mult, op1=ALU.add),
                                     reads=[txo, tjft], writes=[ted])
                        if edge_needed:
                            dcol = (poff[1] + PAD - 1) if s == 0 else (poff[0] + PAD + seqs[0])
                            g.op("sp", lambda e, s_, ed=ed, dcol=dcol: e.dma_start(out=xa.rearrange("(k p) n -> p k n", p=128)[:, :, dcol:dcol + 1], in_=ed[:]).then_inc(s_, 16),
                                 reads=[ted], dma="st_" + eds)
                    phase_end()

            if "p3" in phases:
                with ExitStack() as es:
                    sb = lambda n, s, d: es.enter_context(nc.sbuf_tensor(un(n), s, d))
                    Wup = sb("Wup", [128, 8, 5632], BF16); tWup = T()
                    Wdn = sb("Wdn", [128, 22, 1024], BF16); tWdn = T()
                    gv = sb("gv", [128, 8], F32); tgv = T()
                    cw = sb("cw", [128, 44, 4], F32); tcw = T()
                    load_vec(gv[:, 0:8], tgv, W["ffn_norm"][l].rearrange("(k p) -> p k", p=128), "v0")
                    for k in range(3):
                        load_vec(cw[:, :, k:k + 1], tcw, W["ffn_conv_w"][l][k].rearrange("(c p o) -> p c o", p=128, o=1), "v%d" % (k + 1))
                    load_vec(cw[:, :, 3:4], tcw, W["ffn_conv_b"][l].rearrange("(c p o) -> p c o", p=128, o=1), "v0")
                    xr = Ring(nc, es, "xw", [128, 8, 512], F32, 1)
                    sqr = Ring(nc, es, "sq", [128, 8, 512], BF16, 1)
                    hr = Ring(nc, es, "hT", [128, 8, 512], BF16, 1)
                    rsr = Ring(nc, es, "rs", [128, 512], F32, 2)
                    uar = Ring(nc, es, "ua", [128, 512], F32, 2)
                    ubr = Ring(nc, es, "ubf", [128, 512], F32, 2)
                    gTr = Ring(nc, es, "gT", [128, 22, 512], BF16, 1)
                    xor_ = Ring(nc, es, "xo", [128, 512], F32, 2)
                    prot = Rot(BK[0:8])
                    xav = xa.rearrange("(k p) n -> p k n", p=128)
                    xdv = xdst3.rearrange("(k p) n -> p k n", p=128)
                    gtv = gTr.tiles[0][:].rearrange("p a b -> p (a b)").bitcast(F32)
                    stage = BigStage([gtv[:, 0:2816], gtv[:, 2816:5632], xr.tiles[0][:].rearrange("p a b -> p (a b)")[:, 0:2816]], 2816, [gTr.ts[0], xr.ts[0]])
                    load_w(stage, Wup, tWup, W["ffn_w_up"][l], gv[:, 0:8], tgv)
                    load_w(stage, Wdn, tWdn, W["ffn_w_down"][l], None, None)
                    stage.release()
                    wins = []
                    for s, S in enumerate(seqs):
                        for i in range(S // 512):
                            wins.append((s, 512 * i, 510))

                    def front3(w):
                        s, ta, NO = w
                        NW = NO + 2
                        col0 = poff[s] + PAD + ta - 1
                        xw, txw, xs = xr.next()
                        g.op("sp", lambda e, s_, xw=xw, col0=col0, NW=NW: e.dma_start(out=xw[:, :, 0:NW], in_=xav[:, :, col0:col0 + NW]).then_inc(s_, 16), writes=[txw], dma=xs)
                        return rms_x(es, xw, txw, NW, sqr, hr, rsr, prot.next())

                    nxt3 = front3(wins[0])
                    for wi, (s, ta, NO) in enumerate(wins):
                            NW = NO + 2
                            col0 = poff[s] + PAD + ta - 1
                            hT, thT = nxt3
                            gT, tgT, _ = gTr.next()
                            for ca in range(22):
                                res = []
                                for half, cc in enumerate((ca, 22 + ca)):
                                    pu, tpu = prot.next()
                                    for k in range(8):
                                        g.op("pe", lambda e, pu=pu, k=k, cc=cc, NW=NW, hT=hT: e.matmul(pu[:, 0:NW], lhsT=Wup[:, k, cc * 128:(cc + 1) * 128], rhs=hT[:, k, 0:NW], start=(k == 0), stop=(k == 7)),
                                             reads=[tWup, thT], writes=[tpu])
                                    u, tu, _ = (uar if half == 0 else ubr).next()
                                    g.op("act", lambda e, u=u, pu=pu, cc=cc, NO=NO: e.activation(out=u[:, 0:NO], in_=pu[:, 1:NO + 1], func=AF.Identity, bias=cw[:, cc, 3:4], scale=cw[:, cc, 1:2]),
                                         reads=[tpu, tcw], writes=[tu])
                                    g.op("dve", lambda e, u=u, pu=pu, cc=cc, NO=NO: e.scalar_tensor_tensor(out=u[:, 0:NO], in0=pu[:, 0:NO], scalar=cw[:, cc, 0:1], in1=u[:, 0:NO], op0=ALU.mult, op1=ALU.add),
                                         reads=[tpu, tcw, tu], writes=[tu])
                                    g.op("dve", lambda e, u=u, pu=pu, cc=cc, NO=NO: e.scalar_tensor_tensor(out=u[:, 0:NO], in0=pu[:, 2:NO + 2], scalar=cw[:, cc, 2:3], in1=u[:, 0:NO], op0=ALU.mult, op1=ALU.add),
                                         reads=[tpu, tcw, tu], writes=[tu])
                                    res.append((u, tu))
                                (ua, tua), (ub, tub) = res
                                g.op("act", lambda e, ua=ua, NO=NO: e.activation(out=ua[:, 0:NO], in_=ua[:, 0:NO], func=AF.Silu), reads=[tua], writes=[tua])
                                g.op("pool", lambda e, ua=ua, ub=ub, gT=gT, ca=ca, NO=NO: e.tensor_tensor(out=gT[:, ca, 0:NO], in0=ua[:, 0:NO], in1=ub[:, 0:NO], op=ALU.mult),
                                     reads=[tua, tub], writes=[tgT])
                            if wi + 1 < len(wins):
                                nxt3 = front3(wins[wi + 1])
                            n0 = soff[s] + ta
                            for m in range(8):
                                xo, txo, xos = xor_.next()
                                g.op("sp", lambda e, s_, xo=xo, col0=col0, NO=NO, m=m: e.dma_start(out=xo[:, 0:NO], in_=xav[:, m, col0 + 1:col0 + 1 + NO]).then_inc(s_, 16), writes=[txo], dma=xos)
                                pd, tpd = prot.next()
                                for c in range(22):
                                    g.op("pe", lambda e, pd=pd, c=c, m=m, NO=NO, gT=gT: e.matmul(pd[:, 0:NO], lhsT=Wdn[:, c, m * 128:(m + 1) * 128], rhs=gT[:, c, 0:NO], start=(c == 0), stop=(c == 21)),
                                         reads=[tWdn, tgT], writes=[tpd])
                                g.op("dve", lambda e, pd=pd, xo=xo, NO=NO: e.tensor_tensor(out=xo[:, 0:NO], in0=pd[:, 0:NO], in1=xo[:, 0:NO], op=ALU.add),
                                     reads=[tpd, txo], writes=[txo])
                                g.op("sp", lambda e, s_, xo=xo, n0=n0, NO=NO, m=m: e.dma_start(out=xdv[:, m, n0:n0 + NO], in_=xo[:, 0:NO]).then_inc(s_, 16), reads=[txo], dma="st_" + xos)
                    tgroups = [(s, i) for s in range(len(seqs)) for i in range(seqs[s] // 512)]
                    NG = len(tgroups)
                    NWt, NOt = 4 * NG, 2 * NG
                    xw, txw, xs = xr.next()

                    def ldt(e, s_, xw=xw):
                        for gi_, (s, i) in enumerate(tgroups):
                            c0 = poff[s] + PAD + 512 * i + 509
                            e.dma_start(out=xw[:, :, 4 * gi_:4 * gi_ + 4], in_=xav[:, :, c0:c0 + 4]).then_inc(s_, 16)
                    g.op("sp", ldt, writes=[txw], dma=xs, ndma=NG)
                    hT, thT = rms_x(es, xw, txw, NWt, sqr, hr, rsr, prot.next())
                    gT, tgT, _ = gTr.next()
                    for ca in range(22):
                        res = []
                        for half, cc in enumerate((ca, 22 + ca)):
                            pu, tpu = prot.next()
                            for k in range(8):
                                g.op("pe", lambda e, pu=pu, k=k, cc=cc, hT=hT: e.matmul(pu[:, 0:NWt], lhsT=Wup[:, k, cc * 128:(cc + 1) * 128], rhs=hT[:, k, 0:NWt], start=(k == 0), stop=(k == 7)),
                                     reads=[tWup, thT], writes=[tpu])
                            u, tu, _ = (uar if half == 0 else ubr).next()
                            puv = pu[:, 0:NWt].rearrange("p (g c) -> p g c", c=4)
                            uv = u[:, 0:NOt].rearrange("p (g c) -> p g c", c=2)
                            g.op("act", lambda e, uv=uv, puv=puv, cc=cc: e.activation(out=uv, in_=puv[:, :, 1:3], func=AF.Identity, bias=cw[:, cc, 3:4], scale=cw[:, cc, 1:2]),
                                 reads=[tpu, tcw], writes=[tu])
                            g.op("dve", lambda e, uv=uv, puv=puv, cc=cc: e.scalar_tensor_tensor(out=uv, in0=puv[:, :, 0:2], scalar=cw[:, cc, 0:1], in1=uv, op0=ALU.mult, op1=ALU.add),
                                 reads=[tpu, tcw, tu], writes=[tu])
                            g.op("dve", lambda e, uv=uv, puv=puv, cc=cc: e.scalar_tensor_tensor(out=uv, in0=puv[:, :, 2:4], scalar=cw[:, cc, 2:3], in1=uv, op0=ALU.mult, op1=ALU.add),
                                 reads=[tpu, tcw, tu], writes=[tu])
                            res.append((u, tu))
                        (ua, tua), (ub, tub) = res
                        g.op("act", lambda e, ua=ua: e.activation(out=ua[:, 0:NOt], in_=ua[:, 0:NOt], func=AF.Silu), reads=[tua], writes=[tua])
                        g.op("pool", lambda e, ua=ua, ub=ub, gT=gT, ca=ca: e.tensor_tensor(out=gT[:, ca, 0:NOt], in0=ua[:, 0:NOt], in1=ub[:, 0:NOt], op=ALU.mult),
                             reads=[tua, tub], writes=[tgT])
                    for m in range(8):
                        xo, txo, xos = xor_.next()
                        pd, tpd = prot.next()
                        for c in range(22):
                            g.op("pe", lambda e, pd=pd, c=c, m=m, gT=gT: e.matmul(pd[:, 0:NOt], lhsT=Wdn[:, c, m * 128:(m + 1) * 128], rhs=gT[:, c, 0:NOt], start=(c == 0), stop=(c == 21)),
                                 reads=[tWdn, tgT], writes=[tpd])
                        g.op("dve", lambda e, pd=pd, xo=xo, xw=xw, m=m: e.tensor_tensor(out=xo[:, 0:NOt].rearrange("p (g c) -> p g c", c=2), in0=pd[:, 0:NOt].rearrange("p (g c) -> p g c", c=2),
                                                                                      in1=xw[:, m, 0:NWt].rearrange("p (g c) -> p g c", c=4)[:, :, 1:3], op=ALU.add),
                             reads=[tpd, txw], writes=[txo])
                        g.op("sp", lambda e, s_, xo=xo, m=m: e.dma_start(out=xdv[:, m, 0:NTOK].rearrange("p (q c) -> p q c", c=512)[:, :, 510:512],
                                                                        in_=xo[:, 0:NOt].rearrange("p (g c) -> p g c", c=2)).then_inc(s_, 16), reads=[txo], dma="st_" + xos)
                    phase_end()

        if debug:
            with ExitStack() as es:
                cp = es.enter_context(nc.sbuf_tensor(un("cpb"), [128, 2048], F32))
                tcp = T()
                def dcopy(dst, src):
                    g.op("sp", lambda e, s_: e.dma_start(out=dst, in_=src).then_inc(s_, 16), dma="dbgc")
                for nm, src in (("qc", qc), ("kc", kc), ("vc", vc), ("qd", qd), ("kd", kd), ("vd", vd), ("oc", oc), ("od", od), ("hA", hA), ("yB", yB), ("xa", xa)):
                    dcopy(dbg[nm], src)
                phase_end()
        print("total ops recorded:", g.nops)
    return nc


def gelu_to(g, src, tsrc, dst, tdst, tmr, nw=512):
    P = src.shape[0]
    t1, tt1, _ = tmr.next()
    g.op("act", lambda e: e.activation(out=t1[0:P, 0:nw], in_=src, func=AF.Square), reads=[tsrc], writes=[tt1])
    g.op("dve", lambda e: e.tensor_scalar(out=t1[0:P, 0:nw], in0=t1[0:P, 0:nw], scalar1=0.044715, scalar2=1.0, op0=ALU.mult, op1=ALU.add), reads=[tt1], writes=[tt1])
    g.op("dve", lambda e: e.tensor_tensor(out=t1[0:P, 0:nw], in0=t1[0:P, 0:nw], in1=src, op=ALU.mult), reads=[tt1, tsrc], writes=[tt1])
    g.op("act", lambda e: e.activation(out=t1[0:P, 0:nw], in_=t1[0:P, 0:nw], func=AF.Sigmoid, scale=1.5957691216057308), reads=[tt1], writes=[tt1])
    g.op("dve", lambda e: e.tensor_tensor(out=dst, in0=t1[0:P, 0:nw], in1=src, op=ALU.mult), reads=[tt1, tsrc], writes=[tdst])


SEQS = [2048, 2048, 2048, 2048, 2048]
_CACHE = {}


def _core_parts(c):
    if c < 4:
        return [("s", c, 0), ("s", c, 1)] + [("p", 3 * c + i, 0) for i in range(3)]
    return [("p", 12 + 5 * (c - 4) + i, 0) for i in range(5)]


def kernel(**inputs):
    x_prompt = np.asarray(inputs["x_prompt"], np.float32)
    x_sample = np.asarray(inputs["x_sample"], np.float32)
    n = 8
    consts = host_consts()
    wsT = np.ascontiguousarray(np.transpose(np.asarray(inputs["sgu_w_s"], np.float32), (0, 1, 3, 2)))
    if "nc" not in _CACHE:
        _CACHE["nc"] = build(SEQS, joined=True)
    nc = _CACHE["nc"]
    in_maps = []
    for c in range(n):
        parts = []
        for kind, idx, hf in _core_parts(c):
            parts.append(x_sample[idx, hf * 2048:(hf + 1) * 2048] if kind == "s" else x_prompt[idx])
        xT = np.ascontiguousarray(np.concatenate(parts, 0).T)
        jf = np.zeros((128, 2), np.float32)
        if c < 4:
            jf[:, 0] = 1.0
        else:
            jf[:, 1] = -30000.0
        m = {"xT": xT, "wsT": wsT, "jf": jf}
        for k in WNAMES:
            m[k] = np.ascontiguousarray(np.asarray(inputs[k], np.float32))
        m.update(consts)
        in_maps.append(m)
    res = run_bass_kernel_spmd(nc, in_maps, core_ids=list(range(n)))
    y_prompt = np.empty_like(x_prompt)
    y_sample = np.empty_like(x_sample)
    for c in range(n):
        y = np.asarray(res.results[c]["yT"]).T
        for i, (kind, idx, hf) in enumerate(_core_parts(c)):
            blk = y[2048 * i:2048 * (i + 1)]
            if kind == "s":
                y_sample[idx, hf * 2048:(hf + 1) * 2048] = blk
            else:
                y_prompt[idx] = blk
    return (y_prompt, y_sample)
```
